# Optimizing a Trainium2 kernel written in Bass

```python
import math
import jax, jax.numpy as jnp
from jax import lax
import numpy as np

D_MODEL = 2048
BATCH = 4
SEQ = 4096
DEPTH = 4

HEAD_DIM = 128
ROPE_THETA = 10000.0
NEG_INF = -1e30
LN_EPS = 1e-5
SUBLN_EPS = 1e-5
DIL_PAIRS = ((128, 1), (512, 4), (2048, 16))
A_HEADS_PER_GROUP = 6
A_HEADS = len(DIL_PAIRS) * A_HEADS_PER_GROUP
DIL_BLOCK = 64
B_HEADS = 8
B_DIM = 64
DENSE_Q_BLOCK = 128
C_Q_HEADS = 8
C_KV_HEADS = 2
C_HALF_WINDOW = 128
C_BLOCK = 128
D_HEADS = 8
GRID_W = 64
NA_WIN_H = 8
NA_WIN_W = 16
N_BRANCHES = 4
N_EXPERTS = 16
EC_CAPACITY_FACTOR = 2
D_EXPERT = D_MODEL // 2

A_W = A_HEADS * HEAD_DIM
B_W = B_HEADS * 2 * B_DIM
C_Q_W = C_Q_HEADS * HEAD_DIM
C_KV_W = C_KV_HEADS * HEAD_DIM
D_W = D_HEADS * HEAD_DIM
IN_WIDTHS = (A_W, A_W, A_W, B_W, B_W, B_W, C_Q_W, C_KV_W, C_KV_W, D_W, D_W, D_W, N_BRANCHES * D_MODEL)
VALUE_SLOTS = (2, 5, 8, 11)
IN_TOTAL = sum(IN_WIDTHS)
BRANCH_WIDTHS = (A_HEADS_PER_GROUP * HEAD_DIM, B_W, C_Q_W, D_W)
BRANCH_TOTAL = sum(BRANCH_WIDTHS)

kernel_name = "hybrid_dilated_diff_window_na_ec_encoder"


def _offsets(widths):
    offs = [0]
    for w in widths:
        offs.append(offs[-1] + w)
    return offs


def layer_norm(x, g, b):
    xf = x.astype(jnp.float32)
    mu = jnp.mean(xf, axis=-1, keepdims=True)
    var = jnp.mean(jnp.square(xf - mu), axis=-1, keepdims=True)
    return ((xf - mu) * lax.rsqrt(var + LN_EPS) * g.astype(jnp.float32) + b.astype(jnp.float32)).astype(x.dtype)


def rope(x, pos):
    half = x.shape[-1] // 2
    inv = ROPE_THETA ** (-jnp.arange(half, dtype=jnp.float32) / half)
    ang = pos.astype(jnp.float32)[:, None] * inv[None, :]
    cos = jnp.cos(ang)[:, None, :]
    sin = jnp.sin(ang)[:, None, :]
    xf = x.astype(jnp.float32)
    x1, x2 = xf[..., :half], xf[..., half:]
    return jnp.concatenate([x1 * cos - x2 * sin, x2 * cos + x1 * sin], axis=-1).astype(x.dtype)


def banded_attention(q, k, v, half_width, block, sink=None):
    N, L, Hq, d = q.shape
    Hkv = k.shape[2]
    G = Hq // Hkv
    nb = -(-L // block)
    Lp = nb * block
    qb = jnp.pad(q, ((0, 0), (0, Lp - L), (0, 0), (0, 0))).reshape(N, nb, block, Hkv, G, d)

    def band(t):
        tp = jnp.pad(t, ((0, 0), (block, Lp - L + block), (0, 0), (0, 0))).reshape(N, nb + 2, block, Hkv, d)
        return jnp.concatenate([tp[:, :nb], tp[:, 1:nb + 1], tp[:, 2:]], axis=2)

    kb, vb = band(k), band(v)
    s = jnp.einsum('nbqhgd,nbkhd->nbhgqk', qb, kb, preferred_element_type=jnp.float32) * (d ** -0.5)
    qpos = jnp.arange(nb)[:, None] * block + jnp.arange(block)[None, :]
    kpos = (jnp.arange(nb)[:, None] - 1) * block + jnp.arange(3 * block)[None, :]
    valid = ((jnp.abs(qpos[:, :, None] - kpos[:, None, :]) <= half_width)
             & (kpos[:, None, :] >= 0) & (kpos[:, None, :] < L))
    s = jnp.where(valid[None, :, None, None], s, NEG_INF)
    m = jnp.max(s, axis=-1)
    if sink is not None:
        sk = sink.astype(jnp.float32).reshape(Hkv, G)[None, None, :, :, None]
        m = jnp.maximum(m, sk)
    p = jnp.exp(s - m[..., None])
    denom = jnp.sum(p, axis=-1)
    if sink is not None:
        denom = denom + jnp.exp(sk - m)
    o = jnp.einsum('nbhgqk,nbkhd->nbqhgd', (p / denom[..., None]).astype(v.dtype), vb)
    o = o.reshape(N, Lp, Hq, d)[:, :L]
    lse = (m + jnp.log(denom)).transpose(0, 1, 4, 2, 3).reshape(N, Lp, Hq)[:, :L]
    return o, lse


def dilated_attention(q, k, v):
    B, S = q.shape[:2]
    hg = A_HEADS_PER_GROUP
    outs, lses = [], []
    for g, (w, r) in enumerate(DIL_PAIRS):
        hs = slice(g * hg, (g + 1) * hg)
        n_side = (w // 2) // r
        L = S // r

        def to_sub(t):
            t = t[:, :, hs].reshape(B, L, r, hg, HEAD_DIM).transpose(0, 2, 1, 3, 4)
            return t.reshape(B * r, L, hg, HEAD_DIM)

        o, lse = banded_attention(to_sub(q), to_sub(k), to_sub(v), n_side, DIL_BLOCK)
        outs.append(o.reshape(B, r, L, hg, HEAD_DIM).transpose(0, 2, 1, 3, 4).reshape(B, S, hg, HEAD_DIM))
        lses.append(lse.reshape(B, r, L, hg).transpose(0, 2, 1, 3).reshape(B, S, hg))
    wts = jax.nn.softmax(jnp.stack(lses, axis=0), axis=0)
    out = jnp.einsum('gbsh,gbshd->bshd', wts.astype(q.dtype), jnp.stack(outs, axis=0))
    return out.reshape(B, S, hg * HEAD_DIM)


def diff_attention(q, k, v, lam, subln_w, lambda_init):
    B, S = q.shape[:2]
    lamf = lam.astype(jnp.float32)
    lmbda = jnp.exp(jnp.sum(lamf[0] * lamf[1])) - jnp.exp(jnp.sum(lamf[2] * lamf[3])) + lambda_init
    nq = S // DENSE_Q_BLOCK
    qblocks = q.reshape(B, nq, DENSE_Q_BLOCK, B_HEADS, 2, B_DIM).transpose(1, 0, 2, 3, 4, 5)
    scale = B_DIM ** -0.5

    def block_fn(qblk):
        s = jnp.einsum('bqhmd,bkhmd->bhmqk', qblk, k, preferred_element_type=jnp.float32) * scale
        a = jax.nn.softmax(s, axis=-1)
        a = a[:, :, 0] - lmbda * a[:, :, 1]
        return jnp.einsum('bhqk,bkhe->bqhe', a.astype(v.dtype), v)

    o = lax.map(block_fn, qblocks)
    of = o.transpose(1, 0, 2, 3, 4).reshape(B, S, B_HEADS, 2 * B_DIM).astype(jnp.float32)
    of = of * lax.rsqrt(jnp.mean(of * of, axis=-1, keepdims=True) + SUBLN_EPS) * subln_w.astype(jnp.float32)
    return (of * (1.0 - lambda_init)).astype(q.dtype).reshape(B, S, B_W)


def neighborhood_attention(q, k, v, rpb):
    B, S, H, d = q.shape
    rows = S // GRID_W
    kh = min(NA_WIN_H, rows)
    kw = NA_WIN_W
    r = jnp.arange(rows)
    c = jnp.arange(GRID_W)
    row_idx = jnp.clip(r - kh // 2, 0, rows - kh)[:, None] + jnp.arange(kh)[None, :]
    col_start = jnp.clip(c - kw // 2, 0, GRID_W - kw)
    col_ok = (c[None, :] >= col_start[:, None]) & (c[None, :] < col_start[:, None] + kw)
    qg = q.reshape(B, rows, GRID_W, H, d)
    kband = k.reshape(B, rows, GRID_W, H, d)[:, row_idx]
    vband = v.reshape(B, rows, GRID_W, H, d)[:, row_idx]
    s = jnp.einsum('brchd,brkwhd->brhckw', qg, kband, preferred_element_type=jnp.float32) * (d ** -0.5)
    roff = row_idx - r[:, None] + (NA_WIN_H - 1)
    coff = jnp.clip(c[None, :] - c[:, None], -(kw - 1), kw - 1) + (NA_WIN_W - 1)
    bias = rpb.astype(jnp.float32)[:, roff[:, None, :, None], coff[None, :, None, :]]
    s = s + bias.transpose(1, 0, 2, 3, 4)[None]
    s = jnp.where(col_ok[:, None, :], s, NEG_INF)
    p = jax.nn.softmax(s.reshape(B, rows, H, GRID_W, kh * GRID_W), axis=-1)
    o = jnp.einsum('brhcK,brKhd->brchd', p.astype(v.dtype), vband.reshape(B, rows, kh * GRID_W, H, d))
    return o.reshape(B, S, H * d)


def expert_choice_ffn(x, w_router, w_gate, w_up, w_down):
    B, N, D = x.shape
    cap = EC_CAPACITY_FACTOR * N // N_EXPERTS
    aff = jax.nn.softmax(jnp.einsum('bnd,de->bne', x, w_router, preferred_element_type=jnp.float32), axis=-1)
    g, idx = lax.top_k(aff.transpose(0, 2, 1), cap)
    xin = jax.vmap(lambda xb, ib: xb[ib])(x, idx)
    h = jax.nn.silu(jnp.einsum('becd,edf->becf', xin, w_gate)) * jnp.einsum('becd,edf->becf', xin, w_up)
    y = jnp.einsum('becf,efd->becd', h, w_down) * g[..., None].astype(x.dtype)
    return jax.vmap(lambda yb, ib: jnp.zeros((N, D), yb.dtype).at[ib.reshape(-1)].add(yb.reshape(-1, D)))(y, idx)


def setup_inputs(seed: int = 0) -> dict:
    key = jax.random.key(seed)
    ks = jax.random.split(key, 15)
    f32 = jnp.float32
    beta = (8.0 * DEPTH) ** -0.25
    in_offs = _offsets(IN_WIDTHS)
    col_scale = np.full((IN_TOTAL,), D_MODEL ** -0.5, dtype=np.float32)
    for slot in VALUE_SLOTS:
        col_scale[in_offs[slot]:in_offs[slot + 1]] *= beta
    br_offs = _offsets(BRANCH_WIDTHS)
    row_scale = np.zeros((BRANCH_TOTAL,), dtype=np.float32)
    for i, w in enumerate(BRANCH_WIDTHS):
        row_scale[br_offs[i]:br_offs[i + 1]] = beta * w ** -0.5
    x = jax.random.normal(ks[0], (BATCH, SEQ, D_MODEL), f32)
    w_in = jax.random.normal(ks[1], (DEPTH, D_MODEL, IN_TOTAL), f32) * jnp.asarray(col_scale)
    b_gate = 0.02 * jax.random.normal(ks[2], (DEPTH, N_BRANCHES * D_MODEL), f32)
    w_branch = jax.random.normal(ks[3], (DEPTH, BRANCH_TOTAL, D_MODEL), f32) * jnp.asarray(row_scale)[:, None]
    w_out = jax.random.normal(ks[4], (DEPTH, D_MODEL, D_MODEL), f32) * (beta * D_MODEL ** -0.5)
    diff_lambda = 0.1 * jax.random.normal(ks[5], (DEPTH, 4, B_DIM), f32)
    diff_subln = 1.0 + 0.02 * jax.random.normal(ks[6], (DEPTH, 2 * B_DIM), f32)
    sink_logit = 0.5 * jax.random.normal(ks[7], (DEPTH, C_Q_HEADS), f32)
    na_rpb = 0.1 * jax.random.normal(ks[8], (DEPTH, D_HEADS, 2 * NA_WIN_H - 1, 2 * NA_WIN_W - 1), f32)
    w_router = jax.random.normal(ks[9], (DEPTH, D_MODEL, N_EXPERTS), f32) * D_MODEL ** -0.5
    w_exp_gate = jax.random.normal(ks[10], (DEPTH, N_EXPERTS, D_MODEL, D_EXPERT), f32) * D_MODEL ** -0.5
    w_exp_up = jax.random.normal(ks[11], (DEPTH, N_EXPERTS, D_MODEL, D_EXPERT), f32) * (beta * D_MODEL ** -0.5)
    w_exp_down = jax.random.normal(ks[12], (DEPTH, N_EXPERTS, D_EXPERT, D_MODEL), f32) * (beta * D_EXPERT ** -0.5)
    ln_gain = 1.0 + 0.02 * jax.random.normal(ks[13], (DEPTH, 2, D_MODEL), f32)
    ln_bias = 0.02 * jax.random.normal(ks[14], (DEPTH, 2, D_MODEL), f32)
    return {"x": x, "w_in": w_in, "b_gate": b_gate, "w_branch": w_branch, "w_out": w_out,
            "diff_lambda": diff_lambda, "diff_subln": diff_subln, "sink_logit": sink_logit,
            "na_rpb": na_rpb, "w_router": w_router, "w_exp_gate": w_exp_gate, "w_exp_up": w_exp_up,
            "w_exp_down": w_exp_down, "ln_gain": ln_gain, "ln_bias": ln_bias}


def reference(x, w_in, b_gate, w_branch, w_out, diff_lambda, diff_subln, sink_logit, na_rpb,
              w_router, w_exp_gate, w_exp_up, w_exp_down, ln_gain, ln_bias):
    B, S, D = x.shape
    pos = jnp.arange(S, dtype=jnp.int32)
    alpha = (2.0 * DEPTH) ** 0.25
    in_offs = _offsets(IN_WIDTHS)
    br_offs = _offsets(BRANCH_WIDTHS)
    for l in range(DEPTH):
        lambda_init = 0.8 - 0.6 * math.exp(-0.3 * l)
        proj = jnp.einsum('bsd,dn->bsn', x, w_in[l])
        parts = [proj[..., in_offs[i]:in_offs[i + 1]] for i in range(len(IN_WIDTHS))]
        q_a, k_a, v_a, q_b, k_b, v_b, q_c, k_c, v_c, q_d, k_d, v_d, gate_pre = parts
        o_a = dilated_attention(rope(q_a.reshape(B, S, A_HEADS, HEAD_DIM), pos),
                                rope(k_a.reshape(B, S, A_HEADS, HEAD_DIM), pos),
                                v_a.reshape(B, S, A_HEADS, HEAD_DIM))
        qb = rope(q_b.reshape(B, S, B_HEADS * 2, B_DIM), pos).reshape(B, S, B_HEADS, 2, B_DIM)
        kb = rope(k_b.reshape(B, S, B_HEADS * 2, B_DIM), pos).reshape(B, S, B_HEADS, 2, B_DIM)
        o_b = diff_attention(qb, kb, v_b.reshape(B, S, B_HEADS, 2 * B_DIM), diff_lambda[l], diff_subln[l], lambda_init)
        o_c, _ = banded_attention(rope(q_c.reshape(B, S, C_Q_HEADS, HEAD_DIM), pos),
                                  rope(k_c.reshape(B, S, C_KV_HEADS, HEAD_DIM), pos),
                                  v_c.reshape(B, S, C_KV_HEADS, HEAD_DIM),
                                  C_HALF_WINDOW, C_BLOCK, sink_logit[l])
        o_c = o_c.reshape(B, S, C_Q_W)
        o_d = neighborhood_attention(q_d.reshape(B, S, D_HEADS, HEAD_DIM), k_d.reshape(B, S, D_HEADS, HEAD_DIM),
                                     v_d.reshape(B, S, D_HEADS, HEAD_DIM), na_rpb[l])
        gates = jax.nn.sigmoid(gate_pre + b_gate[l]).reshape(B, S, N_BRANCHES, D)
        mixed = None
        for i, o in enumerate((o_a, o_b, o_c, o_d)):
            term = gates[:, :, i] * jnp.einsum('bsk,kd->bsd', o, w_branch[l, br_offs[i]:br_offs[i + 1]])
            mixed = term if mixed is None else mixed + term
        x = layer_norm(alpha * x + jnp.einsum('bsd,de->bse', mixed, w_out[l]), ln_gain[l, 0], ln_bias[l, 0])
        x = layer_norm(alpha * x + expert_choice_ffn(x, w_router[l], w_exp_gate[l], w_exp_up[l], w_exp_down[l]),
                       ln_gain[l, 1], ln_bias[l, 1])
    return x
```

```python
import math
from contextlib import ExitStack

import numpy as np
import ml_dtypes
import concourse.bass as bass
import concourse.mybir as mybir
from concourse.bass_utils import run_bass_kernel_spmd

F32 = mybir.dt.float32
BF16 = mybir.dt.bfloat16
I32 = mybir.dt.int32
AF = mybir.ActivationFunctionType
ALU = mybir.AluOpType

D = 2048
T = 4096
NT = T // 128
DEPTH = 4
A_HPG = 6
IN_WIDTHS = (2304, 2304, 2304, 1024, 1024, 1024, 1024, 256, 256, 1024, 1024, 1024, 8192)
IN_OFF = [0]
for _w in IN_WIDTHS:
    IN_OFF.append(IN_OFF[-1] + _w)
IN_TOTAL = IN_OFF[-1]
BR_W = (768, 1024, 1024, 1024)
BR_OFF = (0, 768, 1792, 2816)
NSLOT = 30
NEXP = 16
CAP = 512
DEXP = 1024
ALPHA = (2.0 * DEPTH) ** 0.25
LN_EPS = 1e-5
BIGIDX = 1.0e6
NEGM = -30000.0


class Buf:
    __slots__ = ("t", "lw", "rd", "name", "kind")

    def __init__(self, t, name="", kind="sb"):
        self.t = t
        self.lw = None
        self.rd = {}
        self.name = name
        self.kind = kind

    def __getitem__(self, idx):
        return self.t[idx]


class _View:
    def __init__(self, base, ap):
        self._b = base
        self.t = ap

    def __getitem__(self, idx):
        return self.t[idx]

    kind = property(lambda s: s._b.kind)
    lw = property(lambda s: s._b.lw, lambda s, v: setattr(s._b, "lw", v))
    rd = property(lambda s: s._b.rd, lambda s, v: setattr(s._b, "rd", v))


class FW:
    NDMASEM = 12

    def __init__(self, nc, stack):
        self.nc = nc
        self.stack = stack
        self.eng = {"pe": nc.tensor, "act": nc.scalar, "dve": nc.vector, "pool": nc.gpsimd, "sp": nc.sync}
        self.sem = {}
        self.cnt = {}
        for e in self.eng:
            self.sem[e] = stack.enter_context(nc.semaphore("s_" + e))
            self.cnt[e] = 0
        self.dsem = {}
        self.dcnt = {}
        self.dnext = {}
        for q in ("sp", "act", "pool"):
            self.dsem[q] = [stack.enter_context(nc.semaphore(f"d_{q}{i}")) for i in range(self.NDMASEM)]
            self.dcnt[q] = [0] * self.NDMASEM
            self.dnext[q] = 0
        self.known = {e: {} for e in self.eng}
        self.ext_in = []
        self.ninstr = 0
        self.uid = 0
        self.breg = nc.gpsimd.alloc_register("idma_bound")
        nc.gpsimd.reg_mov(self.breg, CAP - 1)

    def sb(self, shape, dt, name=None, stack=None):
        self.uid += 1
        name = (name or "t") + f"_{self.uid}"
        t = (stack or self.stack).enter_context(self.nc.sbuf_tensor(name, list(shape), dt))
        return Buf(t, name)

    def ps(self, shape, dt, name):
        t = self.stack.enter_context(self.nc.psum_tensor(name, list(shape), dt))
        return Buf(t, name, "ps")

    def dram(self, name, shape, dt, kind="Internal"):
        t = self.nc.dram_tensor(name, list(shape), dt, kind=kind)
        if kind == "ExternalInput":
            self.ext_in.append(name)
        return Buf(t.ap(), name, "dram")

    def _wait(self, e, key, count):
        kn = self.known[e]
        if kn.get(key, 0) >= count:
            return
        s = self.sem[key] if isinstance(key, str) else self.dsem[key[0]][key[1]]
        self.eng[e].wait_ge(s, count)
        kn[key] = count

    def _deps(self, e, reads, writes, pe_accum=False):
        for b in reads:
            if b.kind == "dram":
                continue
            if b.lw is not None:
                self._wait(e, b.lw[0], b.lw[1])
            if b.kind == "ps":
                for k, c in b.rd.items():
                    if k != e:
                        self._wait(e, k, c)
        for b in writes:
            if b.kind == "dram":
                continue
            if b.lw is not None:
                if not (pe_accum and b.lw[0] == "pe" and e == "pe"):
                    self._wait(e, b.lw[0], b.lw[1])
            for k, c in b.rd.items():
                self._wait(e, k, c)

    def _mark(self, key, count, reads, writes):
        for b in reads:
            if b.kind != "dram":
                b.rd[key] = count
        for b in writes:
            if b.kind != "dram":
                b.lw = (key, count)
                b.rd = {}

    def op(self, e, fn, reads=(), writes=(), pe_accum=False):
        self._deps(e, reads, writes, pe_accum)
        ins = fn(self.eng[e])
        self.cnt[e] += 1
        ins.then_inc(self.sem[e], 1)
        self._mark(e, self.cnt[e], reads, writes)
        self.ninstr += 1
        return ins

    def _dma_slot(self, q):
        i = self.dnext[q]
        self.dnext[q] = (i + 1) % self.NDMASEM
        key = (q, i)
        if self.dcnt[q][i] > 0:
            self._wait(q, key, self.dcnt[q][i])
        return i, key

    def dma(self, q, out_ap, in_ap, reads=(), writes=(), **kw):
        i, key = self._dma_slot(q)
        self._deps(q, reads, writes)
        ins = self.eng[q].dma_start(out=out_ap, in_=in_ap, **kw)
        self.dcnt[q][i] += 16
        ins.then_inc(self.dsem[q][i], 16)
        self._mark(key, self.dcnt[q][i], reads, writes)
        self.ninstr += 1
        return ins

    def idma(self, out_ap, out_off, in_ap, in_off, bound, reads=(), writes=()):
        q = "pool"
        i, key = self._dma_slot(q)
        self._deps(q, reads, writes)
        ins = self.nc.gpsimd.indirect_dma_start(out=out_ap, out_offset=out_off, in_=in_ap, in_offset=in_off,
                                                bounds_check=self.breg, oob_is_err=False)
        self.dcnt[q][i] += 16
        ins.then_inc(self.dsem[q][i], 16)
        self._mark(key, self.dcnt[q][i], reads, writes)
        self.ninstr += 1
        return ins

    def barrier(self):
        for e in self.eng:
            for e2 in self.eng:
                if e2 != e and self.cnt[e2] > 0:
                    self._wait(e, e2, self.cnt[e2])
            for q in self.dsem:
                for i in range(self.NDMASEM):
                    if self.dcnt[q][i] > 0:
                        self._wait(e, (q, i), self.dcnt[q][i])

    def finish(self, bufs):
        self.barrier()


def _d_patterns():
    kk = np.arange(128)[:, None]
    qq = np.arange(128)[None, :]
    pats, plist, blocks = {}, [], {}
    for i in range(NT):
        r = 2 * i + qq // 64
        c = qq % 64
        rs = np.clip(r - 4, 0, 56)
        cs = np.clip(c - 8, 0, 48)
        for j in range(NT):
            kr = 2 * j + kk // 64
            kc = kk % 64
            valid = (kr >= rs) & (kr < rs + 8) & (kc >= cs) & (kc < cs + 16)
            if not valid.any():
                continue
            roff = np.where(valid, kr - r + 7, 0)
            coff = np.where(valid, np.clip(kc - c, -15, 15) + 15, 0)
            key = (roff.tobytes(), coff.tobytes(), valid.tobytes())
            if key not in pats:
                pats[key] = len(plist)
                plist.append((roff, coff, valid))
            blocks.setdefault(i, []).append((j, pats[key]))
    return blocks, plist


D_BLOCKS, D_PATS = _d_patterns()
NPAT = len(D_PATS)


def _masks():
    kk = np.arange(128)[:, None]
    qq = np.arange(128)[None, :]
    d = kk - qq
    m4 = (d % 4) == 0
    m16 = (d % 16) == 0
    ms = [kk <= qq, kk >= qq, np.abs(d) <= 64, d <= -64, d >= 64,
          m4, m4 & (kk <= qq), m4 & (kk >= qq), m16, m16 & (kk <= qq), m16 & (kk >= qq)]
    return np.stack([m.astype(np.float32) for m in ms], axis=1).astype(ml_dtypes.bfloat16)


def _rope_tables():
    pos = np.arange(T, dtype=np.float32)
    out = []
    for half in (64, 32):
        inv = (np.float32(10000.0) ** (-np.arange(half, dtype=np.float32) / np.float32(half))).astype(np.float32)
        ang = (pos[None, :] * inv[:, None]).astype(np.float32)
        cos = np.cos(ang).astype(np.float32)
        sin = np.sin(ang).astype(np.float32)
        p = np.arange(128)
        i = p % half
        sign = np.where((p % (2 * half)) < half, -1.0, 1.0).astype(np.float32)
        out.append(cos[i])
        out.append(sin[i] * sign[:, None])
    return np.stack(out, 0).astype(np.float32)


def _perms():
    p = np.arange(128)
    pa = np.zeros((128, 128), np.float32)
    pa[(p + 64) % 128, p] = 1.0
    pb = np.zeros((128, 128), np.float32)
    pb[(p // 64) * 64 + ((p % 64) + 32) % 64, p] = 1.0
    return np.stack([pa, pb], 0).astype(ml_dtypes.bfloat16)


def _head_tiles():
    s128 = 128.0 ** -0.5
    s64 = 64.0 ** -0.5
    groups = [("qa", IN_OFF[0], 18, "A", s128), ("ka", IN_OFF[1], 18, "A", 1.0),
              ("qb", IN_OFF[3], 8, "B", s64), ("kb", IN_OFF[4], 8, "B", 1.0),
              ("qc", IN_OFF[6], 8, "A", s128), ("kc", IN_OFF[7], 2, "A", 1.0),
              ("qd", IN_OFF[9], 8, None, s128), ("kd", IN_OFF[10], 8, None, 1.0)]
    tiles = []
    base = {}
    for name, off, n, rk, sc in groups:
        base[name] = len(tiles)
        for t in range(n):
            tiles.append((off + 128 * t, rk, sc))
    return tiles, base


HEAD_TILES, QK_BASE = _head_tiles()
NQK = len(HEAD_TILES)
V_SLABS = []
_dst = 0
for _slot in (2, 5, 8, 11):
    _o, _w = IN_OFF[_slot], IN_WIDTHS[_slot]
    _c = 0
    while _c < _w:
        _ww = min(512, _w - _c)
        V_SLABS.append((_o + _c, _ww, _dst))
        _c += _ww
        _dst += _ww
V_W = _dst
V_BASE = {"a": 0, "b": 2304, "c": 3328, "d": 3584}


class Prog:
    def __init__(self, nlayers, first_layer, debug=False, gather=False):
        self.nl = nlayers
        self.first = first_layer
        self.debug = debug
        self.gather = gather

    def build(self):
        nc = bass.Bass("TRN2", target_bir_lowering=False)
        self.nc = nc
        with ExitStack() as st:
            fw = FW(nc, st)
            self.fw = fw
            self._declare(st)
            for li in range(self.nl):
                self.layer(li)
            fw.finish([self.out])
        return nc

    def _declare(self, st):
        fw, nc, L = self.fw, self.nc, self.nl
        ph = getattr(self, "phases", "0spamf")
        feed = getattr(self, "feed", ())
        want = getattr(self, "want", ())

        def ein(n, s, dt=F32, need=True):
            return fw.dram(n, s, dt, kind="ExternalInput") if need else None

        def win(n, s, need):
            if not need:
                return None
            if not self.gather:
                return fw.dram(n, s, F32, kind="ExternalInput")
            rows = 1
            for d_ in s[:-1]:
                rows *= d_
            assert rows % 8 == 0
            shard = fw.dram(n, [rows // 8, s[-1]], F32, kind="ExternalInput")
            full = fw.dram(n + "_full", [rows, s[-1]], F32, kind="Internal")
            self.gathers.append((shard, full))
            names = "abcdefg"[:len(s) - 1]
            if len(s) == 2:
                return full
            pat = "(" + " ".join(names) + ") z -> " + " ".join(names) + " z"
            kw = {names[i]: s[i] for i in range(len(s) - 2)}
            return Buf(full.t.rearrange(pat, **kw), n + "_fullv") if False else _View(full, full.t.rearrange(pat, **kw))

        self.gathers = []
        self.x = ein("x", [T, D])
        self.w_in = win("w_in", [L, D, IN_TOTAL], "p" in ph)
        self.b_gate = ein("b_gate", [L, 8192])
        self.w_branch = win("w_branch", [L, 3840, D], "m" in ph)
        self.w_out = win("w_out", [L, D, D], "m" in ph)
        self.diff_lambda = ein("diff_lambda", [L, 256])
        self.diff_subln = ein("diff_subln", [L, 128])
        self.sink_logit = ein("sink_logit", [L, 8])
        self.dbias = ein("dbias", [L, 8, 128, NPAT, 128], need="a" in ph)
        self.w_router = ein("w_router", [L, D, NEXP], need="f" in ph)
        self.w_eg = win("w_exp_gate", [L, NEXP, D, DEXP], "f" in ph)
        self.w_eu = win("w_exp_up", [L, NEXP, D, DEXP], "f" in ph)
        self.w_ed = win("w_exp_down", [L, NEXP, DEXP, D], "f" in ph)
        self.ln_gain = ein("ln_gain", [L, 2, D])
        self.ln_bias = ein("ln_bias", [L, 2, D])
        self.lconst = ein("lconst", [L, 4])
        self.c_ident = ein("c_ident", [128, 128])
        self.c_perm = ein("c_perm", [2, 128, 128], BF16)
        self.c_mask = ein("c_mask", [128, 11, 128], BF16)
        self.c_rope = ein("c_rope", [4, 128, T], need="p" in ph)
        self.out = fw.dram("out", [T, D], F32, kind="ExternalOutput")

        def scr(n, s, dt):
            k = "ExternalInput" if n in feed else ("ExternalOutput" if n in want else "Internal")
            return fw.dram(n, s, dt, kind=k)
        self.xT = scr("xT", [128, 16, T], BF16)
        self.xres = scr("xres", [T, D], F32)
        self.qkT = scr("qkT", [NQK, 128, T], BF16)
        self.v = scr("v", [T, V_W], BF16)
        self.gT = scr("gT", [64, 128, T], BF16)
        self.oT = scr("oT", [NSLOT, 128, T], BF16)
        self.mixT = scr("mixT", [128, 16, T], BF16)
        self.x1res = scr("x1res", [T, D], F32)
        self.x1b = scr("x1b", [T, D], BF16)
        self.x1T = scr("x1T", [128, 16, T], BF16)
        self.xin = [scr(f"xin{e}", [CAP, D], BF16) for e in range(NEXP)]
        self.yexp = [scr(f"yexp{e}", [CAP, D], F32) for e in range(NEXP)]
        self.dbg_aff = scr("dbg_aff", [128, NT * NEXP], F32)
        self.dbg_idx = scr("dbg_idx", [128, NT * NEXP], I32)
        self.debug = ("dbg_aff" in want)
        self.gwait = {}
        for shard, full in self.gathers:
            rows, cols = shard.t.shape
            stage = fw.dram(shard.name + "_st", [rows, cols], F32)
            step = max(1, min(rows, (4 << 20) // (cols * 4)))
            keys = []
            for r0 in range(0, rows, step):
                r1 = min(rows, r0 + step)
                fw.dma("sp", stage[r0:r1, :], shard[r0:r1, :])
                q_i = (fw.dnext["sp"] - 1) % fw.NDMASEM
                keys.append((("sp", q_i), fw.dcnt["sp"][q_i]))
            for k_, c_ in keys:
                fw._wait("pool", k_, c_)
            i, key = fw._dma_slot("pool")
            ins = nc.gpsimd.collective_compute("AllGather", ALU.bypass, ins=[stage[:, :]], outs=[full[:, :]],
                                               replica_groups=[list(range(8))])
            fw.dcnt["pool"][i] += 16
            ins.then_inc(fw.dsem["pool"][i], 16)
            self.gwait[full.name] = (key, fw.dcnt["pool"][i])
        self.banks = [fw.ps([128, 512], F32, f"bank{i}") for i in range(8)]
        self.ident_f = fw.sb([128, 128], F32, "ident_f")
        self.ident_b = fw.sb([128, 128], BF16, "ident_b")
        self.perm = fw.sb([128, 2, 128], BF16, "perm")
        self.small = fw.sb([128, 512], F32, "small")
        self.bgc = fw.sb([128, 64], F32, "bgc")
        self.subwb = fw.sb([128, 128], F32, "subwb")
        fw.dma("sp", self.ident_f[:], self.c_ident[:], reads=[self.c_ident], writes=[self.ident_f])
        fw.op("dve", lambda e: e.tensor_copy(out=self.ident_b[:], in_=self.ident_f[:]), reads=[self.ident_f], writes=[self.ident_b])
        fw.dma("sp", self.perm[:], self.c_perm[:].rearrange("a p n -> p a n"), reads=[self.c_perm], writes=[self.perm])

    def need_w(self, *ws):
        for w in ws:
            nm = getattr(getattr(w, "_b", w), "name", None)
            if nm in self.gwait:
                key, cnt = self.gwait.pop(nm)
                for e in self.fw.eng:
                    self.fw._wait(e, key, cnt)

    def emit_T(self, src, dst_dram, n, st_pool, eng_rr):
        fw = self.fw
        xts = st_pool[n % len(st_pool)]
        for half in range(2):
            bank = self.banks[4 + (eng_rr[0] % 4)]
            eng_rr[0] += 1
            bb = bank[:].bitcast(BF16)
            for k in range(8):
                kc = half * 8 + k
                fw.op("pe", lambda e, kc=kc, k=k: e.transpose(out=bb[:, k * 128:(k + 1) * 128], in_=src[:, kc * 128:(kc + 1) * 128],
                                                              identity=self.ident_b[:]), reads=[src, self.ident_b], writes=[bank])
            eng = "act" if half == 0 else "dve"
            if eng == "act":
                fw.op("act", lambda e: e.activation(out=xts[:, half * 8:(half + 1) * 8, :], in_=bb[:, :].rearrange("p (a b) -> p a b", b=128), func=AF.Copy),
                      reads=[bank], writes=[xts])
            else:
                fw.op("dve", lambda e: e.tensor_copy(out=xts[:, half * 8:(half + 1) * 8, :], in_=bb[:, :].rearrange("p (a b) -> p a b", b=128)),
                      reads=[bank], writes=[xts])
        fw.dma("sp", dst_dram[:, :, n * 128:(n + 1) * 128], xts[:], reads=[xts], writes=[dst_dram])

    def layer_norm(self, y, gb, bb_, out, tmp, st6, mv, sm):
        fw = self.fw
        for c in range(4):
            fw.op("dve", lambda e, c=c: e.bn_stats(out=st6[:, c * 6:(c + 1) * 6], in_=y[:, c * 512:(c + 1) * 512]), reads=[y], writes=[st6])
        fw.op("dve", lambda e: e.bn_aggr(out=mv[:], in_=st6[:]), reads=[st6], writes=[mv])
        fw.op("act", lambda e: e.activation(out=sm[:, 0:1], in_=mv[:, 1:2], func=AF.Ln, bias=self.small[:, 9:10]), reads=[mv, self.small], writes=[sm])
        fw.op("act", lambda e: e.activation(out=sm[:, 1:2], in_=sm[:, 0:1], func=AF.Exp, scale=-0.5), reads=[sm], writes=[sm])
        fw.op("dve", lambda e: e.tensor_scalar(out=sm[:, 2:3], in0=mv[:, 0:1], scalar1=sm[:, 1:2], scalar2=-1.0, op0=ALU.mult, op1=ALU.mult),
              reads=[mv, sm], writes=[sm])
        fw.op("act", lambda e: e.activation(out=tmp[:], in_=y[:], func=AF.Identity, scale=sm[:, 1:2], bias=sm[:, 2:3]), reads=[y, sm], writes=[tmp])
        fw.op("dve", lambda e: e.tensor_tensor(out=tmp[:], in0=tmp[:], in1=gb[:], op=ALU.mult), reads=[tmp, gb], writes=[tmp])
        fw.op("pool", lambda e: e.tensor_tensor(out=out[:], in0=tmp[:], in1=bb_[:], op=ALU.add), reads=[tmp, bb_], writes=[out])

    def layer(self, li):
        fw = self.fw
        gl = self.first + li
        ph = getattr(self, "phases", "0spamf")
        if li == 0 and "0" in ph:
            self.phase0()
        if "s" in ph:
            self.phase_small(li)
        if "p" in ph:
            self.phase_proj(li)
        if "a" in ph:
            self.phase_att(li)
        if "m" in ph:
            self.phase_merge(li)
        if "f" in ph:
            self.phase_ffn(li, last=(li == self.nl - 1))

    def phase0(self):
        fw = self.fw
        with ExitStack() as ps:
            xt = [fw.sb([128, D], F32, "p0x", ps) for _ in range(2)]
            xb = [fw.sb([128, D], BF16, "p0b", ps) for _ in range(2)]
            xts = [fw.sb([128, 16, 128], BF16, "p0t", ps) for _ in range(2)]
            rr = [0]
            for n in range(NT):
                a, b = xt[n % 2], xb[n % 2]
                fw.dma("act", a[:], self.x[n * 128:(n + 1) * 128, :], reads=[self.x], writes=[a])
                fw.op("pool", lambda e: e.tensor_copy(out=b[:], in_=a[:]), reads=[a], writes=[b])
                self.emit_T(b, self.xT, n, xts, rr)
            fw.barrier()

    def phase_small(self, li):
        fw = self.fw
        sm = self.small
        with ExitStack() as ps:
            lam = fw.sb([128, 256], F32, "lam", ps)
            tmp = fw.sb([128, 128], F32, "lamt", ps)
            bg64 = fw.sb([64, 128], F32, "bg64", ps)
            fw.dma("sp", bg64[:], self.b_gate[li, :].rearrange("(c p) -> c p", p=128), reads=[self.b_gate], writes=[bg64])
            bank = self.banks[0]
            fw.op("pe", lambda e: e.transpose(out=bank[:, 0:64], in_=bg64[:], identity=self.ident_f[0:64, 0:64]), reads=[bg64, self.ident_f], writes=[bank])
            fw.op("dve", lambda e: e.tensor_copy(out=self.bgc[:], in_=bank[:, 0:64]), reads=[bank], writes=[self.bgc])
            fw.dma("sp", sm[:, 0:8], self.sink_logit[li:li + 1, :].partition_broadcast(128), reads=[self.sink_logit], writes=[sm])
            fw.dma("sp", sm[:, 16:20], self.lconst[li:li + 1, :].partition_broadcast(128), reads=[self.lconst], writes=[sm])
            fw.dma("sp", lam[:], self.diff_lambda[li:li + 1, :].partition_broadcast(128), reads=[self.diff_lambda], writes=[lam])
            fw.dma("sp", self.subwb[:], self.diff_subln[li:li + 1, :].partition_broadcast(128), reads=[self.diff_subln], writes=[self.subwb])
            fw.op("act", lambda e: e.activation(out=sm[:, 0:8], in_=sm[:, 0:8], func=AF.Exp), reads=[sm], writes=[sm])
            fw.op("dve", lambda e: e.memset(sm[:, 9:10], LN_EPS), writes=[sm])
            fw.op("dve", lambda e: e.tensor_tensor(out=tmp[:, 0:64], in0=lam[:, 0:64], in1=lam[:, 64:128], op=ALU.mult), reads=[lam], writes=[tmp])
            fw.op("dve", lambda e: e.tensor_tensor(out=tmp[:, 64:128], in0=lam[:, 128:192], in1=lam[:, 192:256], op=ALU.mult), reads=[lam], writes=[tmp])
            fw.op("dve", lambda e: e.tensor_reduce(out=sm[:, 20:22], in_=tmp[:, :].rearrange("p (a b) -> p a b", b=64), axis=mybir.AxisListType.X, op=ALU.add),
                  reads=[tmp], writes=[sm])
            fw.op("act", lambda e: e.activation(out=sm[:, 20:22], in_=sm[:, 20:22], func=AF.Exp), reads=[sm], writes=[sm])
            fw.op("dve", lambda e: e.tensor_tensor(out=sm[:, 22:23], in0=sm[:, 21:22], in1=sm[:, 20:21], op=ALU.subtract), reads=[sm], writes=[sm])
            fw.op("dve", lambda e: e.tensor_tensor(out=sm[:, 8:9], in0=sm[:, 22:23], in1=sm[:, 16:17], op=ALU.subtract), reads=[sm], writes=[sm])
            fw.op("dve", lambda e: e.tensor_scalar(out=self.subwb[:], in0=self.subwb[:], scalar1=sm[:, 17:18], scalar2=None, op0=ALU.mult),
                  reads=[self.subwb, sm], writes=[self.subwb])
            fw.barrier()

    def phase_proj(self, li):
        fw = self.fw
        TB = 2048
        NCH = TB // 512
        w_in = self.w_in
        self.need_w(w_in)
        fslabs = []
        groups = [("qa", 18), ("ka", 18), ("qb", 8), ("kb", 8), ("qc", 8), ("kc", 2), ("qd", 8), ("kd", 8)]
        for name, n in groups:
            b0 = QK_BASE[name]
            t = 0
            while t < n:
                nt = min(4, n - t)
                c0, rk, sc = HEAD_TILES[b0 + t]
                fslabs.append((c0, nt, rk, sc, "qk", b0 + t))
                t += nt
        for g in range(0, 64, 4):
            fslabs.append((IN_OFF[12] + 128 * g, 4, "G", 1.0, "g", g))
        with ExitStack() as ps:
            xTb = fw.sb([128, 16, TB], BF16, "xTb", ps)
            tabs = fw.sb([128, 4, TB], F32, "tabs", ps)
            sst = [fw.sb([128, 4, 512], F32, "sst", ps) for _ in range(3)]
            sbf = [fw.sb([128, 16, 512], BF16, "sbf", ps) for _ in range(2)]
            qb = [fw.sb([128, 512], BF16, "qb", ps) for _ in range(2)]
            t1 = [fw.sb([128, 512], F32, "t1", ps) for _ in range(2)]
            t2 = [fw.sb([128, 512], F32, "t2", ps) for _ in range(2)]
            ob = [fw.sb([128, 512], BF16, "ob", ps) for _ in range(4)]
            wv = w_in[li].rearrange("(kc p) n -> p kc n", p=128)
            cnt = 0
            pcs = 0
            dbg = getattr(self, 'pdbg', {})

            def load_slab(si, c0, w):
                nonlocal pcs
                sb_ = sbf[si % 2]
                for pz in range(4):
                    s_ = sst[pcs % 3]
                    fw.dma("sp", s_[:, :, 0:w], wv[:, pz * 4:(pz + 1) * 4, c0:c0 + w], reads=[w_in], writes=[s_])
                    dst = sb_[:, pz * 4:(pz + 1) * 4, 0:w]
                    if pcs % 2 == 0:
                        fw.op("pool", lambda e: e.tensor_copy(out=dst, in_=s_[:, :, 0:w]), reads=[s_], writes=[sb_])
                    else:
                        fw.op("act", lambda e: e.activation(out=dst, in_=s_[:, :, 0:w], func=AF.Copy), reads=[s_], writes=[sb_])
                    pcs += 1
                return sb_

            for tb in range(dbg.get('ntb', T // TB)):
                t0 = tb * TB
                for h in range(2):
                    fw.dma("sp", xTb[:, h * 8:(h + 1) * 8, :], self.xT[:, h * 8:(h + 1) * 8, t0:t0 + TB], reads=[self.xT], writes=[xTb])
                for k in range(4):
                    fw.dma("act", tabs[:, k, :], self.c_rope[k, :, t0:t0 + TB], reads=[self.c_rope], writes=[tabs])
                si = 0
                fs = [fslabs[i] for i in dbg['slabs']] if 'slabs' in dbg else fslabs
                for (c0, nt, rk, sc, dk_, d0) in fs:
                    sb_ = load_slab(si, c0, nt * 128)
                    si += 1
                    for t in range(nt):
                        for c in range(NCH):
                            A = self.banks[cnt % 3]
                            for kc in range(16):
                                fw.op("pe", lambda e, kc=kc: e.matmul(A[:], lhsT=sb_[:, kc, t * 128:(t + 1) * 128], rhs=xTb[:, kc, c * 512:(c + 1) * 512],
                                                                      start=(kc == 0), stop=(kc == 15)),
                                      reads=[sb_, xTb], writes=[A], pe_accum=(kc > 0))
                            o = ob[cnt % 4]
                            csl = slice(c * 512, (c + 1) * 512)
                            tsl = slice(t0 + c * 512, t0 + (c + 1) * 512)
                            if rk in ("A", "B"):
                                q_ = qb[cnt % 2]
                                Bk = self.banks[3 + cnt % 2]
                                kc_, ks_, pi = (0, 1, 0) if rk == "A" else (2, 3, 1)
                                fw.op("act", lambda e: e.activation(out=q_[:], in_=A[:], func=AF.Copy, scale=float(sc)), reads=[A], writes=[q_])
                                fw.op("pe", lambda e: e.matmul(Bk[:], lhsT=self.perm[:, pi, :], rhs=q_[:], start=True, stop=True),
                                      reads=[self.perm, q_], writes=[Bk])
                                a1, a2 = t1[cnt % 2], t2[cnt % 2]
                                fw.op("dve", lambda e: e.scalar_tensor_tensor(out=a1[:], in0=A[:], scalar=float(sc), in1=tabs[:, kc_, csl],
                                                                              op0=ALU.mult, op1=ALU.mult), reads=[A, tabs], writes=[a1])
                                fw.op("dve", lambda e: e.tensor_tensor(out=a2[:], in0=Bk[:], in1=tabs[:, ks_, csl], op=ALU.mult),
                                      reads=[Bk, tabs], writes=[a2])
                                fw.op("dve", lambda e: e.tensor_tensor(out=o[:], in0=a1[:], in1=a2[:], op=ALU.add), reads=[a1, a2], writes=[o])
                            elif rk == "G":
                                g = d0 + t
                                fw.op("act", lambda e: e.activation(out=o[:], in_=A[:], func=AF.Sigmoid, bias=self.bgc[:, g:g + 1]),
                                      reads=[A, self.bgc], writes=[o])
                            else:
                                fw.op("act", lambda e: e.activation(out=o[:], in_=A[:], func=AF.Copy, scale=float(sc)), reads=[A], writes=[o])
                            if dk_ == "qk":
                                fw.dma("sp", self.qkT[d0 + t, :, tsl], o[:], reads=[o], writes=[self.qkT])
                            else:
                                fw.dma("sp", self.gT[d0 + t, :, tsl], o[:], reads=[o], writes=[self.gT])
                            cnt += 1
                for (c0, w, dc) in V_SLABS[:dbg.get('nv', 99)]:
                    sb_ = load_slab(si, c0, w)
                    si += 1
                    for tt in range(TB // 128):
                        A = self.banks[cnt % 3]
                        for kc in range(16):
                            fw.op("pe", lambda e, kc=kc: e.matmul(A[:, 0:w], lhsT=xTb[:, kc, tt * 128:(tt + 1) * 128], rhs=sb_[:, kc, 0:w],
                                                                  start=(kc == 0), stop=(kc == 15)),
                                  reads=[sb_, xTb], writes=[A], pe_accum=(kc > 0))
                        o = ob[cnt % 4]
                        if cnt % 2 == 0:
                            fw.op("act", lambda e: e.activation(out=o[:, 0:w], in_=A[:, 0:w], func=AF.Copy), reads=[A], writes=[o])
                        else:
                            fw.op("dve", lambda e: e.tensor_copy(out=o[:, 0:w], in_=A[:, 0:w]), reads=[A], writes=[o])
                        r0 = t0 + tt * 128
                        fw.dma("sp", self.v[r0:r0 + 128, dc:dc + w], o[:, 0:w], reads=[o], writes=[self.v])
                        cnt += 1
            fw.barrier()

    def att_jobs(self):
        jobs = []
        for h in range(A_HPG):
            srcs = []
            for g in range(3):
                hd = g * A_HPG + h
                srcs.append((QK_BASE["qa"] + hd, QK_BASE["ka"] + hd, V_BASE["a"] + 128 * hd, (0, 128)))

            def blocks(i):
                bl = []
                for dj, m in ((-1, 4), (0, 2), (1, 3)):
                    bl.append((0, i + dj, m, 0))
                for dj in range(-2, 3):
                    bl.append((1, i + dj, 7 if dj == -2 else (6 if dj == 2 else 5), 0))
                for dj in range(-8, 9):
                    bl.append((2, i + dj, 10 if dj == -8 else (9 if dj == 8 else 8), 0))
                return [b for b in bl if 0 <= b[1] < NT]
            jobs.append(dict(kind="A", slot=h, srcs=srcs, blocks=blocks, h=h))
        for h in range(8):
            srcs = [(QK_BASE["qb"] + h, QK_BASE["kb"] + h, V_BASE["b"] + 128 * h, (0, 64)),
                    (QK_BASE["qb"] + h, QK_BASE["kb"] + h, None, (64, 128))]

            def blocks(i):
                return [(m, j, None, m) for m in range(2) for j in range(NT)]
            jobs.append(dict(kind="B", slot=6 + h, srcs=srcs, blocks=blocks, h=h))
        for h in range(8):
            srcs = [(QK_BASE["qc"] + h, QK_BASE["kc"] + h // 4, V_BASE["c"] + 128 * (h // 4), (0, 128))]

            def blocks(i):
                bl = [(0, i - 1, 1, 0), (0, i, None, 0), (0, i + 1, 0, 0)]
                return [b for b in bl if 0 <= b[1] < NT]
            jobs.append(dict(kind="C", slot=14 + h, srcs=srcs, blocks=blocks, h=h))
        for h in range(8):
            srcs = [(QK_BASE["qd"] + h, QK_BASE["kd"] + h, V_BASE["d"] + 128 * h, (0, 128))]

            def blocks(i):
                return [(0, j, ("D", p), 0) for (j, p) in D_BLOCKS[i]]
            jobs.append(dict(kind="D", slot=22 + h, srcs=srcs, blocks=blocks, h=h))
        return jobs

    def phase_att(self, li):
        fw = self.fw
        with ExitStack() as ps:
            NSET = 4
            qs = [fw.sb([128, T], BF16, "qs", ps) for _ in range(NSET)]
            ks = [fw.sb([128, T], BF16, "ks", ps) for _ in range(NSET)]
            vs = [fw.sb([128, NT, 129], BF16, "vs", ps) for _ in range(NSET)]
            masks = fw.sb([128, 11, 128], BF16, "masks", ps)
            ebst = fw.sb([128, NPAT, 128], F32, "ebst", ps)
            eb = [fw.sb([128, NPAT, 128], BF16, "eb", ps) for _ in range(2)]
            Pb = [fw.sb([128, 512], BF16, "Pb", ps) for _ in range(4)]
            rec = [fw.sb([128, 8], F32, "rec", ps) for _ in range(4)]
            u1 = [fw.sb([128, 128], F32, "u1", ps) for _ in range(2)]
            of = [fw.sb([128, 128], F32, "of", ps) for _ in range(2)]
            junk = fw.sb([128, 128], F32, "junk", ps)
            ot = [fw.sb([128, 128], BF16, "ot", ps) for _ in range(3)]
            otT = [fw.sb([128, 128], BF16, "otT", ps) for _ in range(3)]
            fw.dma("sp", masks[:], self.c_mask[:], reads=[self.c_mask], writes=[masks])
            for s in range(NSET):
                fw.op("pool", lambda e, s=s: e.memset(vs[s][:, :, 128:129], 1.0), writes=[vs[s]])
            setrr = 0
            sbank = 0
            prr = 0
            cnt = 0
            sm = self.small
            for job in self.att_jobs():
                kind = job["kind"]
                S = []
                for (qi, ki, vc, pr) in job["srcs"]:
                    if vc is None:
                        S.append((S[-1][0], pr))
                        continue
                    s = setrr % NSET
                    setrr += 1
                    fw.dma("sp", qs[s][:], self.qkT[qi], reads=[self.qkT], writes=[qs[s]])
                    fw.dma("act", ks[s][:], self.qkT[ki], reads=[self.qkT], writes=[ks[s]])
                    fw.dma("sp", vs[s][:, :, 0:128], self.v[:, vc:vc + 128].rearrange("(n p) c -> p n c", p=128), reads=[self.v], writes=[vs[s]])
                    S.append((s, pr))
                ebj = None
                if kind == "D":
                    ebj = eb[job["h"] % 2]
                    fw.dma("act", ebst[:], self.dbias[li, job["h"]], reads=[self.dbias], writes=[ebst])
                    fw.op("act", lambda e: e.activation(out=ebj[:], in_=ebst[:], func=AF.Exp), reads=[ebst], writes=[ebj])
                nacc = 2 if kind == "B" else 1
                tasks = []
                for i in range(NT):
                    blocks = job["blocks"](i)
                    seen = [0] * nacc
                    tot = [sum(1 for b in blocks if b[3] == a) for a in range(nacc)]
                    ngr = (len(blocks) + 3) // 4
                    for gi in range(ngr):
                        grp = []
                        for (si, j, m, a) in blocks[gi * 4:(gi + 1) * 4]:
                            first = seen[a] == 0
                            seen[a] += 1
                            grp.append((si, j, m, a, first, seen[a] == tot[a]))
                        tasks.append(dict(i=i, grp=grp, last=(gi == ngr - 1)))

                def emit_qk(tk):
                    nonlocal sbank, prr
                    tk["Sb"] = self.banks[sbank % 4]
                    sbank += 1
                    tk["P"] = Pb[prr % 4]
                    prr += 1
                    i = tk["i"]
                    for bi, (si, j, m, a, first, last) in enumerate(tk["grp"]):
                        s, (p0, p1) = S[si]
                        fw.op("pe", lambda e: e.matmul(tk["Sb"][:, bi * 128:(bi + 1) * 128], lhsT=ks[s][p0:p1, j * 128:(j + 1) * 128],
                                                       rhs=qs[s][p0:p1, i * 128:(i + 1) * 128], start=True, stop=True),
                              reads=[ks[s], qs[s]], writes=[tk["Sb"]])

                pend = []

                def flush_T():
                    nonlocal sbank
                    o_, i_ = pend.pop(0)
                    tb_ = self.banks[sbank % 4]
                    sbank += 1
                    tbb = tb_[:].bitcast(BF16)
                    oT_ = otT[i_ % 3]
                    fw.op("pe", lambda e: e.transpose(out=tbb[:, 0:128], in_=o_[:], identity=self.ident_b[:]), reads=[o_, self.ident_b], writes=[tb_])
                    fw.op("dve", lambda e: e.tensor_copy(out=oT_[:], in_=tbb[:, 0:128]), reads=[tb_], writes=[oT_])
                    fw.dma("sp", self.oT[job["slot"], :, i_ * 128:(i_ + 1) * 128], oT_[:], reads=[oT_], writes=[self.oT])

                if tasks:
                    emit_qk(tasks[0])
                for ti, tk in enumerate(tasks):
                    if ti + 1 < len(tasks):
                        emit_qk(tasks[ti + 1])
                    i = tk["i"]
                    grp = tk["grp"]
                    Sb, P = tk["Sb"], tk["P"]
                    accb = [self.banks[4 + 2 * (i % 2) + a] for a in range(nacc)]
                    n = len(grp) * 128
                    fw.op("act", lambda e: e.activation(out=P[:, 0:n], in_=Sb[:, 0:n], func=AF.Exp), reads=[Sb], writes=[P])
                    for bi, (si, j, m, a, first, last) in enumerate(grp):
                        if m is None:
                            continue
                        mk = ebj[:, m[1], :] if isinstance(m, tuple) else masks[:, m, :]
                        mb = ebj if isinstance(m, tuple) else masks
                        eng = "pool" if (cnt % 2 == 0) else "dve"
                        cnt += 1
                        fw.op(eng, lambda e: e.tensor_tensor(out=P[:, bi * 128:(bi + 1) * 128], in0=P[:, bi * 128:(bi + 1) * 128], in1=mk, op=ALU.mult),
                              reads=[P, mb], writes=[P])
                    for bi, (si, j, m, a, first, last) in enumerate(grp):
                        s, _ = S[si]
                        fw.op("pe", lambda e: e.matmul(accb[a][:, 0:129], lhsT=P[:, bi * 128:(bi + 1) * 128], rhs=vs[s][:, j, :], start=first, stop=last),
                              reads=[P, vs[s]], writes=[accb[a]], pe_accum=(not first))
                    if pend:
                        flush_T()
                    if not tk["last"]:
                        continue
                    r = rec[i % 4]
                    o = ot[i % 3]
                    if kind in ("A", "D"):
                        fw.op("dve", lambda e: e.reciprocal(out=r[:, 0:1], in_=accb[0][:, 128:129]), reads=[accb[0]], writes=[r])
                        fw.op("act", lambda e: e.activation(out=o[:], in_=accb[0][:, 0:128], func=AF.Copy, scale=r[:, 0:1]), reads=[accb[0], r], writes=[o])
                    elif kind == "C":
                        h = job["h"]
                        fw.op("dve", lambda e: e.tensor_tensor(out=r[:, 1:2], in0=accb[0][:, 128:129], in1=sm[:, h:h + 1], op=ALU.add), reads=[accb[0], sm], writes=[r])
                        fw.op("dve", lambda e: e.reciprocal(out=r[:, 0:1], in_=r[:, 1:2]), reads=[r], writes=[r])
                        fw.op("act", lambda e: e.activation(out=o[:], in_=accb[0][:, 0:128], func=AF.Copy, scale=r[:, 0:1]), reads=[accb[0], r], writes=[o])
                    else:
                        ua, ofa = u1[i % 2], of[i % 2]
                        fw.op("dve", lambda e: e.reciprocal(out=r[:, 0:1], in_=accb[0][:, 128:129]), reads=[accb[0]], writes=[r])
                        fw.op("dve", lambda e: e.reciprocal(out=r[:, 1:2], in_=accb[1][:, 128:129]), reads=[accb[1]], writes=[r])
                        fw.op("dve", lambda e: e.tensor_tensor(out=r[:, 2:3], in0=r[:, 1:2], in1=sm[:, 8:9], op=ALU.mult), reads=[r, sm], writes=[r])
                        fw.op("act", lambda e: e.activation(out=ua[:], in_=accb[0][:, 0:128], func=AF.Copy, scale=r[:, 0:1]), reads=[accb[0], r], writes=[ua])
                        fw.op("dve", lambda e: e.scalar_tensor_tensor(out=ofa[:], in0=accb[1][:, 0:128], scalar=r[:, 2:3], in1=ua[:], op0=ALU.mult, op1=ALU.add),
                              reads=[accb[1], r, ua], writes=[ofa])
                        fw.op("act", lambda e: e.activation(out=junk[:], in_=ofa[:], func=AF.Square, accum_out=r[:, 3:4]), reads=[ofa], writes=[junk, r])
                        fw.op("act", lambda e: e.activation(out=r[:, 4:5], in_=r[:, 3:4], func=AF.Ln, scale=1.0 / 128.0, bias=sm[:, 9:10]), reads=[r, sm], writes=[r])
                        fw.op("act", lambda e: e.activation(out=r[:, 5:6], in_=r[:, 4:5], func=AF.Exp, scale=-0.5), reads=[r], writes=[r])
                        fw.op("dve", lambda e: e.scalar_tensor_tensor(out=o[:], in0=ofa[:], scalar=r[:, 5:6], in1=self.subwb[:], op0=ALU.mult, op1=ALU.mult),
                              reads=[ofa, r, self.subwb], writes=[o])
                    pend.append((o, i))
                while pend:
                    flush_T()
            fw.barrier()

    def phase_merge(self, li):
        fw = self.fw
        TBM = 1024
        NCH = TBM // 512
        bslots = [(0, 6), (6, 8), (14, 8), (22, 8)]
        self.need_w(self.w_branch, self.w_out)
        wbv = self.w_branch[li].rearrange("(s p) n -> p s n", p=128)
        wov = self.w_out[li].rearrange("(kc p) n -> p kc n", p=128)
        with ExitStack() as ps:
            oTb = fw.sb([128, NSLOT, TBM], BF16, "oTb", ps)
            wbs = [fw.sb([128, 8, 128], F32, "wbs", ps) for _ in range(3)]
            wbb = [fw.sb([128, 8, 128], BF16, "wbb", ps) for _ in range(3)]
            gt = [fw.sb([128, TBM], BF16, "gt", ps) for _ in range(4)]
            mix = [fw.sb([128, TBM], F32, "mix", ps) for _ in range(2)]
            tmp = [fw.sb([128, TBM], F32, "tmp", ps) for _ in range(3)]
            mo = [fw.sb([128, TBM], BF16, "mo", ps) for _ in range(3)]
            cnt = 0
            for tb in range(T // TBM):
                t0 = tb * TBM
                fw.dma("sp", oTb[:], self.oT[:, :, t0:t0 + TBM].rearrange("s p t -> p s t"), reads=[self.oT], writes=[oTb])
                for dc in range(16):
                    mx = mix[dc % 2]
                    mo_ = mo[dc % 3]
                    for b, (s0, ns) in enumerate(bslots):
                        ws, wb_ = wbs[cnt % 3], wbb[cnt % 3]
                        g_ = gt[cnt % 4]
                        tp = tmp[cnt % 3]
                        cnt += 1
                        fw.dma("act", ws[:, 0:ns, :], wbv[:, s0:s0 + ns, dc * 128:(dc + 1) * 128], reads=[self.w_branch], writes=[ws])
                        fw.op("pool", lambda e: e.tensor_copy(out=wb_[:, 0:ns, :], in_=ws[:, 0:ns, :]), reads=[ws], writes=[wb_])
                        fw.dma("sp", g_[:], self.gT[b * 16 + dc, :, t0:t0 + TBM], reads=[self.gT], writes=[g_])
                        for c in range(NCH):
                            A = self.banks[(cnt * NCH + c) % 8]
                            for k in range(ns):
                                fw.op("pe", lambda e, k=k: e.matmul(A[:], lhsT=wb_[:, k, :], rhs=oTb[:, s0 + k, c * 512:(c + 1) * 512], start=(k == 0), stop=(k == ns - 1)),
                                      reads=[wb_, oTb], writes=[A], pe_accum=(k > 0))
                            dst = mx if b == 0 else tp
                            fw.op("dve", lambda e: e.tensor_tensor(out=dst[:, c * 512:(c + 1) * 512], in0=A[:], in1=g_[:, c * 512:(c + 1) * 512], op=ALU.mult),
                                  reads=[A, g_], writes=[dst])
                        if 0 < b < 3:
                            fw.op("pool", lambda e: e.tensor_tensor(out=mx[:], in0=mx[:], in1=tp[:], op=ALU.add), reads=[mx, tp], writes=[mx])
                        elif b == 3:
                            fw.op("pool", lambda e: e.tensor_tensor(out=mo_[:], in0=mx[:], in1=tp[:], op=ALU.add), reads=[mx, tp], writes=[mo_])
                    fw.dma("sp", self.mixT[:, dc, t0:t0 + TBM], mo_[:], reads=[mo_], writes=[self.mixT])
            fw.barrier()
        with ExitStack() as ps:
            wos = [fw.sb([128, 4, 512], F32, "wos", ps) for _ in range(2)]
            wob = fw.sb([128, 16, D], BF16, "wob", ps)
            mt = [fw.sb([128, 16, 128], BF16, "mt", ps) for _ in range(2)]
            xr = [fw.sb([128, D], F32, "xr", ps) for _ in range(2)]
            y = [fw.sb([128, D], F32, "y", ps) for _ in range(2)]
            lt = fw.sb([128, D], F32, "lt", ps)
            xb = [fw.sb([128, D], BF16, "xb", ps) for _ in range(2)]
            xts = [fw.sb([128, 16, 128], BF16, "xts", ps) for _ in range(2)]
            gb = fw.sb([128, D], F32, "gb", ps)
            bb_ = fw.sb([128, D], F32, "bb", ps)
            st6 = [fw.sb([128, 24], F32, "st6", ps) for _ in range(2)]
            mv = [fw.sb([128, 2], F32, "mv", ps) for _ in range(2)]
            lsm = [fw.sb([128, 4], F32, "lsm", ps) for _ in range(2)]
            fw.dma("sp", gb[:], self.ln_gain[li, 0:1, :].partition_broadcast(128), reads=[self.ln_gain], writes=[gb])
            fw.dma("sp", bb_[:], self.ln_bias[li, 0:1, :].partition_broadcast(128), reads=[self.ln_bias], writes=[bb_])
            k = 0
            for c in range(4):
                for pz in range(4):
                    s_ = wos[k % 2]
                    k += 1
                    fw.dma("act", s_[:], wov[:, pz * 4:(pz + 1) * 4, c * 512:(c + 1) * 512], reads=[self.w_out], writes=[s_])
                    fw.op("pool", lambda e: e.tensor_copy(out=wob[:, pz * 4:(pz + 1) * 4, c * 512:(c + 1) * 512], in_=s_[:]), reads=[s_], writes=[wob])
            xres_src = self.x if li == 0 else self.xres
            rr = [0]
            cnt = 0
            for n in range(NT):
                m_, x_, y_ = mt[n % 2], xr[n % 2], y[n % 2]
                fw.dma("sp", m_[:], self.mixT[:, :, n * 128:(n + 1) * 128], reads=[self.mixT], writes=[m_])
                fw.dma("act", x_[:], xres_src[n * 128:(n + 1) * 128, :], reads=[xres_src], writes=[x_])
                for c in range(4):
                    A = self.banks[cnt % 4]
                    cnt += 1
                    for kc in range(16):
                        fw.op("pe", lambda e, kc=kc: e.matmul(A[:], lhsT=m_[:, kc, :], rhs=wob[:, kc, c * 512:(c + 1) * 512], start=(kc == 0), stop=(kc == 15)),
                              reads=[m_, wob], writes=[A], pe_accum=(kc > 0))
                    fw.op("dve", lambda e: e.scalar_tensor_tensor(out=y_[:, c * 512:(c + 1) * 512], in0=x_[:, c * 512:(c + 1) * 512], scalar=float(ALPHA),
                                                                  in1=A[:], op0=ALU.mult, op1=ALU.add), reads=[x_, A], writes=[y_])
                self.layer_norm(y_, gb, bb_, y_, lt, st6[n % 2], mv[n % 2], lsm[n % 2])
                fw.dma("sp", self.x1res[n * 128:(n + 1) * 128, :], y_[:], reads=[y_], writes=[self.x1res])
                b_ = xb[n % 2]
                fw.op("act", lambda e: e.activation(out=b_[:], in_=y_[:], func=AF.Copy), reads=[y_], writes=[b_])
                fw.dma("sp", self.x1b[n * 128:(n + 1) * 128, :], b_[:], reads=[b_], writes=[self.x1b])
                self.emit_T(b_, self.x1T, n, xts, rr)
            fw.barrier()

    def phase_ffn(self, li, last):
        fw = self.fw
        IOA = bass.IndirectOffsetOnAxis
        self.need_w(self.w_eg, self.w_eu, self.w_ed)
        with ExitStack() as outer:
            idx_all = fw.sb([128, NT * NEXP], I32, "idx_all", outer)
            gsel_all = fw.sb([128, NT * NEXP], F32, "gsel_all", outer)
            with ExitStack() as ps:
                wrs = fw.sb([128, 16, NEXP], F32, "wrs", ps)
                wrb = fw.sb([128, 16, NEXP], BF16, "wrb", ps)
                xt = [fw.sb([128, 16, 128], BF16, "fxt", ps) for _ in range(2)]
                aff_all = fw.sb([128, NT * NEXP], F32, "aff_all", ps)
                ex = [fw.sb([128, NEXP], F32, "ex", ps) for _ in range(2)]
                s4 = [fw.sb([128, 4], F32, "s4", ps) for _ in range(2)]
                affT = fw.sb([16, T], F32, "affT", ps)
                work = fw.sb([16, T], F32, "work", ps)
                maskT = fw.sb([16, T], F32, "maskT", ps)
                onesT = fw.sb([16, T], F32, "onesT", ps)
                idxT = fw.sb([16, T], F32, "idxT", ps)
                gselT = fw.sb([16, T], F32, "gselT", ps)
                m8 = fw.sb([16, 8], F32, "m8", ps)
                fw.dma("sp", wrs[:], self.w_router[li].rearrange("(kc p) e -> p kc e", p=128), reads=[self.w_router], writes=[wrs])
                fw.op("dve", lambda e: e.tensor_copy(out=wrb[:], in_=wrs[:]), reads=[wrs], writes=[wrb])
                fw.op("pool", lambda e: e.memset(onesT[:], 1.0), writes=[onesT])
                for n in range(NT):
                    x_ = xt[n % 2]
                    lg = self.banks[n % 2]
                    tb_ = self.banks[2 + (n // 4) % 2]
                    fw.dma("sp", x_[:], self.x1T[:, :, n * 128:(n + 1) * 128], reads=[self.x1T], writes=[x_])
                    for kc in range(16):
                        fw.op("pe", lambda e, kc=kc: e.matmul(lg[:, 0:NEXP], lhsT=x_[:, kc, :], rhs=wrb[:, kc, :], start=(kc == 0), stop=(kc == 15)),
                              reads=[x_, wrb], writes=[lg], pe_accum=(kc > 0))
                    s_ = s4[n % 2]
                    e_ = ex[n % 2]
                    fw.op("dve", lambda e: e.tensor_reduce(out=s_[:, 0:1], in_=lg[:, 0:NEXP], axis=mybir.AxisListType.X, op=ALU.max), reads=[lg], writes=[s_])
                    fw.op("dve", lambda e: e.tensor_scalar(out=s_[:, 1:2], in0=s_[:, 0:1], scalar1=-1.0, scalar2=None, op0=ALU.mult), reads=[s_], writes=[s_])
                    fw.op("act", lambda e: e.activation(out=e_[:], in_=lg[:, 0:NEXP], func=AF.Exp, bias=s_[:, 1:2], accum_out=s_[:, 2:3]), reads=[lg, s_], writes=[e_, s_])
                    fw.op("dve", lambda e: e.reciprocal(out=s_[:, 3:4], in_=s_[:, 2:3]), reads=[s_], writes=[s_])
                    fw.op("dve", lambda e: e.tensor_scalar(out=aff_all[:, n * NEXP:(n + 1) * NEXP], in0=e_[:], scalar1=s_[:, 3:4], scalar2=None, op0=ALU.mult),
                          reads=[e_, s_], writes=[aff_all])
                    fw.op("pe", lambda e: e.transpose(out=tb_[0:NEXP, (n % 4) * 128:(n % 4 + 1) * 128], in_=aff_all[:, n * NEXP:(n + 1) * NEXP], identity=self.ident_f[:]),
                          reads=[aff_all, self.ident_f], writes=[tb_])
                    if n % 4 == 3:
                        fw.op("act", lambda e: e.activation(out=affT[:, (n - 3) * 128:(n + 1) * 128], in_=tb_[0:NEXP, :], func=AF.Copy), reads=[tb_], writes=[affT])
                cur = affT
                for it in range(CAP // 8):
                    fw.op("dve", lambda e: e.max(out=m8[:], in_=cur[:]), reads=[cur], writes=[m8])
                    if it < CAP // 8 - 1:
                        fw.op("dve", lambda e: e.match_replace(out=work[:], in_to_replace=m8[:], in_values=cur[:], imm_value=-1.0), reads=[m8, cur], writes=[work])
                        cur = work
                fw.op("dve", lambda e: e.tensor_scalar(out=maskT[:], in0=affT[:], scalar1=m8[:, 7:8], scalar2=None, op0=ALU.is_ge), reads=[affT, m8], writes=[maskT])
                fw.op("dve", lambda e: e.tensor_tensor_scan(out=work[:], data0=onesT[:], data1=maskT[:], initial=0.0, op0=ALU.mult, op1=ALU.add),
                      reads=[onesT, maskT], writes=[work])
                fw.op("dve", lambda e: e.scalar_tensor_tensor(out=idxT[:], in0=maskT[:], scalar=-(1.0 + BIGIDX), in1=work[:], op0=ALU.mult, op1=ALU.add),
                      reads=[maskT, work], writes=[idxT])
                fw.op("dve", lambda e: e.tensor_scalar(out=idxT[:], in0=idxT[:], scalar1=BIGIDX, scalar2=None, op0=ALU.add), reads=[idxT], writes=[idxT])
                fw.op("pool", lambda e: e.tensor_tensor(out=gselT[:], in0=maskT[:], in1=affT[:], op=ALU.mult), reads=[maskT, affT], writes=[gselT])
                bi_, bg_ = self.banks[4], self.banks[5]
                for n in range(NT):
                    fw.op("pe", lambda e: e.transpose(out=bi_[:, n * NEXP:(n + 1) * NEXP], in_=idxT[:, n * 128:(n + 1) * 128], identity=self.ident_f[0:NEXP, 0:NEXP]),
                          reads=[idxT, self.ident_f], writes=[bi_])
                    fw.op("pe", lambda e: e.transpose(out=bg_[:, n * NEXP:(n + 1) * NEXP], in_=gselT[:, n * 128:(n + 1) * 128], identity=self.ident_f[0:NEXP, 0:NEXP]),
                          reads=[gselT, self.ident_f], writes=[bg_])
                fw.op("dve", lambda e: e.tensor_copy(out=idx_all[:], in_=bi_[:]), reads=[bi_], writes=[idx_all])
                fw.op("act", lambda e: e.activation(out=gsel_all[:], in_=bg_[:], func=AF.Copy), reads=[bg_], writes=[gsel_all])
                if self.debug:
                    fw.dma("sp", self.dbg_aff[:], gsel_all[:], reads=[gsel_all], writes=[self.dbg_aff])
                    fw.dma("sp", self.dbg_idx[:], idx_all[:], reads=[idx_all], writes=[self.dbg_idx])
                fw.barrier()
            with ExitStack() as ps:
                xb = [fw.sb([128, D], BF16, "sxb", ps) for _ in range(3)]
                for n in range(NT):
                    b_ = xb[n % 3]
                    fw.dma("sp", b_[:], self.x1b[n * 128:(n + 1) * 128, :], reads=[self.x1b], writes=[b_])
                    for e_i in range(NEXP):
                        c = n * NEXP + e_i
                        fw.idma(self.xin[e_i][:, :], IOA(ap=idx_all[:, c:c + 1], axis=0), b_[:, :], None, CAP - 1, reads=[b_, idx_all], writes=[self.xin[e_i]])
                fw.barrier()
            with ExitStack() as ps:
                xi = [fw.sb([128, D], BF16, "xi", ps) for _ in range(4)]
                xinT = fw.sb([128, 16, CAP], BF16, "xinT", ps)
                wgs = [fw.sb([128, 16, 128], F32, "wgs", ps) for _ in range(4)]
                wgb = [fw.sb([128, 16, 128], BF16, "wgb", ps) for _ in range(4)]
                sg = [fw.sb([128, CAP], F32, "sg", ps) for _ in range(2)]
                hT = fw.sb([128, 8, CAP], BF16, "hT", ps)
                wds = [fw.sb([128, D], F32, "wds", ps) for _ in range(2)]
                wdb = fw.sb([128, 8, D], BF16, "wdb", ps)
                yo = [fw.sb([128, D], F32, "yo", ps) for _ in range(2)]
                cnt = 0
                for e_i in range(NEXP):
                    for jt in range(4):
                        fw.dma("sp", xi[jt][:], self.xin[e_i][jt * 128:(jt + 1) * 128, :], reads=[self.xin[e_i]], writes=[xi[jt]])
                        for half in range(2):
                            bank = self.banks[cnt % 2]
                            cnt += 1
                            bb = bank[:].bitcast(BF16)
                            for k in range(8):
                                kc = half * 8 + k
                                fw.op("pe", lambda e, kc=kc, k=k: e.transpose(out=bb[:, k * 128:(k + 1) * 128], in_=xi[jt][:, kc * 128:(kc + 1) * 128], identity=self.ident_b[:]),
                                      reads=[xi[jt], self.ident_b], writes=[bank])
                            src = bb[:, :].rearrange("p (a b) -> p a b", b=128)
                            dst = xinT[:, half * 8:(half + 1) * 8, jt * 128:(jt + 1) * 128]
                            if half == 0:
                                fw.op("act", lambda e: e.activation(out=dst, in_=src, func=AF.Copy), reads=[bank], writes=[xinT])
                            else:
                                fw.op("dve", lambda e: e.tensor_copy(out=dst, in_=src), reads=[bank], writes=[xinT])
                    wgv = self.w_eg[li, e_i].rearrange("(kc p) f -> p kc f", p=128)
                    wuv = self.w_eu[li, e_i].rearrange("(kc p) f -> p kc f", p=128)
                    for ft in range(8):
                        gs_, gb_ = wgs[(2 * ft) % 4], wgb[(2 * ft) % 4]
                        us_, ub_ = wgs[(2 * ft + 1) % 4], wgb[(2 * ft + 1) % 4]
                        fw.dma("sp", gs_[:], wgv[:, :, ft * 128:(ft + 1) * 128], reads=[self.w_eg], writes=[gs_])
                        fw.dma("act", us_[:], wuv[:, :, ft * 128:(ft + 1) * 128], reads=[self.w_eu], writes=[us_])
                        fw.op("pool", lambda e: e.tensor_copy(out=gb_[:], in_=gs_[:]), reads=[gs_], writes=[gb_])
                        fw.op("act", lambda e: e.activation(out=ub_[:], in_=us_[:], func=AF.Copy), reads=[us_], writes=[ub_])
                        G = self.banks[2 + (ft % 2) * 2]
                        U = self.banks[3 + (ft % 2) * 2]
                        for kc in range(16):
                            fw.op("pe", lambda e, kc=kc: e.matmul(G[:], lhsT=gb_[:, kc, :], rhs=xinT[:, kc, :], start=(kc == 0), stop=(kc == 15)),
                                  reads=[gb_, xinT], writes=[G], pe_accum=(kc > 0))
                        for kc in range(16):
                            fw.op("pe", lambda e, kc=kc: e.matmul(U[:], lhsT=ub_[:, kc, :], rhs=xinT[:, kc, :], start=(kc == 0), stop=(kc == 15)),
                                  reads=[ub_, xinT], writes=[U], pe_accum=(kc > 0))
                        s_ = sg[ft % 2]
                        fw.op("act", lambda e: e.activation(out=s_[:], in_=G[:], func=AF.Silu), reads=[G], writes=[s_])
                        fw.op("dve", lambda e: e.tensor_tensor(out=hT[:, ft, :], in0=s_[:], in1=U[:], op=ALU.mult), reads=[s_, U], writes=[hT])
                    for fc in range(8):
                        d_ = wds[fc % 2]
                        fw.dma("act", d_[:], self.w_ed[li, e_i, fc * 128:(fc + 1) * 128, :], reads=[self.w_ed], writes=[d_])
                        if fc % 2 == 0:
                            fw.op("pool", lambda e: e.tensor_copy(out=wdb[:, fc, :], in_=d_[:]), reads=[d_], writes=[wdb])
                        else:
                            fw.op("dve", lambda e: e.tensor_copy(out=wdb[:, fc, :], in_=d_[:]), reads=[d_], writes=[wdb])
                    for jt in range(4):
                        y_ = yo[jt % 2]
                        for dc in range(4):
                            Y = self.banks[6 + dc % 2]
                            for fc in range(8):
                                fw.op("pe", lambda e, fc=fc: e.matmul(Y[:], lhsT=hT[:, fc, jt * 128:(jt + 1) * 128], rhs=wdb[:, fc, dc * 512:(dc + 1) * 512],
                                                                      start=(fc == 0), stop=(fc == 7)), reads=[hT, wdb], writes=[Y], pe_accum=(fc > 0))
                            if dc % 2 == 0:
                                fw.op("act", lambda e: e.activation(out=y_[:, dc * 512:(dc + 1) * 512], in_=Y[:], func=AF.Copy), reads=[Y], writes=[y_])
                            else:
                                fw.op("dve", lambda e: e.tensor_copy(out=y_[:, dc * 512:(dc + 1) * 512], in_=Y[:]), reads=[Y], writes=[y_])
                        fw.dma("sp", self.yexp[e_i][jt * 128:(jt + 1) * 128, :], y_[:], reads=[y_], writes=[self.yexp[e_i]])
                fw.barrier()
            with ExitStack() as ps:
                acc = [fw.sb([128, D], F32, "acc", ps) for _ in range(2)]
                gbuf = [fw.sb([128, D], F32, "gbuf", ps) for _ in range(4)]
                lt = fw.sb([128, D], F32, "lt2", ps)
                gb = fw.sb([128, D], F32, "gb2", ps)
                bb_ = fw.sb([128, D], F32, "bb2", ps)
                xb = [fw.sb([128, D], BF16, "xb2", ps) for _ in range(2)]
                xts = [fw.sb([128, 16, 128], BF16, "xts2", ps) for _ in range(2)]
                st6 = [fw.sb([128, 24], F32, "st62", ps) for _ in range(2)]
                mv = [fw.sb([128, 2], F32, "mv2", ps) for _ in range(2)]
                lsm = [fw.sb([128, 4], F32, "lsm2", ps) for _ in range(2)]
                fw.dma("sp", gb[:], self.ln_gain[li, 1:2, :].partition_broadcast(128), reads=[self.ln_gain], writes=[gb])
                fw.dma("sp", bb_[:], self.ln_bias[li, 1:2, :].partition_broadcast(128), reads=[self.ln_bias], writes=[bb_])
                for g_ in gbuf:
                    fw.op("pool", lambda e: e.memset(g_[:], 0.0), writes=[g_])
                rr = [0]
                k = 0
                for n in range(NT):
                    a_ = acc[n % 2]
                    fw.dma("sp", a_[:], self.x1res[n * 128:(n + 1) * 128, :], reads=[self.x1res], writes=[a_])
                    fw.op("act", lambda e: e.activation(out=a_[:], in_=a_[:], func=AF.Copy, scale=float(ALPHA)), reads=[a_], writes=[a_])
                    for e_i in range(NEXP):
                        c = n * NEXP + e_i
                        g_ = gbuf[k % 4]
                        k += 1
                        fw.idma(g_[:, :], None, self.yexp[e_i][:, :], IOA(ap=idx_all[:, c:c + 1], axis=0), CAP - 1, reads=[self.yexp[e_i], idx_all], writes=[g_])
                        fw.op("dve", lambda e: e.scalar_tensor_tensor(out=a_[:], in0=g_[:], scalar=gsel_all[:, c:c + 1], in1=a_[:], op0=ALU.mult, op1=ALU.add),
                              reads=[g_, gsel_all, a_], writes=[a_])
                    self.layer_norm(a_, gb, bb_, a_, lt, st6[n % 2], mv[n % 2], lsm[n % 2])
                    if last:
                        fw.dma("sp", self.out[n * 128:(n + 1) * 128, :], a_[:], reads=[a_], writes=[self.out])
                    else:
                        fw.dma("sp", self.xres[n * 128:(n + 1) * 128, :], a_[:], reads=[a_], writes=[self.xres])
                        b_ = xb[n % 2]
                        fw.op("act", lambda e: e.activation(out=b_[:], in_=a_[:], func=AF.Copy), reads=[a_], writes=[b_])
                        self.emit_T(b_, self.xT, n, xts, rr)
                fw.barrier()


_CONSTS = None


def _consts():
    global _CONSTS
    if _CONSTS is None:
        _CONSTS = dict(c_ident=np.eye(128, dtype=np.float32), c_perm=_perms(), c_mask=_masks(), c_rope=_rope_tables())
    return _CONSTS


def _dbias(na_rpb):
    L = na_rpb.shape[0]
    out = np.empty((L, 8, 128, NPAT, 128), np.float32)
    for p, (roff, coff, valid) in enumerate(D_PATS):
        g = na_rpb[:, :, roff, coff]
        out[:, :, :, p, :] = np.where(valid[None, None], g, np.float32(NEGM))
    return out


def _lconst(l0, n):
    out = np.zeros((n, 4), np.float32)
    for i in range(n):
        li = 0.8 - 0.6 * math.exp(-0.3 * (l0 + i))
        out[i, 0] = li
        out[i, 1] = 1.0 - li
    return out


def _layer_inputs(inp, l0, n):
    sl = slice(l0, l0 + n)
    f = lambda a: np.ascontiguousarray(np.asarray(a, dtype=np.float32))
    d = dict(w_in=f(inp["w_in"][sl]), b_gate=f(inp["b_gate"][sl]), w_branch=f(inp["w_branch"][sl]), w_out=f(inp["w_out"][sl]),
             diff_lambda=f(inp["diff_lambda"][sl]).reshape(n, 256), diff_subln=f(inp["diff_subln"][sl]), sink_logit=f(inp["sink_logit"][sl]),
             dbias=_dbias(np.asarray(inp["na_rpb"][sl], dtype=np.float32)), w_router=f(inp["w_router"][sl]),
             w_exp_gate=f(inp["w_exp_gate"][sl]), w_exp_up=f(inp["w_exp_up"][sl]), w_exp_down=f(inp["w_exp_down"][sl]),
             ln_gain=f(inp["ln_gain"][sl]), ln_bias=f(inp["ln_bias"][sl]), lconst=_lconst(l0, n))
    d.update(_consts())
    return d


NLAYERS_PER_LAUNCH = 4
NCORES = 8
GATHERED = ("w_in", "w_branch", "w_out", "w_exp_gate", "w_exp_up", "w_exp_down")
_PROGS = {}


def _get_prog(nl, gather=True):
    key = (nl, gather)
    if key not in _PROGS:
        _PROGS[key] = Prog(nl, 0, False, gather).build()
    return _PROGS[key]


def _shard_rows(a, r):
    a2 = a.reshape(-1, a.shape[-1])
    n = a2.shape[0] // 8
    return np.ascontiguousarray(a2[r * n:(r + 1) * n])


def kernel(**inputs):
    x = np.ascontiguousarray(np.asarray(inputs["x"], dtype=np.float32))
    B = x.shape[0]
    cur = [x[b] for b in range(B)]
    nl = NLAYERS_PER_LAUNCH
    nc = _get_prog(nl, False)
    for l0 in range(0, DEPTH, nl):
        shared = _layer_inputs(inputs, l0, nl)
        in_maps = []
        for b in range(B):
            m = dict(shared)
            m["x"] = cur[b]
            in_maps.append(m)
        res = run_bass_kernel_spmd(nc, in_maps, core_ids=list(range(B)))
        cur = [np.asarray(res.results[b]["out"], dtype=np.float32) for b in range(B)]
        del res, in_maps, shared
    return np.stack(cur, 0)
```

```python
import math
from contextlib import ExitStack

import numpy as np
import ml_dtypes
import concourse.bass as bass
import concourse.mybir as mybir
from concourse.bass_utils import run_bass_kernel_spmd

F32 = mybir.dt.float32
BF16 = mybir.dt.bfloat16
I32 = mybir.dt.int32
AF = mybir.ActivationFunctionType
ALU = mybir.AluOpType

D = 2048
T = 4096
NT = T // 128
DEPTH = 4
A_HPG = 6
IN_WIDTHS = (2304, 2304, 2304, 1024, 1024, 1024, 1024, 256, 256, 1024, 1024, 1024, 8192)
IN_OFF = [0]
for _w in IN_WIDTHS:
    IN_OFF.append(IN_OFF[-1] + _w)
IN_TOTAL = IN_OFF[-1]
BR_W = (768, 1024, 1024, 1024)
BR_OFF = (0, 768, 1792, 2816)
NSLOT = 30
NEXP = 16
CAP = 512
DEXP = 1024
ALPHA = (2.0 * DEPTH) ** 0.25
LN_EPS = 1e-5
BIGIDX = 1.0e6
NEGM = -30000.0


class Buf:
    __slots__ = ("t", "lw", "rd", "name", "kind")

    def __init__(self, t, name="", kind="sb"):
        self.t = t
        self.lw = None
        self.rd = {}
        self.name = name
        self.kind = kind

    def __getitem__(self, idx):
        return self.t[idx]


class _View:
    def __init__(self, base, ap):
        self._b = base
        self.t = ap

    def __getitem__(self, idx):
        return self.t[idx]

    kind = property(lambda s: s._b.kind)
    lw = property(lambda s: s._b.lw, lambda s, v: setattr(s._b, "lw", v))
    rd = property(lambda s: s._b.rd, lambda s, v: setattr(s._b, "rd", v))


class FW:
    NDMASEM = 12

    def __init__(self, nc, stack):
        self.nc = nc
        self.stack = stack
        self.eng = {"pe": nc.tensor, "act": nc.scalar, "dve": nc.vector, "pool": nc.gpsimd, "sp": nc.sync}
        self.sem = {}
        self.cnt = {}
        for e in self.eng:
            self.sem[e] = stack.enter_context(nc.semaphore("s_" + e))
            self.cnt[e] = 0
        self.dsem = {}
        self.dcnt = {}
        self.dnext = {}
        self.nds = {"sp": 24, "act": self.NDMASEM, "pool": self.NDMASEM}
        for q in ("sp", "act", "pool"):
            self.dsem[q] = [stack.enter_context(nc.semaphore(f"d_{q}{i}")) for i in range(self.nds[q])]
            self.dcnt[q] = [0] * self.nds[q]
            self.dnext[q] = 0
        self.known = {e: {} for e in self.eng}
        self.ext_in = []
        self.ninstr = 0
        self.uid = 0
        self.breg = nc.gpsimd.alloc_register("idma_bound")
        nc.gpsimd.reg_mov(self.breg, CAP - 1)

    def sb(self, shape, dt, name=None, stack=None):
        self.uid += 1
        name = (name or "t") + f"_{self.uid}"
        t = (stack or self.stack).enter_context(self.nc.sbuf_tensor(name, list(shape), dt))
        return Buf(t, name)

    def ps(self, shape, dt, name):
        t = self.stack.enter_context(self.nc.psum_tensor(name, list(shape), dt))
        return Buf(t, name, "ps")

    def dram(self, name, shape, dt, kind="Internal"):
        t = self.nc.dram_tensor(name, list(shape), dt, kind=kind)
        if kind == "ExternalInput":
            self.ext_in.append(name)
        return Buf(t.ap(), name, "dram")

    def _wait(self, e, key, count):
        kn = self.known[e]
        if kn.get(key, 0) >= count:
            return
        s = self.sem[key] if isinstance(key, str) else self.dsem[key[0]][key[1]]
        self.eng[e].wait_ge(s, count)
        kn[key] = count

    def _deps(self, e, reads, writes, pe_accum=False):
        for b in reads:
            if b.kind == "dram":
                continue
            if b.lw is not None:
                self._wait(e, b.lw[0], b.lw[1])
            if b.kind == "ps":
                for k, c in b.rd.items():
                    if k != e:
                        self._wait(e, k, c)
        for b in writes:
            if b.kind == "dram":
                continue
            if b.lw is not None:
                if not (pe_accum and b.lw[0] == "pe" and e == "pe"):
                    self._wait(e, b.lw[0], b.lw[1])
            for k, c in b.rd.items():
                self._wait(e, k, c)

    def _mark(self, key, count, reads, writes):
        for b in reads:
            if b.kind != "dram":
                b.rd[key] = count
        for b in writes:
            if b.kind != "dram":
                b.lw = (key, count)
                b.rd = {}

    def op(self, e, fn, reads=(), writes=(), pe_accum=False):
        self._deps(e, reads, writes, pe_accum)
        ins = fn(self.eng[e])
        self.cnt[e] += 1
        ins.then_inc(self.sem[e], 1)
        self._mark(e, self.cnt[e], reads, writes)
        self.ninstr += 1
        return ins

    def _dma_slot(self, q):
        i = self.dnext[q]
        self.dnext[q] = (i + 1) % self.nds[q]
        key = (q, i)
        if self.dcnt[q][i] > 0:
            self._wait(q, key, self.dcnt[q][i])
        return i, key

    def dma(self, q, out_ap, in_ap, reads=(), writes=(), **kw):
        i, key = self._dma_slot(q)
        self._deps(q, reads, writes)
        ins = self.eng[q].dma_start(out=out_ap, in_=in_ap, **kw)
        self.dcnt[q][i] += 16
        ins.then_inc(self.dsem[q][i], 16)
        self._mark(key, self.dcnt[q][i], reads, writes)
        self.ninstr += 1
        return ins

    def idma(self, out_ap, out_off, in_ap, in_off, bound, reads=(), writes=()):
        q = "pool"
        i, key = self._dma_slot(q)
        self._deps(q, reads, writes)
        ins = self.nc.gpsimd.indirect_dma_start(out=out_ap, out_offset=out_off, in_=in_ap, in_offset=in_off,
                                                bounds_check=self.breg, oob_is_err=False)
        self.dcnt[q][i] += 16
        ins.then_inc(self.dsem[q][i], 16)
        self._mark(key, self.dcnt[q][i], reads, writes)
        self.ninstr += 1
        return ins

    def barrier(self):
        for e in self.eng:
            for e2 in self.eng:
                if e2 != e and self.cnt[e2] > 0:
                    self._wait(e, e2, self.cnt[e2])
            for q in self.dsem:
                for i in range(self.nds[q]):
                    if self.dcnt[q][i] > 0:
                        self._wait(e, (q, i), self.dcnt[q][i])

    def finish(self, bufs):
        self.barrier()


def _d_patterns():
    kk = np.arange(128)[:, None]
    qq = np.arange(128)[None, :]
    pats, plist, blocks = {}, [], {}
    for i in range(NT):
        r = 2 * i + qq // 64
        c = qq % 64
        rs = np.clip(r - 4, 0, 56)
        cs = np.clip(c - 8, 0, 48)
        for j in range(NT):
            kr = 2 * j + kk // 64
            kc = kk % 64
            valid = (kr >= rs) & (kr < rs + 8) & (kc >= cs) & (kc < cs + 16)
            if not valid.any():
                continue
            roff = np.where(valid, kr - r + 7, 0)
            coff = np.where(valid, np.clip(kc - c, -15, 15) + 15, 0)
            key = (roff.tobytes(), coff.tobytes(), valid.tobytes())
            if key not in pats:
                pats[key] = len(plist)
                plist.append((roff, coff, valid))
            blocks.setdefault(i, []).append((j, pats[key]))
    return blocks, plist


D_BLOCKS, D_PATS = _d_patterns()
NPAT = len(D_PATS)


def _masks():
    kk = np.arange(128)[:, None]
    qq = np.arange(128)[None, :]
    d = kk - qq
    m4 = (d % 4) == 0
    m16 = (d % 16) == 0
    ms = [kk <= qq, kk >= qq, np.abs(d) <= 64, d <= -64, d >= 64,
          m4, m4 & (kk <= qq), m4 & (kk >= qq), m16, m16 & (kk <= qq), m16 & (kk >= qq)]
    return np.stack([m.astype(np.float32) for m in ms], axis=1).astype(ml_dtypes.bfloat16)


def _rope_tables():
    pos = np.arange(T, dtype=np.float32)
    out = []
    for half in (64, 32):
        inv = (np.float32(10000.0) ** (-np.arange(half, dtype=np.float32) / np.float32(half))).astype(np.float32)
        ang = (pos[None, :] * inv[:, None]).astype(np.float32)
        cos = np.cos(ang).astype(np.float32)
        sin = np.sin(ang).astype(np.float32)
        p = np.arange(128)
        i = p % half
        sign = np.where((p % (2 * half)) < half, -1.0, 1.0).astype(np.float32)
        out.append(cos[i])
        out.append(sin[i] * sign[:, None])
    return np.stack(out, 0).astype(np.float32)


def _perms():
    p = np.arange(128)
    pa = np.zeros((128, 128), np.float32)
    pa[(p + 64) % 128, p] = 1.0
    pb = np.zeros((128, 128), np.float32)
    pb[(p // 64) * 64 + ((p % 64) + 32) % 64, p] = 1.0
    return np.stack([pa, pb], 0).astype(ml_dtypes.bfloat16)


def _head_tiles():
    s128 = 128.0 ** -0.5
    s64 = 64.0 ** -0.5
    groups = [("qa", IN_OFF[0], 18, "A", s128), ("ka", IN_OFF[1], 18, "A", 1.0),
              ("qb", IN_OFF[3], 8, "B", s64), ("kb", IN_OFF[4], 8, "B", 1.0),
              ("qc", IN_OFF[6], 8, "A", s128), ("kc", IN_OFF[7], 2, "A", 1.0),
              ("qd", IN_OFF[9], 8, None, s128), ("kd", IN_OFF[10], 8, None, 1.0)]
    tiles = []
    base = {}
    for name, off, n, rk, sc in groups:
        base[name] = len(tiles)
        for t in range(n):
            tiles.append((off + 128 * t, rk, sc))
    return tiles, base


HEAD_TILES, QK_BASE = _head_tiles()
NQK = len(HEAD_TILES)
V_SLABS = []
_dst = 0
for _slot in (2, 5, 8, 11):
    _o, _w = IN_OFF[_slot], IN_WIDTHS[_slot]
    _c = 0
    while _c < _w:
        _ww = min(512, _w - _c)
        V_SLABS.append((_o + _c, _ww, _dst))
        _c += _ww
        _dst += _ww
V_W = _dst
V_BASE = {"a": 0, "b": 2304, "c": 3328, "d": 3584}


class Prog:
    def __init__(self, nlayers, first_layer, debug=False, gather=False):
        self.nl = nlayers
        self.first = first_layer
        self.debug = debug
        self.gather = gather

    def build(self):
        nc = bass.Bass("TRN2", target_bir_lowering=False)
        self.nc = nc
        with ExitStack() as st:
            fw = FW(nc, st)
            self.fw = fw
            self._declare(st)
            for li in range(self.nl):
                self.layer(li)
            fw.finish([self.out])
        return nc

    def _declare(self, st):
        fw, nc, L = self.fw, self.nc, self.nl
        ph = getattr(self, "phases", "0spamf")
        feed = getattr(self, "feed", ())
        want = getattr(self, "want", ())

        def ein(n, s, dt=F32, need=True):
            return fw.dram(n, s, dt, kind="ExternalInput") if need else None

        def win(n, s, need):
            if not need:
                return None
            if not self.gather:
                return fw.dram(n, s, F32, kind="ExternalInput")
            rows = 1
            for d_ in s[:-1]:
                rows *= d_
            assert rows % 8 == 0
            shard = fw.dram(n, [rows // 8, s[-1]], F32, kind="ExternalInput")
            full = fw.dram(n + "_full", [rows, s[-1]], F32, kind="Internal")
            self.gathers.append((shard, full))
            names = "abcdefg"[:len(s) - 1]
            if len(s) == 2:
                return full
            pat = "(" + " ".join(names) + ") z -> " + " ".join(names) + " z"
            kw = {names[i]: s[i] for i in range(len(s) - 2)}
            return Buf(full.t.rearrange(pat, **kw), n + "_fullv") if False else _View(full, full.t.rearrange(pat, **kw))

        self.gathers = []
        self.x = ein("x", [T, D])
        self.w_in = win("w_in", [L, D, IN_TOTAL], "p" in ph)
        self.b_gate = ein("b_gate", [L, 8192])
        self.w_branch = win("w_branch", [L, 3840, D], "m" in ph)
        self.w_out = win("w_out", [L, D, D], "m" in ph)
        self.diff_lambda = ein("diff_lambda", [L, 256])
        self.diff_subln = ein("diff_subln", [L, 128])
        self.sink_logit = ein("sink_logit", [L, 8])
        self.dbias = ein("dbias", [L, 8, 128, NPAT, 128], need="a" in ph)
        self.w_router = ein("w_router", [L, D, NEXP], need="f" in ph)
        self.w_eg = win("w_exp_gate", [L, NEXP, D, DEXP], "f" in ph)
        self.w_eu = win("w_exp_up", [L, NEXP, D, DEXP], "f" in ph)
        self.w_ed = win("w_exp_down", [L, NEXP, DEXP, D], "f" in ph)
        self.ln_gain = ein("ln_gain", [L, 2, D])
        self.ln_bias = ein("ln_bias", [L, 2, D])
        self.lconst = ein("lconst", [L, 4])
        self.c_ident = ein("c_ident", [128, 128])
        self.c_perm = ein("c_perm", [2, 128, 128], BF16)
        self.c_mask = ein("c_mask", [128, 11, 128], BF16)
        self.c_rope = ein("c_rope", [4, 128, T], need="p" in ph)
        self.out = fw.dram("out", [T, D], F32, kind="ExternalOutput")

        def scr(n, s, dt):
            k = "ExternalInput" if n in feed else ("ExternalOutput" if n in want else "Internal")
            return fw.dram(n, s, dt, kind=k)
        self.xT = scr("xT", [128, 16, T], BF16)
        self.xres = scr("xres", [T, D], F32)
        self.qkT = scr("qkT", [NQK, 128, T], BF16)
        self.v = scr("v", [T, V_W], BF16)
        self.gT = scr("gT", [64, 128, T], BF16)
        self.oT = scr("oT", [NSLOT, 128, T], BF16)
        self.mixT = scr("mixT", [128, 16, T], BF16)
        self.x1res = scr("x1res", [T, D], F32)
        self.x1b = scr("x1b", [T, D], BF16)
        self.x1T = scr("x1T", [128, 16, T], BF16)
        self.xin = [scr(f"xin{e}", [CAP, D], BF16) for e in range(NEXP)]
        self.yexp = [scr(f"yexp{e}", [CAP, D], F32) for e in range(NEXP)]
        self.dbg_aff = scr("dbg_aff", [128, NT * NEXP], F32)
        self.dbg_idx = scr("dbg_idx", [128, NT * NEXP], I32)
        self.debug = ("dbg_aff" in want)
        self.gwait = {}
        for shard, full in self.gathers:
            rows, cols = shard.t.shape
            stage = fw.dram(shard.name + "_st", [rows, cols], F32)
            step = max(1, min(rows, (4 << 20) // (cols * 4)))
            keys = []
            for r0 in range(0, rows, step):
                r1 = min(rows, r0 + step)
                fw.dma("sp", stage[r0:r1, :], shard[r0:r1, :])
                q_i = (fw.dnext["sp"] - 1) % fw.nds["sp"]
                keys.append((("sp", q_i), fw.dcnt["sp"][q_i]))
            for k_, c_ in keys:
                fw._wait("pool", k_, c_)
            i, key = fw._dma_slot("pool")
            ins = nc.gpsimd.collective_compute("AllGather", ALU.bypass, ins=[stage[:, :]], outs=[full[:, :]],
                                               replica_groups=[list(range(8))])
            fw.dcnt["pool"][i] += 16
            ins.then_inc(fw.dsem["pool"][i], 16)
            self.gwait[full.name] = (key, fw.dcnt["pool"][i])
        self.banks = [fw.ps([128, 512], F32, f"bank{i}") for i in range(8)]
        self.ident_f = fw.sb([128, 128], F32, "ident_f")
        self.ident_b = fw.sb([128, 128], BF16, "ident_b")
        self.perm = fw.sb([128, 2, 128], BF16, "perm")
        self.small = fw.sb([128, 512], F32, "small")
        self.bgc = fw.sb([128, 64], F32, "bgc")
        self.subwb = fw.sb([128, 128], F32, "subwb")
        fw.dma("sp", self.ident_f[:], self.c_ident[:], reads=[self.c_ident], writes=[self.ident_f])
        fw.op("dve", lambda e: e.tensor_copy(out=self.ident_b[:], in_=self.ident_f[:]), reads=[self.ident_f], writes=[self.ident_b])
        fw.dma("sp", self.perm[:], self.c_perm[:].rearrange("a p n -> p a n"), reads=[self.c_perm], writes=[self.perm])

    def need_w(self, *ws):
        for w in ws:
            nm = getattr(getattr(w, "_b", w), "name", None)
            if nm in self.gwait:
                key, cnt = self.gwait.pop(nm)
                for e in self.fw.eng:
                    self.fw._wait(e, key, cnt)

    def emit_T(self, src, dst_dram, n, st_pool, eng_rr):
        fw = self.fw
        xts = st_pool[n % len(st_pool)]
        for half in range(2):
            bank = self.banks[4 + (eng_rr[0] % 4)]
            eng_rr[0] += 1
            bb = bank[:].bitcast(BF16)
            for k in range(8):
                kc = half * 8 + k
                fw.op("pe", lambda e, kc=kc, k=k: e.transpose(out=bb[:, k * 128:(k + 1) * 128], in_=src[:, kc * 128:(kc + 1) * 128],
                                                              identity=self.ident_b[:]), reads=[src, self.ident_b], writes=[bank])
            eng = "act" if half == 0 else "dve"
            if eng == "act":
                fw.op("act", lambda e: e.activation(out=xts[:, half * 8:(half + 1) * 8, :], in_=bb[:, :].rearrange("p (a b) -> p a b", b=128), func=AF.Copy),
                      reads=[bank], writes=[xts])
            else:
                fw.op("dve", lambda e: e.tensor_copy(out=xts[:, half * 8:(half + 1) * 8, :], in_=bb[:, :].rearrange("p (a b) -> p a b", b=128)),
                      reads=[bank], writes=[xts])
        fw.dma("sp", dst_dram[:, :, n * 128:(n + 1) * 128], xts[:], reads=[xts], writes=[dst_dram])

    def layer_norm(self, y, gb, bb_, out, tmp, st6, mv, sm):
        fw = self.fw
        for c in range(4):
            fw.op("dve", lambda e, c=c: e.bn_stats(out=st6[:, c * 6:(c + 1) * 6], in_=y[:, c * 512:(c + 1) * 512]), reads=[y], writes=[st6])
        fw.op("dve", lambda e: e.bn_aggr(out=mv[:], in_=st6[:]), reads=[st6], writes=[mv])
        fw.op("act", lambda e: e.activation(out=sm[:, 0:1], in_=mv[:, 1:2], func=AF.Ln, bias=self.small[:, 9:10]), reads=[mv, self.small], writes=[sm])
        fw.op("act", lambda e: e.activation(out=sm[:, 1:2], in_=sm[:, 0:1], func=AF.Exp, scale=-0.5), reads=[sm], writes=[sm])
        fw.op("dve", lambda e: e.tensor_scalar(out=sm[:, 2:3], in0=mv[:, 0:1], scalar1=sm[:, 1:2], scalar2=-1.0, op0=ALU.mult, op1=ALU.mult),
              reads=[mv, sm], writes=[sm])
        fw.op("act", lambda e: e.activation(out=tmp[:], in_=y[:], func=AF.Identity, scale=sm[:, 1:2], bias=sm[:, 2:3]), reads=[y, sm], writes=[tmp])
        fw.op("dve", lambda e: e.tensor_tensor(out=tmp[:], in0=tmp[:], in1=gb[:], op=ALU.mult), reads=[tmp, gb], writes=[tmp])
        fw.op("pool", lambda e: e.tensor_tensor(out=out[:], in0=tmp[:], in1=bb_[:], op=ALU.add), reads=[tmp, bb_], writes=[out])

    def layer(self, li):
        fw = self.fw
        gl = self.first + li
        ph = getattr(self, "phases", "0spamf")
        if li == 0 and "0" in ph:
            self.phase0()
        if "s" in ph:
            self.phase_small(li)
        if "p" in ph:
            self.phase_proj(li)
        if "a" in ph:
            self.phase_att(li)
        if "m" in ph:
            self.phase_merge(li)
        if "f" in ph:
            self.phase_ffn(li, last=(li == self.nl - 1))

    def phase0(self):
        fw = self.fw
        with ExitStack() as ps:
            xt = [fw.sb([128, D], F32, "p0x", ps) for _ in range(2)]
            xb = [fw.sb([128, D], BF16, "p0b", ps) for _ in range(2)]
            xts = [fw.sb([128, 16, 128], BF16, "p0t", ps) for _ in range(2)]
            rr = [0]
            for n in range(NT):
                a, b = xt[n % 2], xb[n % 2]
                fw.dma("act", a[:], self.x[n * 128:(n + 1) * 128, :], reads=[self.x], writes=[a])
                fw.op("pool", lambda e: e.tensor_copy(out=b[:], in_=a[:]), reads=[a], writes=[b])
                self.emit_T(b, self.xT, n, xts, rr)
            fw.barrier()

    def phase_small(self, li):
        fw = self.fw
        sm = self.small
        with ExitStack() as ps:
            lam = fw.sb([128, 256], F32, "lam", ps)
            tmp = fw.sb([128, 128], F32, "lamt", ps)
            bg64 = fw.sb([64, 128], F32, "bg64", ps)
            fw.dma("sp", bg64[:], self.b_gate[li, :].rearrange("(c p) -> c p", p=128), reads=[self.b_gate], writes=[bg64])
            bank = self.banks[0]
            fw.op("pe", lambda e: e.transpose(out=bank[:, 0:64], in_=bg64[:], identity=self.ident_f[0:64, 0:64]), reads=[bg64, self.ident_f], writes=[bank])
            fw.op("dve", lambda e: e.tensor_copy(out=self.bgc[:], in_=bank[:, 0:64]), reads=[bank], writes=[self.bgc])
            fw.dma("sp", sm[:, 0:8], self.sink_logit[li:li + 1, :].partition_broadcast(128), reads=[self.sink_logit], writes=[sm])
            fw.dma("sp", sm[:, 16:20], self.lconst[li:li + 1, :].partition_broadcast(128), reads=[self.lconst], writes=[sm])
            fw.dma("sp", lam[:], self.diff_lambda[li:li + 1, :].partition_broadcast(128), reads=[self.diff_lambda], writes=[lam])
            fw.dma("sp", self.subwb[:], self.diff_subln[li:li + 1, :].partition_broadcast(128), reads=[self.diff_subln], writes=[self.subwb])
            fw.op("act", lambda e: e.activation(out=sm[:, 0:8], in_=sm[:, 0:8], func=AF.Exp), reads=[sm], writes=[sm])
            fw.op("dve", lambda e: e.memset(sm[:, 9:10], LN_EPS), writes=[sm])
            fw.op("dve", lambda e: e.tensor_tensor(out=tmp[:, 0:64], in0=lam[:, 0:64], in1=lam[:, 64:128], op=ALU.mult), reads=[lam], writes=[tmp])
            fw.op("dve", lambda e: e.tensor_tensor(out=tmp[:, 64:128], in0=lam[:, 128:192], in1=lam[:, 192:256], op=ALU.mult), reads=[lam], writes=[tmp])
            fw.op("dve", lambda e: e.tensor_reduce(out=sm[:, 20:22], in_=tmp[:, :].rearrange("p (a b) -> p a b", b=64), axis=mybir.AxisListType.X, op=ALU.add),
                  reads=[tmp], writes=[sm])
            fw.op("act", lambda e: e.activation(out=sm[:, 20:22], in_=sm[:, 20:22], func=AF.Exp), reads=[sm], writes=[sm])
            fw.op("dve", lambda e: e.tensor_tensor(out=sm[:, 22:23], in0=sm[:, 21:22], in1=sm[:, 20:21], op=ALU.subtract), reads=[sm], writes=[sm])
            fw.op("dve", lambda e: e.tensor_tensor(out=sm[:, 8:9], in0=sm[:, 22:23], in1=sm[:, 16:17], op=ALU.subtract), reads=[sm], writes=[sm])
            fw.op("dve", lambda e: e.tensor_scalar(out=self.subwb[:], in0=self.subwb[:], scalar1=sm[:, 17:18], scalar2=None, op0=ALU.mult),
                  reads=[self.subwb, sm], writes=[self.subwb])
            fw.barrier()

    def phase_proj(self, li):
        fw = self.fw
        TB = 2048
        NCH = TB // 512
        w_in = self.w_in
        self.need_w(w_in)
        fslabs = []
        groups = [("qa", 18), ("ka", 18), ("qb", 8), ("kb", 8), ("qc", 8), ("kc", 2), ("qd", 8), ("kd", 8)]
        for name, n in groups:
            b0 = QK_BASE[name]
            t = 0
            while t < n:
                nt = min(4, n - t)
                c0, rk, sc = HEAD_TILES[b0 + t]
                fslabs.append((c0, nt, rk, sc, "qk", b0 + t))
                t += nt
        for g in range(0, 64, 4):
            fslabs.append((IN_OFF[12] + 128 * g, 4, "G", 1.0, "g", g))
        with ExitStack() as ps:
            xTb = fw.sb([128, 16, TB], BF16, "xTb", ps)
            tabs = fw.sb([128, 4, TB], F32, "tabs", ps)
            sst = [fw.sb([128, 4, 512], F32, "sst", ps) for _ in range(3)]
            sbf = [fw.sb([128, 16, 512], BF16, "sbf", ps) for _ in range(2)]
            qb = [fw.sb([128, 512], BF16, "qb", ps) for _ in range(2)]
            t1 = [fw.sb([128, 512], F32, "t1", ps) for _ in range(2)]
            t2 = [fw.sb([128, 512], F32, "t2", ps) for _ in range(2)]
            ob = [fw.sb([128, 512], BF16, "ob", ps) for _ in range(4)]
            wv = w_in[li].rearrange("(kc p) n -> p kc n", p=128)
            cnt = 0
            pcs = 0
            dbg = getattr(self, 'pdbg', {})

            pend = []

            def flush_rope():
                while pend:
                    q_, Bk, a1, a2, o, pi, ks_, csl, di, tsl = pend.pop(0)
                    fw.op("pe", lambda e: e.matmul(Bk[:], lhsT=self.perm[:, pi, :], rhs=q_[:], start=True, stop=True),
                          reads=[self.perm, q_], writes=[Bk])
                    fw.op("dve", lambda e: e.tensor_tensor(out=a2[:], in0=Bk[:], in1=tabs[:, ks_, csl], op=ALU.mult),
                          reads=[Bk, tabs], writes=[a2])
                    fw.op("dve", lambda e: e.tensor_tensor(out=o[:], in0=a1[:], in1=a2[:], op=ALU.add), reads=[a1, a2], writes=[o])
                    fw.dma("sp", self.qkT[di, :, tsl], o[:], reads=[o], writes=[self.qkT])

            def load_slab(si, c0, w):
                nonlocal pcs
                sb_ = sbf[si % 2]
                for pz in range(4):
                    s_ = sst[pcs % 3]
                    fw.dma("sp", s_[:, :, 0:w], wv[:, pz * 4:(pz + 1) * 4, c0:c0 + w], reads=[w_in], writes=[s_])
                    dst = sb_[:, pz * 4:(pz + 1) * 4, 0:w]
                    if pcs % 2 == 0:
                        fw.op("pool", lambda e: e.tensor_copy(out=dst, in_=s_[:, :, 0:w]), reads=[s_], writes=[sb_])
                    else:
                        fw.op("act", lambda e: e.activation(out=dst, in_=s_[:, :, 0:w], func=AF.Copy), reads=[s_], writes=[sb_])
                    pcs += 1
                return sb_

            for tb in range(dbg.get('ntb', T // TB)):
                t0 = tb * TB
                for h in range(2):
                    fw.dma("sp", xTb[:, h * 8:(h + 1) * 8, :], self.xT[:, h * 8:(h + 1) * 8, t0:t0 + TB], reads=[self.xT], writes=[xTb])
                for k in range(4):
                    fw.dma("act", tabs[:, k, :], self.c_rope[k, :, t0:t0 + TB], reads=[self.c_rope], writes=[tabs])
                si = 0
                fs = [fslabs[i] for i in dbg['slabs']] if 'slabs' in dbg else fslabs
                for (c0, nt, rk, sc, dk_, d0) in fs:
                    sb_ = load_slab(si, c0, nt * 128)
                    si += 1
                    for t in range(nt):
                        for c in range(NCH):
                            A = self.banks[cnt % 3]
                            for kc in range(16):
                                fw.op("pe", lambda e, kc=kc: e.matmul(A[:], lhsT=sb_[:, kc, t * 128:(t + 1) * 128], rhs=xTb[:, kc, c * 512:(c + 1) * 512],
                                                                      start=(kc == 0), stop=(kc == 15)),
                                      reads=[sb_, xTb], writes=[A], pe_accum=(kc > 0))
                            o = ob[cnt % 4]
                            csl = slice(c * 512, (c + 1) * 512)
                            tsl = slice(t0 + c * 512, t0 + (c + 1) * 512)
                            flush_rope()
                            if rk in ("A", "B"):
                                q_ = qb[cnt % 2]
                                Bk = self.banks[3 + cnt % 2]
                                kc_, ks_, pi = (0, 1, 0) if rk == "A" else (2, 3, 1)
                                fw.op("act", lambda e: e.activation(out=q_[:], in_=A[:], func=AF.Copy, scale=float(sc)), reads=[A], writes=[q_])
                                a1, a2 = t1[cnt % 2], t2[cnt % 2]
                                fw.op("dve", lambda e: e.scalar_tensor_tensor(out=a1[:], in0=A[:], scalar=float(sc), in1=tabs[:, kc_, csl],
                                                                              op0=ALU.mult, op1=ALU.mult), reads=[A, tabs], writes=[a1])
                                pend.append((q_, Bk, a1, a2, o, pi, ks_, csl, d0 + t, tsl))
                                cnt += 1
                                continue
                            elif rk == "G":
                                g = d0 + t
                                fw.op("act", lambda e: e.activation(out=o[:], in_=A[:], func=AF.Sigmoid, bias=self.bgc[:, g:g + 1]),
                                      reads=[A, self.bgc], writes=[o])
                            else:
                                fw.op("act", lambda e: e.activation(out=o[:], in_=A[:], func=AF.Copy, scale=float(sc)), reads=[A], writes=[o])
                            if dk_ == "qk":
                                fw.dma("sp", self.qkT[d0 + t, :, tsl], o[:], reads=[o], writes=[self.qkT])
                            else:
                                fw.dma("sp", self.gT[d0 + t, :, tsl], o[:], reads=[o], writes=[self.gT])
                            cnt += 1
                flush_rope()
                for (c0, w, dc) in V_SLABS[:dbg.get('nv', 99)]:
                    sb_ = load_slab(si, c0, w)
                    si += 1
                    for tt in range(TB // 128):
                        A = self.banks[cnt % 3]
                        for kc in range(16):
                            fw.op("pe", lambda e, kc=kc: e.matmul(A[:, 0:w], lhsT=xTb[:, kc, tt * 128:(tt + 1) * 128], rhs=sb_[:, kc, 0:w],
                                                                  start=(kc == 0), stop=(kc == 15)),
                                  reads=[sb_, xTb], writes=[A], pe_accum=(kc > 0))
                        o = ob[cnt % 4]
                        if cnt % 2 == 0:
                            fw.op("act", lambda e: e.activation(out=o[:, 0:w], in_=A[:, 0:w], func=AF.Copy), reads=[A], writes=[o])
                        else:
                            fw.op("dve", lambda e: e.tensor_copy(out=o[:, 0:w], in_=A[:, 0:w]), reads=[A], writes=[o])
                        r0 = t0 + tt * 128
                        fw.dma("sp", self.v[r0:r0 + 128, dc:dc + w], o[:, 0:w], reads=[o], writes=[self.v])
                        cnt += 1
            fw.barrier()

    def att_jobs(self):
        jobs = []
        for h in range(A_HPG):
            srcs = []
            for g in range(3):
                hd = g * A_HPG + h
                srcs.append((QK_BASE["qa"] + hd, QK_BASE["ka"] + hd, V_BASE["a"] + 128 * hd, (0, 128)))

            def blocks(i):
                bl = []
                for dj, m in ((-1, 4), (0, 2), (1, 3)):
                    bl.append((0, i + dj, m, 0))
                for dj in range(-2, 3):
                    bl.append((1, i + dj, 7 if dj == -2 else (6 if dj == 2 else 5), 0))
                for dj in range(-8, 9):
                    bl.append((2, i + dj, 10 if dj == -8 else (9 if dj == 8 else 8), 0))
                return [b for b in bl if 0 <= b[1] < NT]
            jobs.append(dict(kind="A", slot=h, srcs=srcs, blocks=blocks, h=h))
        for h in range(8):
            srcs = [(QK_BASE["qb"] + h, QK_BASE["kb"] + h, V_BASE["b"] + 128 * h, (0, 64)),
                    (QK_BASE["qb"] + h, QK_BASE["kb"] + h, None, (64, 128))]

            def blocks(i):
                return [(m, j, None, m) for m in range(2) for j in range(NT)]
            jobs.append(dict(kind="B", slot=6 + h, srcs=srcs, blocks=blocks, h=h))
        for h in range(8):
            srcs = [(QK_BASE["qc"] + h, QK_BASE["kc"] + h // 4, V_BASE["c"] + 128 * (h // 4), (0, 128))]

            def blocks(i):
                bl = [(0, i - 1, 1, 0), (0, i, None, 0), (0, i + 1, 0, 0)]
                return [b for b in bl if 0 <= b[1] < NT]
            jobs.append(dict(kind="C", slot=14 + h, srcs=srcs, blocks=blocks, h=h))
        for h in range(8):
            srcs = [(QK_BASE["qd"] + h, QK_BASE["kd"] + h, V_BASE["d"] + 128 * h, (0, 128))]

            def blocks(i):
                return [(0, j, ("D", p), 0) for (j, p) in D_BLOCKS[i]]
            jobs.append(dict(kind="D", slot=22 + h, srcs=srcs, blocks=blocks, h=h))
        return jobs

    def phase_att(self, li):
        fw = self.fw
        with ExitStack() as ps:
            NSET = 4
            qs = [fw.sb([128, T], BF16, "qs", ps) for _ in range(NSET)]
            ks = [fw.sb([128, T], BF16, "ks", ps) for _ in range(NSET)]
            vs = [fw.sb([128, NT, 129], BF16, "vs", ps) for _ in range(NSET)]
            masks = fw.sb([128, 11, 128], BF16, "masks", ps)
            ebst = fw.sb([128, NPAT, 128], F32, "ebst", ps)
            eb = [fw.sb([128, NPAT, 128], BF16, "eb", ps) for _ in range(2)]
            Pb = [fw.sb([128, 512], BF16, "Pb", ps) for _ in range(6)]
            rec = [fw.sb([128, 8], F32, "rec", ps) for _ in range(4)]
            u1 = [fw.sb([128, 128], F32, "u1", ps) for _ in range(2)]
            of = [fw.sb([128, 128], F32, "of", ps) for _ in range(2)]
            junk = fw.sb([128, 128], F32, "junk", ps)
            ot = [fw.sb([128, 128], BF16, "ot", ps) for _ in range(3)]
            otT = [fw.sb([128, 128], BF16, "otT", ps) for _ in range(3)]
            fw.dma("sp", masks[:], self.c_mask[:], reads=[self.c_mask], writes=[masks])
            for s in range(NSET):
                fw.op("pool", lambda e, s=s: e.memset(vs[s][:, :, 128:129], 1.0), writes=[vs[s]])
            setrr = 0
            sbank = 0
            prr = 0
            cnt = 0
            sm = self.small
            for job in self.att_jobs():
                kind = job["kind"]
                S = []
                for (qi, ki, vc, pr) in job["srcs"]:
                    if vc is None:
                        S.append((S[-1][0], pr))
                        continue
                    s = setrr % NSET
                    setrr += 1
                    fw.dma("sp", qs[s][:], self.qkT[qi], reads=[self.qkT], writes=[qs[s]])
                    fw.dma("act", ks[s][:], self.qkT[ki], reads=[self.qkT], writes=[ks[s]])
                    fw.dma("sp", vs[s][:, :, 0:128], self.v[:, vc:vc + 128].rearrange("(n p) c -> p n c", p=128), reads=[self.v], writes=[vs[s]])
                    S.append((s, pr))
                ebj = None
                if kind == "D":
                    ebj = eb[job["h"] % 2]
                    fw.dma("act", ebst[:], self.dbias[li, job["h"]], reads=[self.dbias], writes=[ebst])
                    fw.op("act", lambda e: e.activation(out=ebj[:], in_=ebst[:], func=AF.Exp), reads=[ebst], writes=[ebj])
                nacc = 2 if kind == "B" else 1
                tasks = []
                for i in range(NT):
                    blocks = job["blocks"](i)
                    seen = [0] * nacc
                    tot = [sum(1 for b in blocks if b[3] == a) for a in range(nacc)]
                    ngr = (len(blocks) + 3) // 4
                    for gi in range(ngr):
                        grp = []
                        for (si, j, m, a) in blocks[gi * 4:(gi + 1) * 4]:
                            first = seen[a] == 0
                            seen[a] += 1
                            grp.append((si, j, m, a, first, seen[a] == tot[a]))
                        tasks.append(dict(i=i, grp=grp, last=(gi == ngr - 1)))

                def emit_qk(tk):
                    nonlocal sbank, prr
                    tk["Sb"] = self.banks[sbank % 3]
                    sbank += 1
                    tk["P"] = Pb[prr % 6]
                    prr += 1
                    i = tk["i"]
                    for bi, (si, j, m, a, first, last) in enumerate(tk["grp"]):
                        s, (p0, p1) = S[si]
                        fw.op("pe", lambda e: e.matmul(tk["Sb"][:, bi * 128:(bi + 1) * 128], lhsT=ks[s][p0:p1, j * 128:(j + 1) * 128],
                                                       rhs=qs[s][p0:p1, i * 128:(i + 1) * 128], start=True, stop=True),
                              reads=[ks[s], qs[s]], writes=[tk["Sb"]])

                pend = []

                def flush_T():
                    nonlocal sbank
                    o_, i_ = pend.pop(0)
                    tb_ = self.banks[3]
                    tbb = tb_[:].bitcast(BF16)
                    oT_ = otT[i_ % 3]
                    fw.op("pe", lambda e: e.transpose(out=tbb[:, 0:128], in_=o_[:], identity=self.ident_b[:]), reads=[o_, self.ident_b], writes=[tb_])
                    fw.op("dve", lambda e: e.tensor_copy(out=oT_[:], in_=tbb[:, 0:128]), reads=[tb_], writes=[oT_])
                    fw.dma("sp", self.oT[job["slot"], :, i_ * 128:(i_ + 1) * 128], oT_[:], reads=[oT_], writes=[self.oT])

                for tk0 in tasks[:2]:
                    emit_qk(tk0)
                for ti, tk in enumerate(tasks):
                    if ti + 2 < len(tasks):
                        emit_qk(tasks[ti + 2])
                    i = tk["i"]
                    grp = tk["grp"]
                    Sb, P = tk["Sb"], tk["P"]
                    accb = [self.banks[4 + 2 * (i % 2) + a] for a in range(nacc)]
                    n = len(grp) * 128
                    fw.op("act", lambda e: e.activation(out=P[:, 0:n], in_=Sb[:, 0:n], func=AF.Exp), reads=[Sb], writes=[P])
                    for bi, (si, j, m, a, first, last) in enumerate(grp):
                        if m is None:
                            continue
                        mk = ebj[:, m[1], :] if isinstance(m, tuple) else masks[:, m, :]
                        mb = ebj if isinstance(m, tuple) else masks
                        eng = "pool" if (cnt % 2 == 0) else "dve"
                        cnt += 1
                        fw.op(eng, lambda e: e.tensor_tensor(out=P[:, bi * 128:(bi + 1) * 128], in0=P[:, bi * 128:(bi + 1) * 128], in1=mk, op=ALU.mult),
                              reads=[P, mb], writes=[P])
                    for bi, (si, j, m, a, first, last) in enumerate(grp):
                        s, _ = S[si]
                        fw.op("pe", lambda e: e.matmul(accb[a][:, 0:129], lhsT=P[:, bi * 128:(bi + 1) * 128], rhs=vs[s][:, j, :], start=first, stop=last),
                              reads=[P, vs[s]], writes=[accb[a]], pe_accum=(not first))
                    if pend:
                        flush_T()
                    if not tk["last"]:
                        continue
                    r = rec[i % 4]
                    o = ot[i % 3]
                    if kind in ("A", "D"):
                        fw.op("dve", lambda e: e.reciprocal(out=r[:, 0:1], in_=accb[0][:, 128:129]), reads=[accb[0]], writes=[r])
                        fw.op("act", lambda e: e.activation(out=o[:], in_=accb[0][:, 0:128], func=AF.Copy, scale=r[:, 0:1]), reads=[accb[0], r], writes=[o])
                    elif kind == "C":
                        h = job["h"]
                        fw.op("dve", lambda e: e.tensor_tensor(out=r[:, 1:2], in0=accb[0][:, 128:129], in1=sm[:, h:h + 1], op=ALU.add), reads=[accb[0], sm], writes=[r])
                        fw.op("dve", lambda e: e.reciprocal(out=r[:, 0:1], in_=r[:, 1:2]), reads=[r], writes=[r])
                        fw.op("act", lambda e: e.activation(out=o[:], in_=accb[0][:, 0:128], func=AF.Copy, scale=r[:, 0:1]), reads=[accb[0], r], writes=[o])
                    else:
                        ua, ofa = u1[i % 2], of[i % 2]
                        fw.op("dve", lambda e: e.reciprocal(out=r[:, 0:1], in_=accb[0][:, 128:129]), reads=[accb[0]], writes=[r])
                        fw.op("dve", lambda e: e.reciprocal(out=r[:, 1:2], in_=accb[1][:, 128:129]), reads=[accb[1]], writes=[r])
                        fw.op("dve", lambda e: e.tensor_tensor(out=r[:, 2:3], in0=r[:, 1:2], in1=sm[:, 8:9], op=ALU.mult), reads=[r, sm], writes=[r])
                        fw.op("act", lambda e: e.activation(out=ua[:], in_=accb[0][:, 0:128], func=AF.Copy, scale=r[:, 0:1]), reads=[accb[0], r], writes=[ua])
                        fw.op("dve", lambda e: e.scalar_tensor_tensor(out=ofa[:], in0=accb[1][:, 0:128], scalar=r[:, 2:3], in1=ua[:], op0=ALU.mult, op1=ALU.add),
                              reads=[accb[1], r, ua], writes=[ofa])
                        fw.op("act", lambda e: e.activation(out=junk[:], in_=ofa[:], func=AF.Square, accum_out=r[:, 3:4]), reads=[ofa], writes=[junk, r])
                        fw.op("act", lambda e: e.activation(out=r[:, 4:5], in_=r[:, 3:4], func=AF.Ln, scale=1.0 / 128.0, bias=sm[:, 9:10]), reads=[r, sm], writes=[r])
                        fw.op("act", lambda e: e.activation(out=r[:, 5:6], in_=r[:, 4:5], func=AF.Exp, scale=-0.5), reads=[r], writes=[r])
                        fw.op("dve", lambda e: e.scalar_tensor_tensor(out=o[:], in0=ofa[:], scalar=r[:, 5:6], in1=self.subwb[:], op0=ALU.mult, op1=ALU.mult),
                              reads=[ofa, r, self.subwb], writes=[o])
                    pend.append((o, i))
                while pend:
                    flush_T()
            fw.barrier()

    def phase_merge(self, li):
        fw = self.fw
        TBM = 1024
        NCH = TBM // 512
        bslots = [(0, 6), (6, 8), (14, 8), (22, 8)]
        self.need_w(self.w_branch, self.w_out)
        wbv = self.w_branch[li].rearrange("(s p) n -> p s n", p=128)
        wov = self.w_out[li].rearrange("(kc p) n -> p kc n", p=128)
        with ExitStack() as ps:
            oTb = fw.sb([128, NSLOT, TBM], BF16, "oTb", ps)
            wbs = [fw.sb([128, 8, 128], F32, "wbs", ps) for _ in range(3)]
            wbb = [fw.sb([128, 8, 128], BF16, "wbb", ps) for _ in range(3)]
            gt = [fw.sb([128, TBM], BF16, "gt", ps) for _ in range(4)]
            mix = [fw.sb([128, TBM], F32, "mix", ps) for _ in range(2)]
            tmp = [fw.sb([128, TBM], F32, "tmp", ps) for _ in range(3)]
            mo = [fw.sb([128, TBM], BF16, "mo", ps) for _ in range(3)]
            cnt = 0
            for tb in range(T // TBM):
                t0 = tb * TBM
                fw.dma("sp", oTb[:], self.oT[:, :, t0:t0 + TBM].rearrange("s p t -> p s t"), reads=[self.oT], writes=[oTb])
                for dc in range(16):
                    mx = mix[dc % 2]
                    mo_ = mo[dc % 3]
                    for b, (s0, ns) in enumerate(bslots):
                        ws, wb_ = wbs[cnt % 3], wbb[cnt % 3]
                        g_ = gt[cnt % 4]
                        tp = tmp[cnt % 3]
                        cnt += 1
                        fw.dma("act", ws[:, 0:ns, :], wbv[:, s0:s0 + ns, dc * 128:(dc + 1) * 128], reads=[self.w_branch], writes=[ws])
                        fw.op("pool", lambda e: e.tensor_copy(out=wb_[:, 0:ns, :], in_=ws[:, 0:ns, :]), reads=[ws], writes=[wb_])
                        fw.dma("sp", g_[:], self.gT[b * 16 + dc, :, t0:t0 + TBM], reads=[self.gT], writes=[g_])
                        for c in range(NCH):
                            A = self.banks[(cnt * NCH + c) % 8]
                            for k in range(ns):
                                fw.op("pe", lambda e, k=k: e.matmul(A[:], lhsT=wb_[:, k, :], rhs=oTb[:, s0 + k, c * 512:(c + 1) * 512], start=(k == 0), stop=(k == ns - 1)),
                                      reads=[wb_, oTb], writes=[A], pe_accum=(k > 0))
                            dst = mx if b == 0 else tp
                            fw.op("dve", lambda e: e.tensor_tensor(out=dst[:, c * 512:(c + 1) * 512], in0=A[:], in1=g_[:, c * 512:(c + 1) * 512], op=ALU.mult),
                                  reads=[A, g_], writes=[dst])
                        if 0 < b < 3:
                            fw.op("pool", lambda e: e.tensor_tensor(out=mx[:], in0=mx[:], in1=tp[:], op=ALU.add), reads=[mx, tp], writes=[mx])
                        elif b == 3:
                            fw.op("pool", lambda e: e.tensor_tensor(out=mo_[:], in0=mx[:], in1=tp[:], op=ALU.add), reads=[mx, tp], writes=[mo_])
                    fw.dma("sp", self.mixT[:, dc, t0:t0 + TBM], mo_[:], reads=[mo_], writes=[self.mixT])
            fw.barrier()
        with ExitStack() as ps:
            wos = [fw.sb([128, 4, 512], F32, "wos", ps) for _ in range(2)]
            wob = fw.sb([128, 16, D], BF16, "wob", ps)
            mt = [fw.sb([128, 16, 128], BF16, "mt", ps) for _ in range(2)]
            xr = [fw.sb([128, D], F32, "xr", ps) for _ in range(2)]
            y = [fw.sb([128, D], F32, "y", ps) for _ in range(2)]
            lt = fw.sb([128, D], F32, "lt", ps)
            xb = [fw.sb([128, D], BF16, "xb", ps) for _ in range(2)]
            xts = [fw.sb([128, 16, 128], BF16, "xts", ps) for _ in range(2)]
            gb = fw.sb([128, D], F32, "gb", ps)
            bb_ = fw.sb([128, D], F32, "bb", ps)
            st6 = [fw.sb([128, 24], F32, "st6", ps) for _ in range(2)]
            mv = [fw.sb([128, 2], F32, "mv", ps) for _ in range(2)]
            lsm = [fw.sb([128, 4], F32, "lsm", ps) for _ in range(2)]
            fw.dma("sp", gb[:], self.ln_gain[li, 0:1, :].partition_broadcast(128), reads=[self.ln_gain], writes=[gb])
            fw.dma("sp", bb_[:], self.ln_bias[li, 0:1, :].partition_broadcast(128), reads=[self.ln_bias], writes=[bb_])
            k = 0
            for c in range(4):
                for pz in range(4):
                    s_ = wos[k % 2]
                    k += 1
                    fw.dma("act", s_[:], wov[:, pz * 4:(pz + 1) * 4, c * 512:(c + 1) * 512], reads=[self.w_out], writes=[s_])
                    fw.op("pool", lambda e: e.tensor_copy(out=wob[:, pz * 4:(pz + 1) * 4, c * 512:(c + 1) * 512], in_=s_[:]), reads=[s_], writes=[wob])
            xres_src = self.x if li == 0 else self.xres
            rr = [0]
            cnt = 0
            for n in range(NT):
                m_, x_, y_ = mt[n % 2], xr[n % 2], y[n % 2]
                fw.dma("sp", m_[:], self.mixT[:, :, n * 128:(n + 1) * 128], reads=[self.mixT], writes=[m_])
                fw.dma("act", x_[:], xres_src[n * 128:(n + 1) * 128, :], reads=[xres_src], writes=[x_])
                for c in range(4):
                    A = self.banks[cnt % 4]
                    cnt += 1
                    for kc in range(16):
                        fw.op("pe", lambda e, kc=kc: e.matmul(A[:], lhsT=m_[:, kc, :], rhs=wob[:, kc, c * 512:(c + 1) * 512], start=(kc == 0), stop=(kc == 15)),
                              reads=[m_, wob], writes=[A], pe_accum=(kc > 0))
                    fw.op("dve", lambda e: e.scalar_tensor_tensor(out=y_[:, c * 512:(c + 1) * 512], in0=x_[:, c * 512:(c + 1) * 512], scalar=float(ALPHA),
                                                                  in1=A[:], op0=ALU.mult, op1=ALU.add), reads=[x_, A], writes=[y_])
                self.layer_norm(y_, gb, bb_, y_, lt, st6[n % 2], mv[n % 2], lsm[n % 2])
                fw.dma("sp", self.x1res[n * 128:(n + 1) * 128, :], y_[:], reads=[y_], writes=[self.x1res])
                b_ = xb[n % 2]
                fw.op("act", lambda e: e.activation(out=b_[:], in_=y_[:], func=AF.Copy), reads=[y_], writes=[b_])
                fw.dma("sp", self.x1b[n * 128:(n + 1) * 128, :], b_[:], reads=[b_], writes=[self.x1b])
                self.emit_T(b_, self.x1T, n, xts, rr)
            fw.barrier()

    def phase_ffn(self, li, last):
        fw = self.fw
        IOA = bass.IndirectOffsetOnAxis
        self.need_w(self.w_eg, self.w_eu, self.w_ed)
        with ExitStack() as outer:
            idx_all = fw.sb([128, NT * NEXP], I32, "idx_all", outer)
            gsel_all = fw.sb([128, NT * NEXP], F32, "gsel_all", outer)
            with ExitStack() as ps:
                wrs = fw.sb([128, 16, NEXP], F32, "wrs", ps)
                wrb = fw.sb([128, 16, NEXP], BF16, "wrb", ps)
                xt = [fw.sb([128, 16, 128], BF16, "fxt", ps) for _ in range(2)]
                aff_all = fw.sb([128, NT * NEXP], F32, "aff_all", ps)
                ex = [fw.sb([128, NEXP], F32, "ex", ps) for _ in range(2)]
                s4 = [fw.sb([128, 4], F32, "s4", ps) for _ in range(2)]
                affT = fw.sb([16, T], F32, "affT", ps)
                work = fw.sb([16, T], F32, "work", ps)
                maskT = fw.sb([16, T], F32, "maskT", ps)
                onesT = fw.sb([16, T], F32, "onesT", ps)
                idxT = fw.sb([16, T], F32, "idxT", ps)
                gselT = fw.sb([16, T], F32, "gselT", ps)
                m8 = fw.sb([16, 8], F32, "m8", ps)
                fw.dma("sp", wrs[:], self.w_router[li].rearrange("(kc p) e -> p kc e", p=128), reads=[self.w_router], writes=[wrs])
                fw.op("dve", lambda e: e.tensor_copy(out=wrb[:], in_=wrs[:]), reads=[wrs], writes=[wrb])
                fw.op("pool", lambda e: e.memset(onesT[:], 1.0), writes=[onesT])
                for n in range(NT):
                    x_ = xt[n % 2]
                    lg = self.banks[n % 2]
                    tb_ = self.banks[2 + (n // 4) % 2]
                    fw.dma("sp", x_[:], self.x1T[:, :, n * 128:(n + 1) * 128], reads=[self.x1T], writes=[x_])
                    for kc in range(16):
                        fw.op("pe", lambda e, kc=kc: e.matmul(lg[:, 0:NEXP], lhsT=x_[:, kc, :], rhs=wrb[:, kc, :], start=(kc == 0), stop=(kc == 15)),
                              reads=[x_, wrb], writes=[lg], pe_accum=(kc > 0))
                    s_ = s4[n % 2]
                    e_ = ex[n % 2]
                    fw.op("dve", lambda e: e.tensor_reduce(out=s_[:, 0:1], in_=lg[:, 0:NEXP], axis=mybir.AxisListType.X, op=ALU.max), reads=[lg], writes=[s_])
                    fw.op("dve", lambda e: e.tensor_scalar(out=s_[:, 1:2], in0=s_[:, 0:1], scalar1=-1.0, scalar2=None, op0=ALU.mult), reads=[s_], writes=[s_])
                    fw.op("act", lambda e: e.activation(out=e_[:], in_=lg[:, 0:NEXP], func=AF.Exp, bias=s_[:, 1:2], accum_out=s_[:, 2:3]), reads=[lg, s_], writes=[e_, s_])
                    fw.op("dve", lambda e: e.reciprocal(out=s_[:, 3:4], in_=s_[:, 2:3]), reads=[s_], writes=[s_])
                    fw.op("dve", lambda e: e.tensor_scalar(out=aff_all[:, n * NEXP:(n + 1) * NEXP], in0=e_[:], scalar1=s_[:, 3:4], scalar2=None, op0=ALU.mult),
                          reads=[e_, s_], writes=[aff_all])
                    fw.op("pe", lambda e: e.transpose(out=tb_[0:NEXP, (n % 4) * 128:(n % 4 + 1) * 128], in_=aff_all[:, n * NEXP:(n + 1) * NEXP], identity=self.ident_f[:]),
                          reads=[aff_all, self.ident_f], writes=[tb_])
                    if n % 4 == 3:
                        fw.op("act", lambda e: e.activation(out=affT[:, (n - 3) * 128:(n + 1) * 128], in_=tb_[0:NEXP, :], func=AF.Copy), reads=[tb_], writes=[affT])
                cur = affT
                for it in range(CAP // 8):
                    fw.op("dve", lambda e: e.max(out=m8[:], in_=cur[:]), reads=[cur], writes=[m8])
                    if it < CAP // 8 - 1:
                        fw.op("dve", lambda e: e.match_replace(out=work[:], in_to_replace=m8[:], in_values=cur[:], imm_value=-1.0), reads=[m8, cur], writes=[work])
                        cur = work
                fw.op("dve", lambda e: e.tensor_scalar(out=maskT[:], in0=affT[:], scalar1=m8[:, 7:8], scalar2=None, op0=ALU.is_ge), reads=[affT, m8], writes=[maskT])
                fw.op("dve", lambda e: e.tensor_tensor_scan(out=work[:], data0=onesT[:], data1=maskT[:], initial=0.0, op0=ALU.mult, op1=ALU.add),
                      reads=[onesT, maskT], writes=[work])
                fw.op("dve", lambda e: e.scalar_tensor_tensor(out=idxT[:], in0=maskT[:], scalar=-(1.0 + BIGIDX), in1=work[:], op0=ALU.mult, op1=ALU.add),
                      reads=[maskT, work], writes=[idxT])
                fw.op("dve", lambda e: e.tensor_scalar(out=idxT[:], in0=idxT[:], scalar1=BIGIDX, scalar2=None, op0=ALU.add), reads=[idxT], writes=[idxT])
                fw.op("pool", lambda e: e.tensor_tensor(out=gselT[:], in0=maskT[:], in1=affT[:], op=ALU.mult), reads=[maskT, affT], writes=[gselT])
                bi_, bg_ = self.banks[4], self.banks[5]
                for n in range(NT):
                    fw.op("pe", lambda e: e.transpose(out=bi_[:, n * NEXP:(n + 1) * NEXP], in_=idxT[:, n * 128:(n + 1) * 128], identity=self.ident_f[0:NEXP, 0:NEXP]),
                          reads=[idxT, self.ident_f], writes=[bi_])
                    fw.op("pe", lambda e: e.transpose(out=bg_[:, n * NEXP:(n + 1) * NEXP], in_=gselT[:, n * 128:(n + 1) * 128], identity=self.ident_f[0:NEXP, 0:NEXP]),
                          reads=[gselT, self.ident_f], writes=[bg_])
                fw.op("dve", lambda e: e.tensor_copy(out=idx_all[:], in_=bi_[:]), reads=[bi_], writes=[idx_all])
                fw.op("act", lambda e: e.activation(out=gsel_all[:], in_=bg_[:], func=AF.Copy), reads=[bg_], writes=[gsel_all])
                if self.debug:
                    fw.dma("sp", self.dbg_aff[:], gsel_all[:], reads=[gsel_all], writes=[self.dbg_aff])
                    fw.dma("sp", self.dbg_idx[:], idx_all[:], reads=[idx_all], writes=[self.dbg_idx])
                fw.barrier()
            with ExitStack() as ps:
                xb = [fw.sb([128, D], BF16, "sxb", ps) for _ in range(3)]
                for n in range(NT):
                    b_ = xb[n % 3]
                    fw.dma("sp", b_[:], self.x1b[n * 128:(n + 1) * 128, :], reads=[self.x1b], writes=[b_])
                    for e_i in range(NEXP):
                        c = n * NEXP + e_i
                        fw.idma(self.xin[e_i][:, :], IOA(ap=idx_all[:, c:c + 1], axis=0), b_[:, :], None, CAP - 1, reads=[b_, idx_all], writes=[self.xin[e_i]])
                fw.barrier()
            with ExitStack() as ps:
                xi = [fw.sb([128, D], BF16, "xi", ps) for _ in range(4)]
                xinT = fw.sb([128, 16, CAP], BF16, "xinT", ps)
                wgs = [fw.sb([128, 16, 128], F32, "wgs", ps) for _ in range(4)]
                wgb = [fw.sb([128, 16, 128], BF16, "wgb", ps) for _ in range(4)]
                sg = [fw.sb([128, CAP], F32, "sg", ps) for _ in range(2)]
                hT = fw.sb([128, 8, CAP], BF16, "hT", ps)
                wds = [fw.sb([128, D], F32, "wds", ps) for _ in range(2)]
                wdb = fw.sb([128, 8, D], BF16, "wdb", ps)
                yo = [fw.sb([128, D], F32, "yo", ps) for _ in range(2)]
                cnt = 0
                for e_i in range(NEXP):
                    for jt in range(4):
                        fw.dma("sp", xi[jt][:], self.xin[e_i][jt * 128:(jt + 1) * 128, :], reads=[self.xin[e_i]], writes=[xi[jt]])
                        for half in range(2):
                            bank = self.banks[cnt % 2]
                            cnt += 1
                            bb = bank[:].bitcast(BF16)
                            for k in range(8):
                                kc = half * 8 + k
                                fw.op("pe", lambda e, kc=kc, k=k: e.transpose(out=bb[:, k * 128:(k + 1) * 128], in_=xi[jt][:, kc * 128:(kc + 1) * 128], identity=self.ident_b[:]),
                                      reads=[xi[jt], self.ident_b], writes=[bank])
                            src = bb[:, :].rearrange("p (a b) -> p a b", b=128)
                            dst = xinT[:, half * 8:(half + 1) * 8, jt * 128:(jt + 1) * 128]
                            if half == 0:
                                fw.op("act", lambda e: e.activation(out=dst, in_=src, func=AF.Copy), reads=[bank], writes=[xinT])
                            else:
                                fw.op("dve", lambda e: e.tensor_copy(out=dst, in_=src), reads=[bank], writes=[xinT])
                    wgv = self.w_eg[li, e_i].rearrange("(kc p) f -> p kc f", p=128)
                    wuv = self.w_eu[li, e_i].rearrange("(kc p) f -> p kc f", p=128)
                    for ft in range(8):
                        gs_, gb_ = wgs[(2 * ft) % 4], wgb[(2 * ft) % 4]
                        us_, ub_ = wgs[(2 * ft + 1) % 4], wgb[(2 * ft + 1) % 4]
                        fw.dma("sp", gs_[:], wgv[:, :, ft * 128:(ft + 1) * 128], reads=[self.w_eg], writes=[gs_])
                        fw.dma("act", us_[:], wuv[:, :, ft * 128:(ft + 1) * 128], reads=[self.w_eu], writes=[us_])
                        fw.op("pool", lambda e: e.tensor_copy(out=gb_[:], in_=gs_[:]), reads=[gs_], writes=[gb_])
                        fw.op("act", lambda e: e.activation(out=ub_[:], in_=us_[:], func=AF.Copy), reads=[us_], writes=[ub_])
                        G = self.banks[2 + (ft % 2) * 2]
                        U = self.banks[3 + (ft % 2) * 2]
                        for kc in range(16):
                            fw.op("pe", lambda e, kc=kc: e.matmul(G[:], lhsT=gb_[:, kc, :], rhs=xinT[:, kc, :], start=(kc == 0), stop=(kc == 15)),
                                  reads=[gb_, xinT], writes=[G], pe_accum=(kc > 0))
                        for kc in range(16):
                            fw.op("pe", lambda e, kc=kc: e.matmul(U[:], lhsT=ub_[:, kc, :], rhs=xinT[:, kc, :], start=(kc == 0), stop=(kc == 15)),
                                  reads=[ub_, xinT], writes=[U], pe_accum=(kc > 0))
                        s_ = sg[ft % 2]
                        fw.op("act", lambda e: e.activation(out=s_[:], in_=G[:], func=AF.Silu), reads=[G], writes=[s_])
                        fw.op("dve", lambda e: e.tensor_tensor(out=hT[:, ft, :], in0=s_[:], in1=U[:], op=ALU.mult), reads=[s_, U], writes=[hT])
                    for fc in range(8):
                        d_ = wds[fc % 2]
                        fw.dma("act", d_[:], self.w_ed[li, e_i, fc * 128:(fc + 1) * 128, :], reads=[self.w_ed], writes=[d_])
                        if fc % 2 == 0:
                            fw.op("pool", lambda e: e.tensor_copy(out=wdb[:, fc, :], in_=d_[:]), reads=[d_], writes=[wdb])
                        else:
                            fw.op("dve", lambda e: e.tensor_copy(out=wdb[:, fc, :], in_=d_[:]), reads=[d_], writes=[wdb])
                    for jt in range(4):
                        y_ = yo[jt % 2]
                        for dc in range(4):
                            Y = self.banks[6 + dc % 2]
                            for fc in range(8):
                                fw.op("pe", lambda e, fc=fc: e.matmul(Y[:], lhsT=hT[:, fc, jt * 128:(jt + 1) * 128], rhs=wdb[:, fc, dc * 512:(dc + 1) * 512],
                                                                      start=(fc == 0), stop=(fc == 7)), reads=[hT, wdb], writes=[Y], pe_accum=(fc > 0))
                            if dc % 2 == 0:
                                fw.op("act", lambda e: e.activation(out=y_[:, dc * 512:(dc + 1) * 512], in_=Y[:], func=AF.Copy), reads=[Y], writes=[y_])
                            else:
                                fw.op("dve", lambda e: e.tensor_copy(out=y_[:, dc * 512:(dc + 1) * 512], in_=Y[:]), reads=[Y], writes=[y_])
                        fw.dma("sp", self.yexp[e_i][jt * 128:(jt + 1) * 128, :], y_[:], reads=[y_], writes=[self.yexp[e_i]])
                fw.barrier()
            with ExitStack() as ps:
                acc = [fw.sb([128, D], F32, "acc", ps) for _ in range(2)]
                gbuf = [fw.sb([128, D], F32, "gbuf", ps) for _ in range(4)]
                lt = fw.sb([128, D], F32, "lt2", ps)
                gb = fw.sb([128, D], F32, "gb2", ps)
                bb_ = fw.sb([128, D], F32, "bb2", ps)
                xb = [fw.sb([128, D], BF16, "xb2", ps) for _ in range(2)]
                xts = [fw.sb([128, 16, 128], BF16, "xts2", ps) for _ in range(2)]
                st6 = [fw.sb([128, 24], F32, "st62", ps) for _ in range(2)]
                mv = [fw.sb([128, 2], F32, "mv2", ps) for _ in range(2)]
                lsm = [fw.sb([128, 4], F32, "lsm2", ps) for _ in range(2)]
                fw.dma("sp", gb[:], self.ln_gain[li, 1:2, :].partition_broadcast(128), reads=[self.ln_gain], writes=[gb])
                fw.dma("sp", bb_[:], self.ln_bias[li, 1:2, :].partition_broadcast(128), reads=[self.ln_bias], writes=[bb_])
                for g_ in gbuf:
                    fw.op("pool", lambda e: e.memset(g_[:], 0.0), writes=[g_])
                rr = [0]
                k = 0
                for n in range(NT):
                    a_ = acc[n % 2]
                    fw.dma("sp", a_[:], self.x1res[n * 128:(n + 1) * 128, :], reads=[self.x1res], writes=[a_])
                    fw.op("act", lambda e: e.activation(out=a_[:], in_=a_[:], func=AF.Copy, scale=float(ALPHA)), reads=[a_], writes=[a_])
                    for e_i in range(NEXP):
                        c = n * NEXP + e_i
                        g_ = gbuf[k % 4]
                        k += 1
                        fw.idma(g_[:, :], None, self.yexp[e_i][:, :], IOA(ap=idx_all[:, c:c + 1], axis=0), CAP - 1, reads=[self.yexp[e_i], idx_all], writes=[g_])
                        fw.op("dve", lambda e: e.scalar_tensor_tensor(out=a_[:], in0=g_[:], scalar=gsel_all[:, c:c + 1], in1=a_[:], op0=ALU.mult, op1=ALU.add),
                              reads=[g_, gsel_all, a_], writes=[a_])
                    self.layer_norm(a_, gb, bb_, a_, lt, st6[n % 2], mv[n % 2], lsm[n % 2])
                    if last:
                        fw.dma("sp", self.out[n * 128:(n + 1) * 128, :], a_[:], reads=[a_], writes=[self.out])
                    else:
                        fw.dma("sp", self.xres[n * 128:(n + 1) * 128, :], a_[:], reads=[a_], writes=[self.xres])
                        b_ = xb[n % 2]
                        fw.op("act", lambda e: e.activation(out=b_[:], in_=a_[:], func=AF.Copy), reads=[a_], writes=[b_])
                        self.emit_T(b_, self.xT, n, xts, rr)
                fw.barrier()


_CONSTS = None


def _consts():
    global _CONSTS
    if _CONSTS is None:
        _CONSTS = dict(c_ident=np.eye(128, dtype=np.float32), c_perm=_perms(), c_mask=_masks(), c_rope=_rope_tables())
    return _CONSTS


def _dbias(na_rpb):
    L = na_rpb.shape[0]
    out = np.empty((L, 8, 128, NPAT, 128), np.float32)
    for p, (roff, coff, valid) in enumerate(D_PATS):
        g = na_rpb[:, :, roff, coff]
        out[:, :, :, p, :] = np.where(valid[None, None], g, np.float32(NEGM))
    return out


def _lconst(l0, n):
    out = np.zeros((n, 4), np.float32)
    for i in range(n):
        li = 0.8 - 0.6 * math.exp(-0.3 * (l0 + i))
        out[i, 0] = li
        out[i, 1] = 1.0 - li
    return out


def _layer_inputs(inp, l0, n):
    sl = slice(l0, l0 + n)
    f = lambda a: np.ascontiguousarray(np.asarray(a, dtype=np.float32))
    d = dict(w_in=f(inp["w_in"][sl]), b_gate=f(inp["b_gate"][sl]), w_branch=f(inp["w_branch"][sl]), w_out=f(inp["w_out"][sl]),
             diff_lambda=f(inp["diff_lambda"][sl]).reshape(n, 256), diff_subln=f(inp["diff_subln"][sl]), sink_logit=f(inp["sink_logit"][sl]),
             dbias=_dbias(np.asarray(inp["na_rpb"][sl], dtype=np.float32)), w_router=f(inp["w_router"][sl]),
             w_exp_gate=f(inp["w_exp_gate"][sl]), w_exp_up=f(inp["w_exp_up"][sl]), w_exp_down=f(inp["w_exp_down"][sl]),
             ln_gain=f(inp["ln_gain"][sl]), ln_bias=f(inp["ln_bias"][sl]), lconst=_lconst(l0, n))
    d.update(_consts())
    return d


NLAYERS_PER_LAUNCH = 4
NCORES = 8
GATHERED = ("w_in", "w_branch", "w_out", "w_exp_gate", "w_exp_up", "w_exp_down")
_PROGS = {}


def _get_prog(nl, gather=True):
    key = (nl, gather)
    if key not in _PROGS:
        _PROGS[key] = Prog(nl, 0, False, gather).build()
    return _PROGS[key]


def _shard_rows(a, r):
    a2 = a.reshape(-1, a.shape[-1])
    n = a2.shape[0] // 8
    return np.ascontiguousarray(a2[r * n:(r + 1) * n])


def kernel(**inputs):
    x = np.ascontiguousarray(np.asarray(inputs["x"], dtype=np.float32))
    B = x.shape[0]
    cur = [x[b] for b in range(B)]
    nl = NLAYERS_PER_LAUNCH
    nc = _get_prog(nl, False)
    for l0 in range(0, DEPTH, nl):
        shared = _layer_inputs(inputs, l0, nl)
        in_maps = []
        for b in range(B):
            m = dict(shared)
            m["x"] = cur[b]
            in_maps.append(m)
        res = run_bass_kernel_spmd(nc, in_maps, core_ids=list(range(B)))
        cur = [np.asarray(res.results[b]["out"], dtype=np.float32) for b in range(B)]
        del res, in_maps, shared
    return np.stack(cur, 0)
```

```python
import math
from contextlib import ExitStack

import numpy as np
import ml_dtypes
import concourse.bass as bass
import concourse.mybir as mybir
from concourse.bass_utils import run_bass_kernel_spmd

F32 = mybir.dt.float32
BF16 = mybir.dt.bfloat16
I32 = mybir.dt.int32
AF = mybir.ActivationFunctionType
ALU = mybir.AluOpType

D = 2048
T = 4096
NT = T // 128
DEPTH = 4
A_HPG = 6
IN_WIDTHS = (2304, 2304, 2304, 1024, 1024, 1024, 1024, 256, 256, 1024, 1024, 1024, 8192)
IN_OFF = [0]
for _w in IN_WIDTHS:
    IN_OFF.append(IN_OFF[-1] + _w)
IN_TOTAL = IN_OFF[-1]
BR_W = (768, 1024, 1024, 1024)
BR_OFF = (0, 768, 1792, 2816)
NSLOT = 30
NEXP = 16
CAP = 512
DEXP = 1024
ALPHA = (2.0 * DEPTH) ** 0.25
LN_EPS = 1e-5
BIGIDX = 1.0e6
NEGM = -30000.0


class Buf:
    __slots__ = ("t", "lw", "rd", "name", "kind")

    def __init__(self, t, name="", kind="sb"):
        self.t = t
        self.lw = None
        self.rd = {}
        self.name = name
        self.kind = kind

    def __getitem__(self, idx):
        return self.t[idx]


class _View:
    def __init__(self, base, ap):
        self._b = base
        self.t = ap

    def __getitem__(self, idx):
        return self.t[idx]

    kind = property(lambda s: s._b.kind)
    lw = property(lambda s: s._b.lw, lambda s, v: setattr(s._b, "lw", v))
    rd = property(lambda s: s._b.rd, lambda s, v: setattr(s._b, "rd", v))


class FW:
    NDMASEM = 12

    def __init__(self, nc, stack):
        self.nc = nc
        self.stack = stack
        self.eng = {"pe": nc.tensor, "act": nc.scalar, "dve": nc.vector, "pool": nc.gpsimd, "sp": nc.sync}
        self.sem = {}
        self.cnt = {}
        for e in self.eng:
            self.sem[e] = stack.enter_context(nc.semaphore("s_" + e))
            self.cnt[e] = 0
        self.dsem = {}
        self.dcnt = {}
        self.dnext = {}
        self.nds = {"sp": 24, "act": self.NDMASEM, "pool": self.NDMASEM}
        for q in ("sp", "act", "pool"):
            self.dsem[q] = [stack.enter_context(nc.semaphore(f"d_{q}{i}")) for i in range(self.nds[q])]
            self.dcnt[q] = [0] * self.nds[q]
            self.dnext[q] = 0
        self.known = {e: {} for e in self.eng}
        self.ext_in = []
        self.ninstr = 0
        self.uid = 0
        self.breg = nc.gpsimd.alloc_register("idma_bound")
        nc.gpsimd.reg_mov(self.breg, CAP - 1)

    def sb(self, shape, dt, name=None, stack=None):
        self.uid += 1
        name = (name or "t") + f"_{self.uid}"
        t = (stack or self.stack).enter_context(self.nc.sbuf_tensor(name, list(shape), dt))
        return Buf(t, name)

    def ps(self, shape, dt, name):
        t = self.stack.enter_context(self.nc.psum_tensor(name, list(shape), dt))
        return Buf(t, name, "ps")

    def dram(self, name, shape, dt, kind="Internal"):
        t = self.nc.dram_tensor(name, list(shape), dt, kind=kind)
        if kind == "ExternalInput":
            self.ext_in.append(name)
        return Buf(t.ap(), name, "dram")

    def _wait(self, e, key, count):
        kn = self.known[e]
        if kn.get(key, 0) >= count:
            return
        s = self.sem[key] if isinstance(key, str) else self.dsem[key[0]][key[1]]
        self.eng[e].wait_ge(s, count)
        kn[key] = count

    def _deps(self, e, reads, writes, pe_accum=False):
        for b in reads:
            if b.kind == "dram":
                continue
            if b.lw is not None:
                self._wait(e, b.lw[0], b.lw[1])
            if b.kind == "ps":
                for k, c in b.rd.items():
                    if k != e:
                        self._wait(e, k, c)
        for b in writes:
            if b.kind == "dram":
                continue
            if b.lw is not None:
                if not (pe_accum and b.lw[0] == "pe" and e == "pe"):
                    self._wait(e, b.lw[0], b.lw[1])
            for k, c in b.rd.items():
                self._wait(e, k, c)

    def _mark(self, key, count, reads, writes):
        for b in reads:
            if b.kind != "dram":
                b.rd[key] = count
        for b in writes:
            if b.kind != "dram":
                b.lw = (key, count)
                b.rd = {}

    def op(self, e, fn, reads=(), writes=(), pe_accum=False):
        self._deps(e, reads, writes, pe_accum)
        ins = fn(self.eng[e])
        self.cnt[e] += 1
        ins.then_inc(self.sem[e], 1)
        self._mark(e, self.cnt[e], reads, writes)
        self.ninstr += 1
        return ins

    def _dma_slot(self, q):
        i = self.dnext[q]
        self.dnext[q] = (i + 1) % self.nds[q]
        key = (q, i)
        if self.dcnt[q][i] > 0:
            self._wait(q, key, self.dcnt[q][i])
        return i, key

    def dma(self, q, out_ap, in_ap, reads=(), writes=(), **kw):
        i, key = self._dma_slot(q)
        self._deps(q, reads, writes)
        ins = self.eng[q].dma_start(out=out_ap, in_=in_ap, **kw)
        self.dcnt[q][i] += 16
        ins.then_inc(self.dsem[q][i], 16)
        self._mark(key, self.dcnt[q][i], reads, writes)
        self.ninstr += 1
        return ins

    def idma(self, out_ap, out_off, in_ap, in_off, bound, reads=(), writes=()):
        q = "pool"
        i, key = self._dma_slot(q)
        self._deps(q, reads, writes)
        ins = self.nc.gpsimd.indirect_dma_start(out=out_ap, out_offset=out_off, in_=in_ap, in_offset=in_off,
                                                bounds_check=self.breg, oob_is_err=False)
        self.dcnt[q][i] += 16
        ins.then_inc(self.dsem[q][i], 16)
        self._mark(key, self.dcnt[q][i], reads, writes)
        self.ninstr += 1
        return ins

    def barrier(self):
        for e in self.eng:
            for e2 in self.eng:
                if e2 != e and self.cnt[e2] > 0:
                    self._wait(e, e2, self.cnt[e2])
            for q in self.dsem:
                for i in range(self.nds[q]):
                    if self.dcnt[q][i] > 0:
                        self._wait(e, (q, i), self.dcnt[q][i])

    def finish(self, bufs):
        self.barrier()


def _d_patterns():
    kk = np.arange(128)[:, None]
    qq = np.arange(128)[None, :]
    pats, plist, blocks = {}, [], {}
    for i in range(NT):
        r = 2 * i + qq // 64
        c = qq % 64
        rs = np.clip(r - 4, 0, 56)
        cs = np.clip(c - 8, 0, 48)
        for j in range(NT):
            kr = 2 * j + kk // 64
            kc = kk % 64
            valid = (kr >= rs) & (kr < rs + 8) & (kc >= cs) & (kc < cs + 16)
            if not valid.any():
                continue
            roff = np.where(valid, kr - r + 7, 0)
            coff = np.where(valid, np.clip(kc - c, -15, 15) + 15, 0)
            key = (roff.tobytes(), coff.tobytes(), valid.tobytes())
            if key not in pats:
                pats[key] = len(plist)
                plist.append((roff, coff, valid))
            blocks.setdefault(i, []).append((j, pats[key]))
    return blocks, plist


D_BLOCKS, D_PATS = _d_patterns()
NPAT = len(D_PATS)


def _masks():
    kk = np.arange(128)[:, None]
    qq = np.arange(128)[None, :]
    d = kk - qq
    m4 = (d % 4) == 0
    m16 = (d % 16) == 0
    ms = [kk <= qq, kk >= qq, np.abs(d) <= 64, d <= -64, d >= 64,
          m4, m4 & (kk <= qq), m4 & (kk >= qq), m16, m16 & (kk <= qq), m16 & (kk >= qq)]
    return np.stack([m.astype(np.float32) for m in ms], axis=1).astype(ml_dtypes.bfloat16)


def _rope_tables():
    pos = np.arange(T, dtype=np.float32)
    out = []
    for half in (64, 32):
        inv = (np.float32(10000.0) ** (-np.arange(half, dtype=np.float32) / np.float32(half))).astype(np.float32)
        ang = (pos[None, :] * inv[:, None]).astype(np.float32)
        cos = np.cos(ang).astype(np.float32)
        sin = np.sin(ang).astype(np.float32)
        p = np.arange(128)
        i = p % half
        sign = np.where((p % (2 * half)) < half, -1.0, 1.0).astype(np.float32)
        out.append(cos[i])
        out.append(sin[i] * sign[:, None])
    return np.stack(out, 0).astype(np.float32)


def _perms():
    p = np.arange(128)
    pa = np.zeros((128, 128), np.float32)
    pa[(p + 64) % 128, p] = 1.0
    pb = np.zeros((128, 128), np.float32)
    pb[(p // 64) * 64 + ((p % 64) + 32) % 64, p] = 1.0
    return np.stack([pa, pb], 0).astype(ml_dtypes.bfloat16)


def _head_tiles():
    s128 = 128.0 ** -0.5
    s64 = 64.0 ** -0.5
    groups = [("qa", IN_OFF[0], 18, "A", s128), ("ka", IN_OFF[1], 18, "A", 1.0),
              ("qb", IN_OFF[3], 8, "B", s64), ("kb", IN_OFF[4], 8, "B", 1.0),
              ("qc", IN_OFF[6], 8, "A", s128), ("kc", IN_OFF[7], 2, "A", 1.0),
              ("qd", IN_OFF[9], 8, None, s128), ("kd", IN_OFF[10], 8, None, 1.0)]
    tiles = []
    base = {}
    for name, off, n, rk, sc in groups:
        base[name] = len(tiles)
        for t in range(n):
            tiles.append((off + 128 * t, rk, sc))
    return tiles, base


HEAD_TILES, QK_BASE = _head_tiles()
NQK = len(HEAD_TILES)
V_SLABS = []
_dst = 0
for _slot in (2, 5, 8, 11):
    _o, _w = IN_OFF[_slot], IN_WIDTHS[_slot]
    _c = 0
    while _c < _w:
        _ww = min(512, _w - _c)
        V_SLABS.append((_o + _c, _ww, _dst))
        _c += _ww
        _dst += _ww
V_W = _dst
V_BASE = {"a": 0, "b": 2304, "c": 3328, "d": 3584}


class Prog:
    def __init__(self, nlayers, first_layer, debug=False, gather=False):
        self.nl = nlayers
        self.first = first_layer
        self.debug = debug
        self.gather = gather

    def build(self):
        nc = bass.Bass("TRN2", target_bir_lowering=False)
        self.nc = nc
        with ExitStack() as st:
            fw = FW(nc, st)
            self.fw = fw
            self._declare(st)
            for li in range(self.nl):
                self.layer(li)
            fw.finish([self.out])
        return nc

    def _declare(self, st):
        fw, nc, L = self.fw, self.nc, self.nl
        ph = getattr(self, "phases", "0spamf")
        feed = getattr(self, "feed", ())
        want = getattr(self, "want", ())

        def ein(n, s, dt=F32, need=True):
            return fw.dram(n, s, dt, kind="ExternalInput") if need else None

        def win(n, s, need):
            if not need:
                return None
            if not self.gather:
                return fw.dram(n, s, F32, kind="ExternalInput")
            rows = 1
            for d_ in s[:-1]:
                rows *= d_
            assert rows % 8 == 0
            shard = fw.dram(n, [rows // 8, s[-1]], F32, kind="ExternalInput")
            full = fw.dram(n + "_full", [rows, s[-1]], F32, kind="Internal")
            self.gathers.append((shard, full))
            names = "abcdefg"[:len(s) - 1]
            if len(s) == 2:
                return full
            pat = "(" + " ".join(names) + ") z -> " + " ".join(names) + " z"
            kw = {names[i]: s[i] for i in range(len(s) - 2)}
            return Buf(full.t.rearrange(pat, **kw), n + "_fullv") if False else _View(full, full.t.rearrange(pat, **kw))

        self.gathers = []
        self.x = ein("x", [T, D])
        self.w_in = win("w_in", [L, D, IN_TOTAL], "p" in ph)
        self.b_gate = ein("b_gate", [L, 8192])
        self.w_branch = win("w_branch", [L, 3840, D], "m" in ph)
        self.w_out = win("w_out", [L, D, D], "m" in ph)
        self.diff_lambda = ein("diff_lambda", [L, 256])
        self.diff_subln = ein("diff_subln", [L, 128])
        self.sink_logit = ein("sink_logit", [L, 8])
        self.dbias = ein("dbias", [L, 8, 128, NPAT, 128], need="a" in ph)
        self.w_router = ein("w_router", [L, D, NEXP], need="f" in ph)
        self.w_eg = win("w_exp_gate", [L, NEXP, D, DEXP], "f" in ph)
        self.w_eu = win("w_exp_up", [L, NEXP, D, DEXP], "f" in ph)
        self.w_ed = win("w_exp_down", [L, NEXP, DEXP, D], "f" in ph)
        self.ln_gain = ein("ln_gain", [L, 2, D])
        self.ln_bias = ein("ln_bias", [L, 2, D])
        self.lconst = ein("lconst", [L, 4])
        self.c_ident = ein("c_ident", [128, 128])
        self.c_perm = ein("c_perm", [2, 128, 128], BF16)
        self.c_mask = ein("c_mask", [128, 11, 128], BF16)
        self.c_rope = ein("c_rope", [4, 128, T], need="p" in ph)
        self.out = fw.dram("out", [T, D], F32, kind="ExternalOutput")

        def scr(n, s, dt):
            k = "ExternalInput" if n in feed else ("ExternalOutput" if n in want else "Internal")
            return fw.dram(n, s, dt, kind=k)
        self.xT = scr("xT", [128, 16, T], BF16)
        self.xres = scr("xres", [T, D], F32)
        self.qkT = scr("qkT", [NQK, 128, T], BF16)
        self.v = scr("v", [T, V_W], BF16)
        self.gT = scr("gT", [64, 128, T], BF16)
        self.oT = scr("oT", [NSLOT, 128, T], BF16)
        self.mixT = scr("mixT", [128, 16, T], BF16)
        self.x1res = scr("x1res", [T, D], F32)
        self.x1b = scr("x1b", [T, D], BF16)
        self.x1T = scr("x1T", [128, 16, T], BF16)
        self.xin = [scr(f"xin{e}", [CAP, D], BF16) for e in range(NEXP)]
        self.yexp = [scr(f"yexp{e}", [CAP, D], F32) for e in range(NEXP)]
        self.dbg_aff = scr("dbg_aff", [128, NT * NEXP], F32)
        self.dbg_idx = scr("dbg_idx", [128, NT * NEXP], I32)
        self.debug = ("dbg_aff" in want)
        self.gwait = {}
        for shard, full in self.gathers:
            rows, cols = shard.t.shape
            stage = fw.dram(shard.name + "_st", [rows, cols], F32)
            step = max(1, min(rows, (4 << 20) // (cols * 4)))
            keys = []
            for r0 in range(0, rows, step):
                r1 = min(rows, r0 + step)
                fw.dma("sp", stage[r0:r1, :], shard[r0:r1, :])
                q_i = (fw.dnext["sp"] - 1) % fw.nds["sp"]
                keys.append((("sp", q_i), fw.dcnt["sp"][q_i]))
            for k_, c_ in keys:
                fw._wait("pool", k_, c_)
            i, key = fw._dma_slot("pool")
            ins = nc.gpsimd.collective_compute("AllGather", ALU.bypass, ins=[stage[:, :]], outs=[full[:, :]],
                                               replica_groups=[list(range(8))])
            fw.dcnt["pool"][i] += 16
            ins.then_inc(fw.dsem["pool"][i], 16)
            self.gwait[full.name] = (key, fw.dcnt["pool"][i])
        self.banks = [fw.ps([128, 512], F32, f"bank{i}") for i in range(8)]
        self.ident_f = fw.sb([128, 128], F32, "ident_f")
        self.ident_b = fw.sb([128, 128], BF16, "ident_b")
        self.perm = fw.sb([128, 2, 128], BF16, "perm")
        self.small = fw.sb([128, 512], F32, "small")
        self.bgc = fw.sb([128, 64], F32, "bgc")
        self.subwb = fw.sb([128, 128], F32, "subwb")
        fw.dma("sp", self.ident_f[:], self.c_ident[:], reads=[self.c_ident], writes=[self.ident_f])
        fw.op("dve", lambda e: e.tensor_copy(out=self.ident_b[:], in_=self.ident_f[:]), reads=[self.ident_f], writes=[self.ident_b])
        fw.dma("sp", self.perm[:], self.c_perm[:].rearrange("a p n -> p a n"), reads=[self.c_perm], writes=[self.perm])

    def need_w(self, *ws):
        for w in ws:
            nm = getattr(getattr(w, "_b", w), "name", None)
            if nm in self.gwait:
                key, cnt = self.gwait.pop(nm)
                for e in self.fw.eng:
                    self.fw._wait(e, key, cnt)

    def emit_T(self, src, dst_dram, n, st_pool, eng_rr):
        fw = self.fw
        xts = st_pool[n % len(st_pool)]
        for half in range(2):
            bank = self.banks[4 + (eng_rr[0] % 4)]
            eng_rr[0] += 1
            bb = bank[:].bitcast(BF16)
            for k in range(8):
                kc = half * 8 + k
                fw.op("pe", lambda e, kc=kc, k=k: e.transpose(out=bb[:, k * 128:(k + 1) * 128], in_=src[:, kc * 128:(kc + 1) * 128],
                                                              identity=self.ident_b[:]), reads=[src, self.ident_b], writes=[bank])
            eng = "act" if half == 0 else "dve"
            if eng == "act":
                fw.op("act", lambda e: e.activation(out=xts[:, half * 8:(half + 1) * 8, :], in_=bb[:, :].rearrange("p (a b) -> p a b", b=128), func=AF.Copy),
                      reads=[bank], writes=[xts])
            else:
                fw.op("dve", lambda e: e.tensor_copy(out=xts[:, half * 8:(half + 1) * 8, :], in_=bb[:, :].rearrange("p (a b) -> p a b", b=128)),
                      reads=[bank], writes=[xts])
        fw.dma("sp", dst_dram[:, :, n * 128:(n + 1) * 128], xts[:], reads=[xts], writes=[dst_dram])

    def layer_norm(self, y, gb, bb_, out, tmp, st6, mv, sm):
        fw = self.fw
        for c in range(4):
            fw.op("dve", lambda e, c=c: e.bn_stats(out=st6[:, c * 6:(c + 1) * 6], in_=y[:, c * 512:(c + 1) * 512]), reads=[y], writes=[st6])
        fw.op("dve", lambda e: e.bn_aggr(out=mv[:], in_=st6[:]), reads=[st6], writes=[mv])
        fw.op("act", lambda e: e.activation(out=sm[:, 0:1], in_=mv[:, 1:2], func=AF.Ln, bias=self.small[:, 9:10]), reads=[mv, self.small], writes=[sm])
        fw.op("act", lambda e: e.activation(out=sm[:, 1:2], in_=sm[:, 0:1], func=AF.Exp, scale=-0.5), reads=[sm], writes=[sm])
        fw.op("dve", lambda e: e.tensor_scalar(out=sm[:, 2:3], in0=mv[:, 0:1], scalar1=sm[:, 1:2], scalar2=-1.0, op0=ALU.mult, op1=ALU.mult),
              reads=[mv, sm], writes=[sm])
        fw.op("act", lambda e: e.activation(out=tmp[:], in_=y[:], func=AF.Identity, scale=sm[:, 1:2], bias=sm[:, 2:3]), reads=[y, sm], writes=[tmp])
        fw.op("dve", lambda e: e.tensor_tensor(out=tmp[:], in0=tmp[:], in1=gb[:], op=ALU.mult), reads=[tmp, gb], writes=[tmp])
        fw.op("pool", lambda e: e.tensor_tensor(out=out[:], in0=tmp[:], in1=bb_[:], op=ALU.add), reads=[tmp, bb_], writes=[out])

    def layer(self, li):
        fw = self.fw
        gl = self.first + li
        ph = getattr(self, "phases", "0spamf")
        if li == 0 and "0" in ph:
            self.phase0()
        if "s" in ph:
            self.phase_small(li)
        if "p" in ph:
            self.phase_proj(li)
        if "a" in ph:
            self.phase_att(li)
        if "m" in ph:
            self.phase_merge(li)
        if "f" in ph:
            self.phase_ffn(li, last=(li == self.nl - 1))

    def phase0(self):
        fw = self.fw
        with ExitStack() as ps:
            xt = [fw.sb([128, D], F32, "p0x", ps) for _ in range(2)]
            xb = [fw.sb([128, D], BF16, "p0b", ps) for _ in range(2)]
            xts = [fw.sb([128, 16, 128], BF16, "p0t", ps) for _ in range(2)]
            rr = [0]
            for n in range(NT):
                a, b = xt[n % 2], xb[n % 2]
                fw.dma("act", a[:], self.x[n * 128:(n + 1) * 128, :], reads=[self.x], writes=[a])
                fw.op("pool", lambda e: e.tensor_copy(out=b[:], in_=a[:]), reads=[a], writes=[b])
                self.emit_T(b, self.xT, n, xts, rr)
            fw.barrier()

    def phase_small(self, li):
        fw = self.fw
        sm = self.small
        with ExitStack() as ps:
            lam = fw.sb([128, 256], F32, "lam", ps)
            tmp = fw.sb([128, 128], F32, "lamt", ps)
            bg64 = fw.sb([64, 128], F32, "bg64", ps)
            fw.dma("sp", bg64[:], self.b_gate[li, :].rearrange("(c p) -> c p", p=128), reads=[self.b_gate], writes=[bg64])
            bank = self.banks[0]
            fw.op("pe", lambda e: e.transpose(out=bank[:, 0:64], in_=bg64[:], identity=self.ident_f[0:64, 0:64]), reads=[bg64, self.ident_f], writes=[bank])
            fw.op("dve", lambda e: e.tensor_copy(out=self.bgc[:], in_=bank[:, 0:64]), reads=[bank], writes=[self.bgc])
            fw.dma("sp", sm[:, 0:8], self.sink_logit[li:li + 1, :].partition_broadcast(128), reads=[self.sink_logit], writes=[sm])
            fw.dma("sp", sm[:, 16:20], self.lconst[li:li + 1, :].partition_broadcast(128), reads=[self.lconst], writes=[sm])
            fw.dma("sp", lam[:], self.diff_lambda[li:li + 1, :].partition_broadcast(128), reads=[self.diff_lambda], writes=[lam])
            fw.dma("sp", self.subwb[:], self.diff_subln[li:li + 1, :].partition_broadcast(128), reads=[self.diff_subln], writes=[self.subwb])
            fw.op("act", lambda e: e.activation(out=sm[:, 0:8], in_=sm[:, 0:8], func=AF.Exp), reads=[sm], writes=[sm])
            fw.op("dve", lambda e: e.memset(sm[:, 9:10], LN_EPS), writes=[sm])
            fw.op("dve", lambda e: e.tensor_tensor(out=tmp[:, 0:64], in0=lam[:, 0:64], in1=lam[:, 64:128], op=ALU.mult), reads=[lam], writes=[tmp])
            fw.op("dve", lambda e: e.tensor_tensor(out=tmp[:, 64:128], in0=lam[:, 128:192], in1=lam[:, 192:256], op=ALU.mult), reads=[lam], writes=[tmp])
            fw.op("dve", lambda e: e.tensor_reduce(out=sm[:, 20:22], in_=tmp[:, :].rearrange("p (a b) -> p a b", b=64), axis=mybir.AxisListType.X, op=ALU.add),
                  reads=[tmp], writes=[sm])
            fw.op("act", lambda e: e.activation(out=sm[:, 20:22], in_=sm[:, 20:22], func=AF.Exp), reads=[sm], writes=[sm])
            fw.op("dve", lambda e: e.tensor_tensor(out=sm[:, 22:23], in0=sm[:, 21:22], in1=sm[:, 20:21], op=ALU.subtract), reads=[sm], writes=[sm])
            fw.op("dve", lambda e: e.tensor_tensor(out=sm[:, 8:9], in0=sm[:, 22:23], in1=sm[:, 16:17], op=ALU.subtract), reads=[sm], writes=[sm])
            fw.op("dve", lambda e: e.tensor_scalar(out=self.subwb[:], in0=self.subwb[:], scalar1=sm[:, 17:18], scalar2=None, op0=ALU.mult),
                  reads=[self.subwb, sm], writes=[self.subwb])
            fw.barrier()

    def phase_proj(self, li):
        fw = self.fw
        TB = 2048
        NCH = TB // 512
        w_in = self.w_in
        self.need_w(w_in)
        fslabs = []
        groups = [("qa", 18), ("ka", 18), ("qb", 8), ("kb", 8), ("qc", 8), ("kc", 2), ("qd", 8), ("kd", 8)]
        for name, n in groups:
            b0 = QK_BASE[name]
            t = 0
            while t < n:
                nt = min(4, n - t)
                c0, rk, sc = HEAD_TILES[b0 + t]
                fslabs.append((c0, nt, rk, sc, "qk", b0 + t))
                t += nt
        for g in range(0, 64, 4):
            fslabs.append((IN_OFF[12] + 128 * g, 4, "G", 1.0, "g", g))
        with ExitStack() as ps:
            xTb = fw.sb([128, 16, TB], BF16, "xTb", ps)
            tabs = fw.sb([128, 4, TB], F32, "tabs", ps)
            sst = [fw.sb([128, 4, 512], F32, "sst", ps) for _ in range(3)]
            sbf = [fw.sb([128, 16, 512], BF16, "sbf", ps) for _ in range(2)]
            qb = [fw.sb([128, 512], BF16, "qb", ps) for _ in range(2)]
            t1 = [fw.sb([128, 512], F32, "t1", ps) for _ in range(2)]
            t2 = [fw.sb([128, 512], F32, "t2", ps) for _ in range(2)]
            ob = [fw.sb([128, 512], BF16, "ob", ps) for _ in range(4)]
            wv = w_in[li].rearrange("(kc p) n -> p kc n", p=128)
            cnt = 0
            pcs = 0
            dbg = getattr(self, 'pdbg', {})

            pend = []

            def flush_rope():
                while pend:
                    q_, Bk, a1, a2, o, pi, ks_, csl, di, tsl = pend.pop(0)
                    fw.op("pe", lambda e: e.matmul(Bk[:], lhsT=self.perm[:, pi, :], rhs=q_[:], start=True, stop=True),
                          reads=[self.perm, q_], writes=[Bk])
                    fw.op("dve", lambda e: e.tensor_tensor(out=a2[:], in0=Bk[:], in1=tabs[:, ks_, csl], op=ALU.mult),
                          reads=[Bk, tabs], writes=[a2])
                    fw.op("dve", lambda e: e.tensor_tensor(out=o[:], in0=a1[:], in1=a2[:], op=ALU.add), reads=[a1, a2], writes=[o])
                    fw.dma("sp", self.qkT[di, :, tsl], o[:], reads=[o], writes=[self.qkT])

            def load_slab(si, c0, w):
                nonlocal pcs
                sb_ = sbf[si % 2]
                for pz in range(4):
                    s_ = sst[pcs % 3]
                    fw.dma("sp", s_[:, :, 0:w], wv[:, pz * 4:(pz + 1) * 4, c0:c0 + w], reads=[w_in], writes=[s_])
                    dst = sb_[:, pz * 4:(pz + 1) * 4, 0:w]
                    if pcs % 2 == 0:
                        fw.op("pool", lambda e: e.tensor_copy(out=dst, in_=s_[:, :, 0:w]), reads=[s_], writes=[sb_])
                    else:
                        fw.op("act", lambda e: e.activation(out=dst, in_=s_[:, :, 0:w], func=AF.Copy), reads=[s_], writes=[sb_])
                    pcs += 1
                return sb_

            for tb in range(dbg.get('ntb', T // TB)):
                t0 = tb * TB
                for h in range(2):
                    fw.dma("sp", xTb[:, h * 8:(h + 1) * 8, :], self.xT[:, h * 8:(h + 1) * 8, t0:t0 + TB], reads=[self.xT], writes=[xTb])
                for k in range(4):
                    fw.dma("act", tabs[:, k, :], self.c_rope[k, :, t0:t0 + TB], reads=[self.c_rope], writes=[tabs])
                si = 0
                fs = [fslabs[i] for i in dbg['slabs']] if 'slabs' in dbg else fslabs
                for (c0, nt, rk, sc, dk_, d0) in fs:
                    sb_ = load_slab(si, c0, nt * 128)
                    si += 1
                    for t in range(nt):
                        for c in range(NCH):
                            A = self.banks[cnt % 3]
                            for kc in range(16):
                                fw.op("pe", lambda e, kc=kc: e.matmul(A[:], lhsT=sb_[:, kc, t * 128:(t + 1) * 128], rhs=xTb[:, kc, c * 512:(c + 1) * 512],
                                                                      start=(kc == 0), stop=(kc == 15)),
                                      reads=[sb_, xTb], writes=[A], pe_accum=(kc > 0))
                            o = ob[cnt % 4]
                            csl = slice(c * 512, (c + 1) * 512)
                            tsl = slice(t0 + c * 512, t0 + (c + 1) * 512)
                            flush_rope()
                            if rk in ("A", "B"):
                                q_ = qb[cnt % 2]
                                Bk = self.banks[3 + cnt % 2]
                                kc_, ks_, pi = (0, 1, 0) if rk == "A" else (2, 3, 1)
                                fw.op("act", lambda e: e.activation(out=q_[:], in_=A[:], func=AF.Copy, scale=float(sc)), reads=[A], writes=[q_])
                                a1, a2 = t1[cnt % 2], t2[cnt % 2]
                                fw.op("dve", lambda e: e.scalar_tensor_tensor(out=a1[:], in0=A[:], scalar=float(sc), in1=tabs[:, kc_, csl],
                                                                              op0=ALU.mult, op1=ALU.mult), reads=[A, tabs], writes=[a1])
                                pend.append((q_, Bk, a1, a2, o, pi, ks_, csl, d0 + t, tsl))
                                cnt += 1
                                continue
                            elif rk == "G":
                                g = d0 + t
                                fw.op("act", lambda e: e.activation(out=o[:], in_=A[:], func=AF.Sigmoid, bias=self.bgc[:, g:g + 1]),
                                      reads=[A, self.bgc], writes=[o])
                            else:
                                fw.op("act", lambda e: e.activation(out=o[:], in_=A[:], func=AF.Copy, scale=float(sc)), reads=[A], writes=[o])
                            if dk_ == "qk":
                                fw.dma("sp", self.qkT[d0 + t, :, tsl], o[:], reads=[o], writes=[self.qkT])
                            else:
                                fw.dma("sp", self.gT[d0 + t, :, tsl], o[:], reads=[o], writes=[self.gT])
                            cnt += 1
                flush_rope()
                for (c0, w, dc) in V_SLABS[:dbg.get('nv', 99)]:
                    sb_ = load_slab(si, c0, w)
                    si += 1
                    for tt in range(TB // 128):
                        A = self.banks[cnt % 3]
                        for kc in range(16):
                            fw.op("pe", lambda e, kc=kc: e.matmul(A[:, 0:w], lhsT=xTb[:, kc, tt * 128:(tt + 1) * 128], rhs=sb_[:, kc, 0:w],
                                                                  start=(kc == 0), stop=(kc == 15)),
                                  reads=[sb_, xTb], writes=[A], pe_accum=(kc > 0))
                        o = ob[cnt % 4]
                        if cnt % 2 == 0:
                            fw.op("act", lambda e: e.activation(out=o[:, 0:w], in_=A[:, 0:w], func=AF.Copy), reads=[A], writes=[o])
                        else:
                            fw.op("dve", lambda e: e.tensor_copy(out=o[:, 0:w], in_=A[:, 0:w]), reads=[A], writes=[o])
                        r0 = t0 + tt * 128
                        fw.dma("sp", self.v[r0:r0 + 128, dc:dc + w], o[:, 0:w], reads=[o], writes=[self.v])
                        cnt += 1
            fw.barrier()

    def att_jobs(self):
        jobs = []
        for h in range(A_HPG):
            srcs = []
            for g in range(3):
                hd = g * A_HPG + h
                srcs.append((QK_BASE["qa"] + hd, QK_BASE["ka"] + hd, V_BASE["a"] + 128 * hd, (0, 128)))

            def blocks(i):
                bl = []
                for dj, m in ((-1, 4), (0, 2), (1, 3)):
                    bl.append((0, i + dj, m, 0))
                for dj in range(-2, 3):
                    bl.append((1, i + dj, 7 if dj == -2 else (6 if dj == 2 else 5), 0))
                for dj in range(-8, 9):
                    bl.append((2, i + dj, 10 if dj == -8 else (9 if dj == 8 else 8), 0))
                return [b for b in bl if 0 <= b[1] < NT]
            jobs.append(dict(kind="A", slot=h, srcs=srcs, blocks=blocks, h=h))
        for h in range(8):
            srcs = [(QK_BASE["qb"] + h, QK_BASE["kb"] + h, V_BASE["b"] + 128 * h, (0, 64)),
                    (QK_BASE["qb"] + h, QK_BASE["kb"] + h, None, (64, 128))]

            def blocks(i):
                return [(m, j, None, m) for m in range(2) for j in range(NT)]
            jobs.append(dict(kind="B", slot=6 + h, srcs=srcs, blocks=blocks, h=h))
        for h in range(8):
            srcs = [(QK_BASE["qc"] + h, QK_BASE["kc"] + h // 4, V_BASE["c"] + 128 * (h // 4), (0, 128))]

            def blocks(i):
                bl = [(0, i - 1, 1, 0), (0, i, None, 0), (0, i + 1, 0, 0)]
                return [b for b in bl if 0 <= b[1] < NT]
            jobs.append(dict(kind="C", slot=14 + h, srcs=srcs, blocks=blocks, h=h))
        for h in range(8):
            srcs = [(QK_BASE["qd"] + h, QK_BASE["kd"] + h, V_BASE["d"] + 128 * h, (0, 128))]

            def blocks(i):
                return [(0, j, ("D", p), 0) for (j, p) in D_BLOCKS[i]]
            jobs.append(dict(kind="D", slot=22 + h, srcs=srcs, blocks=blocks, h=h))
        return jobs

    def phase_att(self, li):
        fw = self.fw
        with ExitStack() as ps:
            NSET = 4
            qs = [fw.sb([128, T], BF16, "qs", ps) for _ in range(NSET)]
            ks = [fw.sb([128, T], BF16, "ks", ps) for _ in range(NSET)]
            vs = [fw.sb([128, NT, 129], BF16, "vs", ps) for _ in range(NSET)]
            masks = fw.sb([128, 11, 128], BF16, "masks", ps)
            ebst = fw.sb([128, NPAT, 128], F32, "ebst", ps)
            eb = [fw.sb([128, NPAT, 128], BF16, "eb", ps) for _ in range(2)]
            Pb = [fw.sb([128, 512], BF16, "Pb", ps) for _ in range(6)]
            rec = [fw.sb([128, 8], F32, "rec", ps) for _ in range(4)]
            u1 = [fw.sb([128, 128], F32, "u1", ps) for _ in range(2)]
            of = [fw.sb([128, 128], F32, "of", ps) for _ in range(2)]
            junk = fw.sb([128, 128], F32, "junk", ps)
            ot = [fw.sb([128, 128], BF16, "ot", ps) for _ in range(3)]
            otT = [fw.sb([128, 128], BF16, "otT", ps) for _ in range(3)]
            fw.dma("sp", masks[:], self.c_mask[:], reads=[self.c_mask], writes=[masks])
            for s in range(NSET):
                fw.op("pool", lambda e, s=s: e.memset(vs[s][:, :, 128:129], 1.0), writes=[vs[s]])
            setrr = 0
            sbank = 0
            prr = 0
            cnt = 0
            sm = self.small
            for job in self.att_jobs():
                kind = job["kind"]
                S = []
                for (qi, ki, vc, pr) in job["srcs"]:
                    if vc is None:
                        S.append((S[-1][0], pr))
                        continue
                    s = setrr % NSET
                    setrr += 1
                    fw.dma("sp", qs[s][:], self.qkT[qi], reads=[self.qkT], writes=[qs[s]])
                    fw.dma("act", ks[s][:], self.qkT[ki], reads=[self.qkT], writes=[ks[s]])
                    fw.dma("sp", vs[s][:, :, 0:128], self.v[:, vc:vc + 128].rearrange("(n p) c -> p n c", p=128), reads=[self.v], writes=[vs[s]])
                    S.append((s, pr))
                ebj = None
                if kind == "D":
                    ebj = eb[job["h"] % 2]
                    fw.dma("act", ebst[:], self.dbias[li, job["h"]], reads=[self.dbias], writes=[ebst])
                    fw.op("act", lambda e: e.activation(out=ebj[:], in_=ebst[:], func=AF.Exp), reads=[ebst], writes=[ebj])
                nacc = 2 if kind == "B" else 1
                tasks = []
                for i in range(NT):
                    blocks = job["blocks"](i)
                    seen = [0] * nacc
                    tot = [sum(1 for b in blocks if b[3] == a) for a in range(nacc)]
                    ngr = (len(blocks) + 3) // 4
                    for gi in range(ngr):
                        grp = []
                        for (si, j, m, a) in blocks[gi * 4:(gi + 1) * 4]:
                            first = seen[a] == 0
                            seen[a] += 1
                            grp.append((si, j, m, a, first, seen[a] == tot[a]))
                        tasks.append(dict(i=i, grp=grp, last=(gi == ngr - 1)))

                def emit_qk(tk):
                    nonlocal sbank, prr
                    tk["Sb"] = self.banks[sbank % 3]
                    sbank += 1
                    tk["P"] = Pb[prr % 6]
                    prr += 1
                    i = tk["i"]
                    for bi, (si, j, m, a, first, last) in enumerate(tk["grp"]):
                        s, (p0, p1) = S[si]
                        fw.op("pe", lambda e: e.matmul(tk["Sb"][:, bi * 128:(bi + 1) * 128], lhsT=ks[s][p0:p1, j * 128:(j + 1) * 128],
                                                       rhs=qs[s][p0:p1, i * 128:(i + 1) * 128], start=True, stop=True),
                              reads=[ks[s], qs[s]], writes=[tk["Sb"]])

                pend = []

                def flush_T():
                    nonlocal sbank
                    o_, i_ = pend.pop(0)
                    tb_ = self.banks[3]
                    tbb = tb_[:].bitcast(BF16)
                    oT_ = otT[i_ % 3]
                    fw.op("pe", lambda e: e.transpose(out=tbb[:, 0:128], in_=o_[:], identity=self.ident_b[:]), reads=[o_, self.ident_b], writes=[tb_])
                    fw.op("dve", lambda e: e.tensor_copy(out=oT_[:], in_=tbb[:, 0:128]), reads=[tb_], writes=[oT_])
                    fw.dma("sp", self.oT[job["slot"], :, i_ * 128:(i_ + 1) * 128], oT_[:], reads=[oT_], writes=[self.oT])

                for tk0 in tasks[:2]:
                    emit_qk(tk0)
                for ti, tk in enumerate(tasks):
                    if ti + 2 < len(tasks):
                        emit_qk(tasks[ti + 2])
                    i = tk["i"]
                    grp = tk["grp"]
                    Sb, P = tk["Sb"], tk["P"]
                    accb = [self.banks[4 + (i % 4)]] if nacc == 1 else [self.banks[4 + 2 * (i % 2) + a] for a in range(nacc)]
                    n = len(grp) * 128
                    fw.op("act", lambda e: e.activation(out=P[:, 0:n], in_=Sb[:, 0:n], func=AF.Exp), reads=[Sb], writes=[P])
                    for bi, (si, j, m, a, first, last) in enumerate(grp):
                        if m is None:
                            continue
                        mk = ebj[:, m[1], :] if isinstance(m, tuple) else masks[:, m, :]
                        mb = ebj if isinstance(m, tuple) else masks
                        eng = "pool" if (cnt % 2 == 0) else "dve"
                        cnt += 1
                        fw.op(eng, lambda e: e.tensor_tensor(out=P[:, bi * 128:(bi + 1) * 128], in0=P[:, bi * 128:(bi + 1) * 128], in1=mk, op=ALU.mult),
                              reads=[P, mb], writes=[P])
                    for bi, (si, j, m, a, first, last) in enumerate(grp):
                        s, _ = S[si]
                        fw.op("pe", lambda e: e.matmul(accb[a][:, 0:129], lhsT=P[:, bi * 128:(bi + 1) * 128], rhs=vs[s][:, j, :], start=first, stop=last),
                              reads=[P, vs[s]], writes=[accb[a]], pe_accum=(not first))
                    if pend:
                        flush_T()
                    if not tk["last"]:
                        continue
                    r = rec[i % 4]
                    o = ot[i % 3]
                    if kind in ("A", "D"):
                        fw.op("dve", lambda e: e.reciprocal(out=r[:, 0:1], in_=accb[0][:, 128:129]), reads=[accb[0]], writes=[r])
                        fw.op("act", lambda e: e.activation(out=o[:], in_=accb[0][:, 0:128], func=AF.Copy, scale=r[:, 0:1]), reads=[accb[0], r], writes=[o])
                    elif kind == "C":
                        h = job["h"]
                        fw.op("dve", lambda e: e.tensor_tensor(out=r[:, 1:2], in0=accb[0][:, 128:129], in1=sm[:, h:h + 1], op=ALU.add), reads=[accb[0], sm], writes=[r])
                        fw.op("dve", lambda e: e.reciprocal(out=r[:, 0:1], in_=r[:, 1:2]), reads=[r], writes=[r])
                        fw.op("act", lambda e: e.activation(out=o[:], in_=accb[0][:, 0:128], func=AF.Copy, scale=r[:, 0:1]), reads=[accb[0], r], writes=[o])
                    else:
                        ua, ofa = u1[i % 2], of[i % 2]
                        fw.op("dve", lambda e: e.reciprocal(out=r[:, 0:1], in_=accb[0][:, 128:129]), reads=[accb[0]], writes=[r])
                        fw.op("dve", lambda e: e.reciprocal(out=r[:, 1:2], in_=accb[1][:, 128:129]), reads=[accb[1]], writes=[r])
                        fw.op("dve", lambda e: e.tensor_tensor(out=r[:, 2:3], in0=r[:, 1:2], in1=sm[:, 8:9], op=ALU.mult), reads=[r, sm], writes=[r])
                        fw.op("act", lambda e: e.activation(out=ua[:], in_=accb[0][:, 0:128], func=AF.Copy, scale=r[:, 0:1]), reads=[accb[0], r], writes=[ua])
                        fw.op("dve", lambda e: e.scalar_tensor_tensor(out=ofa[:], in0=accb[1][:, 0:128], scalar=r[:, 2:3], in1=ua[:], op0=ALU.mult, op1=ALU.add),
                              reads=[accb[1], r, ua], writes=[ofa])
                        fw.op("act", lambda e: e.activation(out=junk[:], in_=ofa[:], func=AF.Square, accum_out=r[:, 3:4]), reads=[ofa], writes=[junk, r])
                        fw.op("act", lambda e: e.activation(out=r[:, 4:5], in_=r[:, 3:4], func=AF.Ln, scale=1.0 / 128.0, bias=sm[:, 9:10]), reads=[r, sm], writes=[r])
                        fw.op("act", lambda e: e.activation(out=r[:, 5:6], in_=r[:, 4:5], func=AF.Exp, scale=-0.5), reads=[r], writes=[r])
                        fw.op("dve", lambda e: e.scalar_tensor_tensor(out=o[:], in0=ofa[:], scalar=r[:, 5:6], in1=self.subwb[:], op0=ALU.mult, op1=ALU.mult),
                              reads=[ofa, r, self.subwb], writes=[o])
                    pend.append((o, i))
                while pend:
                    flush_T()
            fw.barrier()

    def phase_merge(self, li):
        fw = self.fw
        TBM = 1024
        NCH = TBM // 512
        bslots = [(0, 6), (6, 8), (14, 8), (22, 8)]
        self.need_w(self.w_branch, self.w_out)
        wbv = self.w_branch[li].rearrange("(s p) n -> p s n", p=128)
        wov = self.w_out[li].rearrange("(kc p) n -> p kc n", p=128)
        with ExitStack() as ps:
            oTb = fw.sb([128, NSLOT, TBM], BF16, "oTb", ps)
            wbs = [fw.sb([128, 8, 128], F32, "wbs", ps) for _ in range(3)]
            wbb = [fw.sb([128, 8, 128], BF16, "wbb", ps) for _ in range(3)]
            gt = [fw.sb([128, TBM], BF16, "gt", ps) for _ in range(4)]
            mix = [fw.sb([128, TBM], F32, "mix", ps) for _ in range(2)]
            tmp = [fw.sb([128, TBM], F32, "tmp", ps) for _ in range(3)]
            mo = [fw.sb([128, TBM], BF16, "mo", ps) for _ in range(3)]
            cnt = 0
            for tb in range(T // TBM):
                t0 = tb * TBM
                fw.dma("sp", oTb[:], self.oT[:, :, t0:t0 + TBM].rearrange("s p t -> p s t"), reads=[self.oT], writes=[oTb])
                for dc in range(16):
                    mx = mix[dc % 2]
                    mo_ = mo[dc % 3]
                    for b, (s0, ns) in enumerate(bslots):
                        ws, wb_ = wbs[cnt % 3], wbb[cnt % 3]
                        g_ = gt[cnt % 4]
                        tp = tmp[cnt % 3]
                        cnt += 1
                        fw.dma("act", ws[:, 0:ns, :], wbv[:, s0:s0 + ns, dc * 128:(dc + 1) * 128], reads=[self.w_branch], writes=[ws])
                        fw.op("pool", lambda e: e.tensor_copy(out=wb_[:, 0:ns, :], in_=ws[:, 0:ns, :]), reads=[ws], writes=[wb_])
                        fw.dma("sp", g_[:], self.gT[b * 16 + dc, :, t0:t0 + TBM], reads=[self.gT], writes=[g_])
                        for c in range(NCH):
                            A = self.banks[(cnt * NCH + c) % 8]
                            for k in range(ns):
                                fw.op("pe", lambda e, k=k: e.matmul(A[:], lhsT=wb_[:, k, :], rhs=oTb[:, s0 + k, c * 512:(c + 1) * 512], start=(k == 0), stop=(k == ns - 1)),
                                      reads=[wb_, oTb], writes=[A], pe_accum=(k > 0))
                            dst = mx if b == 0 else tp
                            fw.op("dve", lambda e: e.tensor_tensor(out=dst[:, c * 512:(c + 1) * 512], in0=A[:], in1=g_[:, c * 512:(c + 1) * 512], op=ALU.mult),
                                  reads=[A, g_], writes=[dst])
                        if 0 < b < 3:
                            fw.op("pool", lambda e: e.tensor_tensor(out=mx[:], in0=mx[:], in1=tp[:], op=ALU.add), reads=[mx, tp], writes=[mx])
                        elif b == 3:
                            fw.op("pool", lambda e: e.tensor_tensor(out=mo_[:], in0=mx[:], in1=tp[:], op=ALU.add), reads=[mx, tp], writes=[mo_])
                    fw.dma("sp", self.mixT[:, dc, t0:t0 + TBM], mo_[:], reads=[mo_], writes=[self.mixT])
            fw.barrier()
        with ExitStack() as ps:
            wos = [fw.sb([128, 4, 512], F32, "wos", ps) for _ in range(2)]
            wob = fw.sb([128, 16, D], BF16, "wob", ps)
            mt = [fw.sb([128, 16, 128], BF16, "mt", ps) for _ in range(2)]
            xr = [fw.sb([128, D], F32, "xr", ps) for _ in range(2)]
            y = [fw.sb([128, D], F32, "y", ps) for _ in range(2)]
            lt = fw.sb([128, D], F32, "lt", ps)
            xb = [fw.sb([128, D], BF16, "xb", ps) for _ in range(2)]
            xts = [fw.sb([128, 16, 128], BF16, "xts", ps) for _ in range(2)]
            gb = fw.sb([128, D], F32, "gb", ps)
            bb_ = fw.sb([128, D], F32, "bb", ps)
            st6 = [fw.sb([128, 24], F32, "st6", ps) for _ in range(2)]
            mv = [fw.sb([128, 2], F32, "mv", ps) for _ in range(2)]
            lsm = [fw.sb([128, 4], F32, "lsm", ps) for _ in range(2)]
            fw.dma("sp", gb[:], self.ln_gain[li, 0:1, :].partition_broadcast(128), reads=[self.ln_gain], writes=[gb])
            fw.dma("sp", bb_[:], self.ln_bias[li, 0:1, :].partition_broadcast(128), reads=[self.ln_bias], writes=[bb_])
            k = 0
            for c in range(4):
                for pz in range(4):
                    s_ = wos[k % 2]
                    k += 1
                    fw.dma("act", s_[:], wov[:, pz * 4:(pz + 1) * 4, c * 512:(c + 1) * 512], reads=[self.w_out], writes=[s_])
                    fw.op("pool", lambda e: e.tensor_copy(out=wob[:, pz * 4:(pz + 1) * 4, c * 512:(c + 1) * 512], in_=s_[:]), reads=[s_], writes=[wob])
            xres_src = self.x if li == 0 else self.xres
            rr = [0]
            cnt = 0
            for n in range(NT):
                m_, x_, y_ = mt[n % 2], xr[n % 2], y[n % 2]
                fw.dma("sp", m_[:], self.mixT[:, :, n * 128:(n + 1) * 128], reads=[self.mixT], writes=[m_])
                fw.dma("act", x_[:], xres_src[n * 128:(n + 1) * 128, :], reads=[xres_src], writes=[x_])
                for c in range(4):
                    A = self.banks[cnt % 4]
                    cnt += 1
                    for kc in range(16):
                        fw.op("pe", lambda e, kc=kc: e.matmul(A[:], lhsT=m_[:, kc, :], rhs=wob[:, kc, c * 512:(c + 1) * 512], start=(kc == 0), stop=(kc == 15)),
                              reads=[m_, wob], writes=[A], pe_accum=(kc > 0))
                    fw.op("dve", lambda e: e.scalar_tensor_tensor(out=y_[:, c * 512:(c + 1) * 512], in0=x_[:, c * 512:(c + 1) * 512], scalar=float(ALPHA),
                                                                  in1=A[:], op0=ALU.mult, op1=ALU.add), reads=[x_, A], writes=[y_])
                self.layer_norm(y_, gb, bb_, y_, lt, st6[n % 2], mv[n % 2], lsm[n % 2])
                fw.dma("sp", self.x1res[n * 128:(n + 1) * 128, :], y_[:], reads=[y_], writes=[self.x1res])
                b_ = xb[n % 2]
                fw.op("act", lambda e: e.activation(out=b_[:], in_=y_[:], func=AF.Copy), reads=[y_], writes=[b_])
                fw.dma("sp", self.x1b[n * 128:(n + 1) * 128, :], b_[:], reads=[b_], writes=[self.x1b])
                self.emit_T(b_, self.x1T, n, xts, rr)
            fw.barrier()

    def phase_ffn(self, li, last):
        fw = self.fw
        IOA = bass.IndirectOffsetOnAxis
        self.need_w(self.w_eg, self.w_eu, self.w_ed)
        with ExitStack() as outer:
            idx_all = fw.sb([128, NT * NEXP], I32, "idx_all", outer)
            gsel_all = fw.sb([128, NT * NEXP], F32, "gsel_all", outer)
            with ExitStack() as ps:
                wrs = fw.sb([128, 16, NEXP], F32, "wrs", ps)
                wrb = fw.sb([128, 16, NEXP], BF16, "wrb", ps)
                xt = [fw.sb([128, 16, 128], BF16, "fxt", ps) for _ in range(2)]
                aff_all = fw.sb([128, NT * NEXP], F32, "aff_all", ps)
                ex = [fw.sb([128, NEXP], F32, "ex", ps) for _ in range(2)]
                s4 = [fw.sb([128, 4], F32, "s4", ps) for _ in range(2)]
                affT = fw.sb([16, T], F32, "affT", ps)
                work = fw.sb([16, T], F32, "work", ps)
                maskT = fw.sb([16, T], F32, "maskT", ps)
                onesT = fw.sb([16, T], F32, "onesT", ps)
                idxT = fw.sb([16, T], F32, "idxT", ps)
                gselT = fw.sb([16, T], F32, "gselT", ps)
                m8 = fw.sb([16, 8], F32, "m8", ps)
                fw.dma("sp", wrs[:], self.w_router[li].rearrange("(kc p) e -> p kc e", p=128), reads=[self.w_router], writes=[wrs])
                fw.op("dve", lambda e: e.tensor_copy(out=wrb[:], in_=wrs[:]), reads=[wrs], writes=[wrb])
                fw.op("pool", lambda e: e.memset(onesT[:], 1.0), writes=[onesT])
                for n in range(NT):
                    x_ = xt[n % 2]
                    lg = self.banks[n % 2]
                    tb_ = self.banks[2 + (n // 4) % 2]
                    fw.dma("sp", x_[:], self.x1T[:, :, n * 128:(n + 1) * 128], reads=[self.x1T], writes=[x_])
                    for kc in range(16):
                        fw.op("pe", lambda e, kc=kc: e.matmul(lg[:, 0:NEXP], lhsT=x_[:, kc, :], rhs=wrb[:, kc, :], start=(kc == 0), stop=(kc == 15)),
                              reads=[x_, wrb], writes=[lg], pe_accum=(kc > 0))
                    s_ = s4[n % 2]
                    e_ = ex[n % 2]
                    fw.op("dve", lambda e: e.tensor_reduce(out=s_[:, 0:1], in_=lg[:, 0:NEXP], axis=mybir.AxisListType.X, op=ALU.max), reads=[lg], writes=[s_])
                    fw.op("dve", lambda e: e.tensor_scalar(out=s_[:, 1:2], in0=s_[:, 0:1], scalar1=-1.0, scalar2=None, op0=ALU.mult), reads=[s_], writes=[s_])
                    fw.op("act", lambda e: e.activation(out=e_[:], in_=lg[:, 0:NEXP], func=AF.Exp, bias=s_[:, 1:2], accum_out=s_[:, 2:3]), reads=[lg, s_], writes=[e_, s_])
                    fw.op("dve", lambda e: e.reciprocal(out=s_[:, 3:4], in_=s_[:, 2:3]), reads=[s_], writes=[s_])
                    fw.op("dve", lambda e: e.tensor_scalar(out=aff_all[:, n * NEXP:(n + 1) * NEXP], in0=e_[:], scalar1=s_[:, 3:4], scalar2=None, op0=ALU.mult),
                          reads=[e_, s_], writes=[aff_all])
                    fw.op("pe", lambda e: e.transpose(out=tb_[0:NEXP, (n % 4) * 128:(n % 4 + 1) * 128], in_=aff_all[:, n * NEXP:(n + 1) * NEXP], identity=self.ident_f[:]),
                          reads=[aff_all, self.ident_f], writes=[tb_])
                    if n % 4 == 3:
                        fw.op("act", lambda e: e.activation(out=affT[:, (n - 3) * 128:(n + 1) * 128], in_=tb_[0:NEXP, :], func=AF.Copy), reads=[tb_], writes=[affT])
                cur = affT
                for it in range(CAP // 8):
                    fw.op("dve", lambda e: e.max(out=m8[:], in_=cur[:]), reads=[cur], writes=[m8])
                    if it < CAP // 8 - 1:
                        fw.op("dve", lambda e: e.match_replace(out=work[:], in_to_replace=m8[:], in_values=cur[:], imm_value=-1.0), reads=[m8, cur], writes=[work])
                        cur = work
                fw.op("dve", lambda e: e.tensor_scalar(out=maskT[:], in0=affT[:], scalar1=m8[:, 7:8], scalar2=None, op0=ALU.is_ge), reads=[affT, m8], writes=[maskT])
                fw.op("dve", lambda e: e.tensor_tensor_scan(out=work[:], data0=onesT[:], data1=maskT[:], initial=0.0, op0=ALU.mult, op1=ALU.add),
                      reads=[onesT, maskT], writes=[work])
                fw.op("dve", lambda e: e.scalar_tensor_tensor(out=idxT[:], in0=maskT[:], scalar=-(1.0 + BIGIDX), in1=work[:], op0=ALU.mult, op1=ALU.add),
                      reads=[maskT, work], writes=[idxT])
                fw.op("dve", lambda e: e.tensor_scalar(out=idxT[:], in0=idxT[:], scalar1=BIGIDX, scalar2=None, op0=ALU.add), reads=[idxT], writes=[idxT])
                fw.op("pool", lambda e: e.tensor_tensor(out=gselT[:], in0=maskT[:], in1=affT[:], op=ALU.mult), reads=[maskT, affT], writes=[gselT])
                bi_, bg_ = self.banks[4], self.banks[5]
                for n in range(NT):
                    fw.op("pe", lambda e: e.transpose(out=bi_[:, n * NEXP:(n + 1) * NEXP], in_=idxT[:, n * 128:(n + 1) * 128], identity=self.ident_f[0:NEXP, 0:NEXP]),
                          reads=[idxT, self.ident_f], writes=[bi_])
                    fw.op("pe", lambda e: e.transpose(out=bg_[:, n * NEXP:(n + 1) * NEXP], in_=gselT[:, n * 128:(n + 1) * 128], identity=self.ident_f[0:NEXP, 0:NEXP]),
                          reads=[gselT, self.ident_f], writes=[bg_])
                fw.op("dve", lambda e: e.tensor_copy(out=idx_all[:], in_=bi_[:]), reads=[bi_], writes=[idx_all])
                fw.op("act", lambda e: e.activation(out=gsel_all[:], in_=bg_[:], func=AF.Copy), reads=[bg_], writes=[gsel_all])
                if self.debug:
                    fw.dma("sp", self.dbg_aff[:], gsel_all[:], reads=[gsel_all], writes=[self.dbg_aff])
                    fw.dma("sp", self.dbg_idx[:], idx_all[:], reads=[idx_all], writes=[self.dbg_idx])
                fw.barrier()
            with ExitStack() as ps:
                xb = [fw.sb([128, D], BF16, "sxb", ps) for _ in range(3)]
                for n in range(NT):
                    b_ = xb[n % 3]
                    fw.dma("sp", b_[:], self.x1b[n * 128:(n + 1) * 128, :], reads=[self.x1b], writes=[b_])
                    for e_i in range(NEXP):
                        c = n * NEXP + e_i
                        fw.idma(self.xin[e_i][:, :], IOA(ap=idx_all[:, c:c + 1], axis=0), b_[:, :], None, CAP - 1, reads=[b_, idx_all], writes=[self.xin[e_i]])
                fw.barrier()
            with ExitStack() as ps:
                xi = [fw.sb([128, D], BF16, "xi", ps) for _ in range(4)]
                xinT = fw.sb([128, 16, CAP], BF16, "xinT", ps)
                wgs = [fw.sb([128, 16, 128], F32, "wgs", ps) for _ in range(4)]
                wgb = [fw.sb([128, 16, 128], BF16, "wgb", ps) for _ in range(4)]
                sg = [fw.sb([128, CAP], F32, "sg", ps) for _ in range(2)]
                hT = fw.sb([128, 8, CAP], BF16, "hT", ps)
                wds = [fw.sb([128, D], F32, "wds", ps) for _ in range(2)]
                wdb = fw.sb([128, 8, D], BF16, "wdb", ps)
                yo = [fw.sb([128, D], F32, "yo", ps) for _ in range(2)]
                cnt = 0
                for e_i in range(NEXP):
                    for jt in range(4):
                        fw.dma("sp", xi[jt][:], self.xin[e_i][jt * 128:(jt + 1) * 128, :], reads=[self.xin[e_i]], writes=[xi[jt]])
                        for half in range(2):
                            bank = self.banks[cnt % 2]
                            cnt += 1
                            bb = bank[:].bitcast(BF16)
                            for k in range(8):
                                kc = half * 8 + k
                                fw.op("pe", lambda e, kc=kc, k=k: e.transpose(out=bb[:, k * 128:(k + 1) * 128], in_=xi[jt][:, kc * 128:(kc + 1) * 128], identity=self.ident_b[:]),
                                      reads=[xi[jt], self.ident_b], writes=[bank])
                            src = bb[:, :].rearrange("p (a b) -> p a b", b=128)
                            dst = xinT[:, half * 8:(half + 1) * 8, jt * 128:(jt + 1) * 128]
                            if half == 0:
                                fw.op("act", lambda e: e.activation(out=dst, in_=src, func=AF.Copy), reads=[bank], writes=[xinT])
                            else:
                                fw.op("dve", lambda e: e.tensor_copy(out=dst, in_=src), reads=[bank], writes=[xinT])
                    wgv = self.w_eg[li, e_i].rearrange("(kc p) f -> p kc f", p=128)
                    wuv = self.w_eu[li, e_i].rearrange("(kc p) f -> p kc f", p=128)
                    for ft in range(8):
                        gs_, gb_ = wgs[(2 * ft) % 4], wgb[(2 * ft) % 4]
                        us_, ub_ = wgs[(2 * ft + 1) % 4], wgb[(2 * ft + 1) % 4]
                        fw.dma("sp", gs_[:], wgv[:, :, ft * 128:(ft + 1) * 128], reads=[self.w_eg], writes=[gs_])
                        fw.dma("act", us_[:], wuv[:, :, ft * 128:(ft + 1) * 128], reads=[self.w_eu], writes=[us_])
                        fw.op("pool", lambda e: e.tensor_copy(out=gb_[:], in_=gs_[:]), reads=[gs_], writes=[gb_])
                        fw.op("act", lambda e: e.activation(out=ub_[:], in_=us_[:], func=AF.Copy), reads=[us_], writes=[ub_])
                        G = self.banks[2 + (ft % 2) * 2]
                        U = self.banks[3 + (ft % 2) * 2]
                        for kc in range(16):
                            fw.op("pe", lambda e, kc=kc: e.matmul(G[:], lhsT=gb_[:, kc, :], rhs=xinT[:, kc, :], start=(kc == 0), stop=(kc == 15)),
                                  reads=[gb_, xinT], writes=[G], pe_accum=(kc > 0))
                        for kc in range(16):
                            fw.op("pe", lambda e, kc=kc: e.matmul(U[:], lhsT=ub_[:, kc, :], rhs=xinT[:, kc, :], start=(kc == 0), stop=(kc == 15)),
                                  reads=[ub_, xinT], writes=[U], pe_accum=(kc > 0))
                        s_ = sg[ft % 2]
                        fw.op("act", lambda e: e.activation(out=s_[:], in_=G[:], func=AF.Silu), reads=[G], writes=[s_])
                        fw.op("dve", lambda e: e.tensor_tensor(out=hT[:, ft, :], in0=s_[:], in1=U[:], op=ALU.mult), reads=[s_, U], writes=[hT])
                    for fc in range(8):
                        d_ = wds[fc % 2]
                        fw.dma("act", d_[:], self.w_ed[li, e_i, fc * 128:(fc + 1) * 128, :], reads=[self.w_ed], writes=[d_])
                        if fc % 2 == 0:
                            fw.op("pool", lambda e: e.tensor_copy(out=wdb[:, fc, :], in_=d_[:]), reads=[d_], writes=[wdb])
                        else:
                            fw.op("dve", lambda e: e.tensor_copy(out=wdb[:, fc, :], in_=d_[:]), reads=[d_], writes=[wdb])
                    for jt in range(4):
                        y_ = yo[jt % 2]
                        for dc in range(4):
                            Y = self.banks[6 + dc % 2]
                            for fc in range(8):
                                fw.op("pe", lambda e, fc=fc: e.matmul(Y[:], lhsT=hT[:, fc, jt * 128:(jt + 1) * 128], rhs=wdb[:, fc, dc * 512:(dc + 1) * 512],
                                                                      start=(fc == 0), stop=(fc == 7)), reads=[hT, wdb], writes=[Y], pe_accum=(fc > 0))
                            if dc % 2 == 0:
                                fw.op("act", lambda e: e.activation(out=y_[:, dc * 512:(dc + 1) * 512], in_=Y[:], func=AF.Copy), reads=[Y], writes=[y_])
                            else:
                                fw.op("dve", lambda e: e.tensor_copy(out=y_[:, dc * 512:(dc + 1) * 512], in_=Y[:]), reads=[Y], writes=[y_])
                        fw.dma("sp", self.yexp[e_i][jt * 128:(jt + 1) * 128, :], y_[:], reads=[y_], writes=[self.yexp[e_i]])
                fw.barrier()
            with ExitStack() as ps:
                acc = [fw.sb([128, D], F32, "acc", ps) for _ in range(2)]
                gbuf = [fw.sb([128, D], F32, "gbuf", ps) for _ in range(4)]
                lt = fw.sb([128, D], F32, "lt2", ps)
                gb = fw.sb([128, D], F32, "gb2", ps)
                bb_ = fw.sb([128, D], F32, "bb2", ps)
                xb = [fw.sb([128, D], BF16, "xb2", ps) for _ in range(2)]
                xts = [fw.sb([128, 16, 128], BF16, "xts2", ps) for _ in range(2)]
                st6 = [fw.sb([128, 24], F32, "st62", ps) for _ in range(2)]
                mv = [fw.sb([128, 2], F32, "mv2", ps) for _ in range(2)]
                lsm = [fw.sb([128, 4], F32, "lsm2", ps) for _ in range(2)]
                fw.dma("sp", gb[:], self.ln_gain[li, 1:2, :].partition_broadcast(128), reads=[self.ln_gain], writes=[gb])
                fw.dma("sp", bb_[:], self.ln_bias[li, 1:2, :].partition_broadcast(128), reads=[self.ln_bias], writes=[bb_])
                for g_ in gbuf:
                    fw.op("pool", lambda e: e.memset(g_[:], 0.0), writes=[g_])
                rr = [0]
                k = 0
                for n in range(NT):
                    a_ = acc[n % 2]
                    fw.dma("sp", a_[:], self.x1res[n * 128:(n + 1) * 128, :], reads=[self.x1res], writes=[a_])
                    fw.op("act", lambda e: e.activation(out=a_[:], in_=a_[:], func=AF.Copy, scale=float(ALPHA)), reads=[a_], writes=[a_])
                    for e_i in range(NEXP):
                        c = n * NEXP + e_i
                        g_ = gbuf[k % 4]
                        k += 1
                        fw.idma(g_[:, :], None, self.yexp[e_i][:, :], IOA(ap=idx_all[:, c:c + 1], axis=0), CAP - 1, reads=[self.yexp[e_i], idx_all], writes=[g_])
                        fw.op("dve", lambda e: e.scalar_tensor_tensor(out=a_[:], in0=g_[:], scalar=gsel_all[:, c:c + 1], in1=a_[:], op0=ALU.mult, op1=ALU.add),
                              reads=[g_, gsel_all, a_], writes=[a_])
                    self.layer_norm(a_, gb, bb_, a_, lt, st6[n % 2], mv[n % 2], lsm[n % 2])
                    if last:
                        fw.dma("sp", self.out[n * 128:(n + 1) * 128, :], a_[:], reads=[a_], writes=[self.out])
                    else:
                        fw.dma("sp", self.xres[n * 128:(n + 1) * 128, :], a_[:], reads=[a_], writes=[self.xres])
                        b_ = xb[n % 2]
                        fw.op("act", lambda e: e.activation(out=b_[:], in_=a_[:], func=AF.Copy), reads=[a_], writes=[b_])
                        self.emit_T(b_, self.xT, n, xts, rr)
                fw.barrier()


_CONSTS = None


def _consts():
    global _CONSTS
    if _CONSTS is None:
        _CONSTS = dict(c_ident=np.eye(128, dtype=np.float32), c_perm=_perms(), c_mask=_masks(), c_rope=_rope_tables())
    return _CONSTS


def _dbias(na_rpb):
    L = na_rpb.shape[0]
    out = np.empty((L, 8, 128, NPAT, 128), np.float32)
    for p, (roff, coff, valid) in enumerate(D_PATS):
        g = na_rpb[:, :, roff, coff]
        out[:, :, :, p, :] = np.where(valid[None, None], g, np.float32(NEGM))
    return out


def _lconst(l0, n):
    out = np.zeros((n, 4), np.float32)
    for i in range(n):
        li = 0.8 - 0.6 * math.exp(-0.3 * (l0 + i))
        out[i, 0] = li
        out[i, 1] = 1.0 - li
    return out


def _layer_inputs(inp, l0, n):
    sl = slice(l0, l0 + n)
    f = lambda a: np.ascontiguousarray(np.asarray(a, dtype=np.float32))
    d = dict(w_in=f(inp["w_in"][sl]), b_gate=f(inp["b_gate"][sl]), w_branch=f(inp["w_branch"][sl]), w_out=f(inp["w_out"][sl]),
             diff_lambda=f(inp["diff_lambda"][sl]).reshape(n, 256), diff_subln=f(inp["diff_subln"][sl]), sink_logit=f(inp["sink_logit"][sl]),
             dbias=_dbias(np.asarray(inp["na_rpb"][sl], dtype=np.float32)), w_router=f(inp["w_router"][sl]),
             w_exp_gate=f(inp["w_exp_gate"][sl]), w_exp_up=f(inp["w_exp_up"][sl]), w_exp_down=f(inp["w_exp_down"][sl]),
             ln_gain=f(inp["ln_gain"][sl]), ln_bias=f(inp["ln_bias"][sl]), lconst=_lconst(l0, n))
    d.update(_consts())
    return d


NLAYERS_PER_LAUNCH = 4
NCORES = 8
GATHERED = ("w_in", "w_branch", "w_out", "w_exp_gate", "w_exp_up", "w_exp_down")
_PROGS = {}


def _get_prog(nl, gather=True):
    key = (nl, gather)
    if key not in _PROGS:
        _PROGS[key] = Prog(nl, 0, False, gather).build()
    return _PROGS[key]


def _shard_rows(a, r):
    a2 = a.reshape(-1, a.shape[-1])
    n = a2.shape[0] // 8
    return np.ascontiguousarray(a2[r * n:(r + 1) * n])


def kernel(**inputs):
    x = np.ascontiguousarray(np.asarray(inputs["x"], dtype=np.float32))
    B = x.shape[0]
    cur = [x[b] for b in range(B)]
    nl = NLAYERS_PER_LAUNCH
    nc = _get_prog(nl, False)
    for l0 in range(0, DEPTH, nl):
        shared = _layer_inputs(inputs, l0, nl)
        in_maps = []
        for b in range(B):
            m = dict(shared)
            m["x"] = cur[b]
            in_maps.append(m)
        res = run_bass_kernel_spmd(nc, in_maps, core_ids=list(range(B)))
        cur = [np.asarray(res.results[b]["out"], dtype=np.float32) for b in range(B)]
        del res, in_maps, shared
    return np.stack(cur, 0)
```

```python
import math
from contextlib import ExitStack

import numpy as np
import ml_dtypes
import concourse.bass as bass
import concourse.mybir as mybir
from concourse.bass_utils import run_bass_kernel_spmd

F32 = mybir.dt.float32
BF16 = mybir.dt.bfloat16
I32 = mybir.dt.int32
AF = mybir.ActivationFunctionType
ALU = mybir.AluOpType

D = 2048
T = 4096
NT = T // 128
DEPTH = 4
A_HPG = 6
IN_WIDTHS = (2304, 2304, 2304, 1024, 1024, 1024, 1024, 256, 256, 1024, 1024, 1024, 8192)
IN_OFF = [0]
for _w in IN_WIDTHS:
    IN_OFF.append(IN_OFF[-1] + _w)
IN_TOTAL = IN_OFF[-1]
BR_W = (768, 1024, 1024, 1024)
BR_OFF = (0, 768, 1792, 2816)
NSLOT = 30
NEXP = 16
CAP = 512
DEXP = 1024
ALPHA = (2.0 * DEPTH) ** 0.25
LN_EPS = 1e-5
BIGIDX = 1.0e6
NEGM = -30000.0


class Buf:
    __slots__ = ("t", "lw", "rd", "name", "kind")

    def __init__(self, t, name="", kind="sb"):
        self.t = t
        self.lw = None
        self.rd = {}
        self.name = name
        self.kind = kind

    def __getitem__(self, idx):
        return self.t[idx]


class _View:
    def __init__(self, base, ap):
        self._b = base
        self.t = ap

    def __getitem__(self, idx):
        return self.t[idx]

    kind = property(lambda s: s._b.kind)
    lw = property(lambda s: s._b.lw, lambda s, v: setattr(s._b, "lw", v))
    rd = property(lambda s: s._b.rd, lambda s, v: setattr(s._b, "rd", v))


class FW:
    NDMASEM = 12

    def __init__(self, nc, stack):
        self.nc = nc
        self.stack = stack
        self.eng = {"pe": nc.tensor, "act": nc.scalar, "dve": nc.vector, "pool": nc.gpsimd, "sp": nc.sync}
        self.sem = {}
        self.cnt = {}
        for e in self.eng:
            self.sem[e] = stack.enter_context(nc.semaphore("s_" + e))
            self.cnt[e] = 0
        self.dsem = {}
        self.dcnt = {}
        self.dnext = {}
        self.nds = {"sp": 24, "act": self.NDMASEM, "pool": self.NDMASEM}
        for q in ("sp", "act", "pool"):
            self.dsem[q] = [stack.enter_context(nc.semaphore(f"d_{q}{i}")) for i in range(self.nds[q])]
            self.dcnt[q] = [0] * self.nds[q]
            self.dnext[q] = 0
        self.known = {e: {} for e in self.eng}
        self.ext_in = []
        self.ninstr = 0
        self.uid = 0
        self.breg = nc.gpsimd.alloc_register("idma_bound")
        nc.gpsimd.reg_mov(self.breg, CAP - 1)

    def sb(self, shape, dt, name=None, stack=None):
        self.uid += 1
        name = (name or "t") + f"_{self.uid}"
        t = (stack or self.stack).enter_context(self.nc.sbuf_tensor(name, list(shape), dt))
        return Buf(t, name)

    def ps(self, shape, dt, name):
        t = self.stack.enter_context(self.nc.psum_tensor(name, list(shape), dt))
        return Buf(t, name, "ps")

    def dram(self, name, shape, dt, kind="Internal"):
        t = self.nc.dram_tensor(name, list(shape), dt, kind=kind)
        if kind == "ExternalInput":
            self.ext_in.append(name)
        return Buf(t.ap(), name, "dram")

    def _wait(self, e, key, count):
        kn = self.known[e]
        if kn.get(key, 0) >= count:
            return
        s = self.sem[key] if isinstance(key, str) else self.dsem[key[0]][key[1]]
        self.eng[e].wait_ge(s, count)
        kn[key] = count

    def _deps(self, e, reads, writes, pe_accum=False):
        for b in reads:
            if b.kind == "dram":
                continue
            if b.lw is not None:
                self._wait(e, b.lw[0], b.lw[1])
            if b.kind == "ps":
                for k, c in b.rd.items():
                    if k != e:
                        self._wait(e, k, c)
        for b in writes:
            if b.kind == "dram":
                continue
            if b.lw is not None:
                if not (pe_accum and b.lw[0] == "pe" and e == "pe"):
                    self._wait(e, b.lw[0], b.lw[1])
            for k, c in b.rd.items():
                self._wait(e, k, c)

    def _mark(self, key, count, reads, writes):
        for b in reads:
            if b.kind != "dram":
                b.rd[key] = count
        for b in writes:
            if b.kind != "dram":
                b.lw = (key, count)
                b.rd = {}

    def op(self, e, fn, reads=(), writes=(), pe_accum=False):
        self._deps(e, reads, writes, pe_accum)
        ins = fn(self.eng[e])
        self.cnt[e] += 1
        ins.then_inc(self.sem[e], 1)
        self._mark(e, self.cnt[e], reads, writes)
        self.ninstr += 1
        return ins

    def _dma_slot(self, q):
        i = self.dnext[q]
        self.dnext[q] = (i + 1) % self.nds[q]
        key = (q, i)
        if self.dcnt[q][i] > 0:
            self._wait(q, key, self.dcnt[q][i])
        return i, key

    def dma(self, q, out_ap, in_ap, reads=(), writes=(), **kw):
        i, key = self._dma_slot(q)
        self._deps(q, reads, writes)
        ins = self.eng[q].dma_start(out=out_ap, in_=in_ap, **kw)
        self.dcnt[q][i] += 16
        ins.then_inc(self.dsem[q][i], 16)
        self._mark(key, self.dcnt[q][i], reads, writes)
        self.ninstr += 1
        return ins

    def idma(self, out_ap, out_off, in_ap, in_off, bound, reads=(), writes=()):
        q = "pool"
        i, key = self._dma_slot(q)
        self._deps(q, reads, writes)
        ins = self.nc.gpsimd.indirect_dma_start(out=out_ap, out_offset=out_off, in_=in_ap, in_offset=in_off,
                                                bounds_check=self.breg, oob_is_err=False)
        self.dcnt[q][i] += 16
        ins.then_inc(self.dsem[q][i], 16)
        self._mark(key, self.dcnt[q][i], reads, writes)
        self.ninstr += 1
        return ins

    def barrier(self):
        for e in self.eng:
            for e2 in self.eng:
                if e2 != e and self.cnt[e2] > 0:
                    self._wait(e, e2, self.cnt[e2])
            for q in self.dsem:
                for i in range(self.nds[q]):
                    if self.dcnt[q][i] > 0:
                        self._wait(e, (q, i), self.dcnt[q][i])

    def finish(self, bufs):
        self.barrier()


def _d_patterns():
    kk = np.arange(128)[:, None]
    qq = np.arange(128)[None, :]
    pats, plist, blocks = {}, [], {}
    for i in range(NT):
        r = 2 * i + qq // 64
        c = qq % 64
        rs = np.clip(r - 4, 0, 56)
        cs = np.clip(c - 8, 0, 48)
        for j in range(NT):
            kr = 2 * j + kk // 64
            kc = kk % 64
            valid = (kr >= rs) & (kr < rs + 8) & (kc >= cs) & (kc < cs + 16)
            if not valid.any():
                continue
            roff = np.where(valid, kr - r + 7, 0)
            coff = np.where(valid, np.clip(kc - c, -15, 15) + 15, 0)
            key = (roff.tobytes(), coff.tobytes(), valid.tobytes())
            if key not in pats:
                pats[key] = len(plist)
                plist.append((roff, coff, valid))
            blocks.setdefault(i, []).append((j, pats[key]))
    return blocks, plist


D_BLOCKS, D_PATS = _d_patterns()
NPAT = len(D_PATS)


def _masks():
    kk = np.arange(128)[:, None]
    qq = np.arange(128)[None, :]
    d = kk - qq
    m4 = (d % 4) == 0
    m16 = (d % 16) == 0
    ms = [kk <= qq, kk >= qq, np.abs(d) <= 64, d <= -64, d >= 64,
          m4, m4 & (kk <= qq), m4 & (kk >= qq), m16, m16 & (kk <= qq), m16 & (kk >= qq)]
    return np.stack([m.astype(np.float32) for m in ms], axis=1).astype(ml_dtypes.bfloat16)


def _rope_tables():
    pos = np.arange(T, dtype=np.float32)
    out = []
    for half in (64, 32):
        inv = (np.float32(10000.0) ** (-np.arange(half, dtype=np.float32) / np.float32(half))).astype(np.float32)
        ang = (pos[None, :] * inv[:, None]).astype(np.float32)
        cos = np.cos(ang).astype(np.float32)
        sin = np.sin(ang).astype(np.float32)
        p = np.arange(128)
        i = p % half
        sign = np.where((p % (2 * half)) < half, -1.0, 1.0).astype(np.float32)
        out.append(cos[i])
        out.append(sin[i] * sign[:, None])
    return np.stack(out, 0).astype(np.float32)


def _perms():
    p = np.arange(128)
    pa = np.zeros((128, 128), np.float32)
    pa[(p + 64) % 128, p] = 1.0
    pb = np.zeros((128, 128), np.float32)
    pb[(p // 64) * 64 + ((p % 64) + 32) % 64, p] = 1.0
    return np.stack([pa, pb], 0).astype(ml_dtypes.bfloat16)


def _head_tiles():
    s128 = 128.0 ** -0.5
    s64 = 64.0 ** -0.5
    groups = [("qa", IN_OFF[0], 18, "A", s128), ("ka", IN_OFF[1], 18, "A", 1.0),
              ("qb", IN_OFF[3], 8, "B", s64), ("kb", IN_OFF[4], 8, "B", 1.0),
              ("qc", IN_OFF[6], 8, "A", s128), ("kc", IN_OFF[7], 2, "A", 1.0),
              ("qd", IN_OFF[9], 8, None, s128), ("kd", IN_OFF[10], 8, None, 1.0)]
    tiles = []
    base = {}
    for name, off, n, rk, sc in groups:
        base[name] = len(tiles)
        for t in range(n):
            tiles.append((off + 128 * t, rk, sc))
    return tiles, base


HEAD_TILES, QK_BASE = _head_tiles()
NQK = len(HEAD_TILES)
V_SLABS = []
_dst = 0
for _slot in (2, 5, 8, 11):
    _o, _w = IN_OFF[_slot], IN_WIDTHS[_slot]
    _c = 0
    while _c < _w:
        _ww = min(512, _w - _c)
        V_SLABS.append((_o + _c, _ww, _dst))
        _c += _ww
        _dst += _ww
V_W = _dst
V_BASE = {"a": 0, "b": 2304, "c": 3328, "d": 3584}


class Prog:
    def __init__(self, nlayers, first_layer, debug=False, gather=False):
        self.nl = nlayers
        self.first = first_layer
        self.debug = debug
        self.gather = gather

    def build(self):
        nc = bass.Bass("TRN2", target_bir_lowering=False)
        self.nc = nc
        with ExitStack() as st:
            fw = FW(nc, st)
            self.fw = fw
            self._declare(st)
            for li in range(self.nl):
                self.layer(li)
            fw.finish([self.out])
        return nc

    def _declare(self, st):
        fw, nc, L = self.fw, self.nc, self.nl
        ph = getattr(self, "phases", "0spamf")
        feed = getattr(self, "feed", ())
        want = getattr(self, "want", ())

        def ein(n, s, dt=F32, need=True):
            return fw.dram(n, s, dt, kind="ExternalInput") if need else None

        def win(n, s, need):
            if not need:
                return None
            if not self.gather:
                return fw.dram(n, s, F32, kind="ExternalInput")
            rows = 1
            for d_ in s[:-1]:
                rows *= d_
            assert rows % 8 == 0
            shard = fw.dram(n, [rows // 8, s[-1]], F32, kind="ExternalInput")
            full = fw.dram(n + "_full", [rows, s[-1]], F32, kind="Internal")
            self.gathers.append((shard, full))
            names = "abcdefg"[:len(s) - 1]
            if len(s) == 2:
                return full
            pat = "(" + " ".join(names) + ") z -> " + " ".join(names) + " z"
            kw = {names[i]: s[i] for i in range(len(s) - 2)}
            return Buf(full.t.rearrange(pat, **kw), n + "_fullv") if False else _View(full, full.t.rearrange(pat, **kw))

        self.gathers = []
        self.x = ein("x", [T, D])
        self.w_in = win("w_in", [L, D, IN_TOTAL], "p" in ph)
        self.b_gate = ein("b_gate", [L, 8192])
        self.w_branch = win("w_branch", [L, 3840, D], "m" in ph)
        self.w_out = win("w_out", [L, D, D], "m" in ph)
        self.diff_lambda = ein("diff_lambda", [L, 256])
        self.diff_subln = ein("diff_subln", [L, 128])
        self.sink_logit = ein("sink_logit", [L, 8])
        self.dbias = ein("dbias", [L, 8, 128, NPAT, 128], need="a" in ph)
        self.w_router = ein("w_router", [L, D, NEXP], need="f" in ph)
        self.w_eg = win("w_exp_gate", [L, NEXP, D, DEXP], "f" in ph)
        self.w_eu = win("w_exp_up", [L, NEXP, D, DEXP], "f" in ph)
        self.w_ed = win("w_exp_down", [L, NEXP, DEXP, D], "f" in ph)
        self.ln_gain = ein("ln_gain", [L, 2, D])
        self.ln_bias = ein("ln_bias", [L, 2, D])
        self.lconst = ein("lconst", [L, 4])
        self.c_ident = ein("c_ident", [128, 128])
        self.c_perm = ein("c_perm", [2, 128, 128], BF16)
        self.c_mask = ein("c_mask", [128, 11, 128], BF16)
        self.c_rope = ein("c_rope", [4, 128, T], need="p" in ph)
        self.out = fw.dram("out", [T, D], F32, kind="ExternalOutput")

        def scr(n, s, dt):
            k = "ExternalInput" if n in feed else ("ExternalOutput" if n in want else "Internal")
            return fw.dram(n, s, dt, kind=k)
        self.xT = scr("xT", [128, 16, T], BF16)
        self.xres = scr("xres", [T, D], F32)
        self.qkT = scr("qkT", [NQK, 128, T], BF16)
        self.v = scr("v", [T, V_W], BF16)
        self.gT = scr("gT", [64, 128, T], BF16)
        self.oT = scr("oT", [NSLOT, 128, T], BF16)
        self.mixT = scr("mixT", [128, 16, T], BF16)
        self.x1res = scr("x1res", [T, D], F32)
        self.x1b = scr("x1b", [T, D], BF16)
        self.x1T = scr("x1T", [128, 16, T], BF16)
        self.xin = [scr(f"xin{e}", [CAP, D], BF16) for e in range(NEXP)]
        self.yexp = [scr(f"yexp{e}", [CAP, D], F32) for e in range(NEXP)]
        self.dbg_aff = scr("dbg_aff", [128, NT * NEXP], F32)
        self.dbg_idx = scr("dbg_idx", [128, NT * NEXP], I32)
        self.debug = ("dbg_aff" in want)
        self.gwait = {}
        for shard, full in self.gathers:
            rows, cols = shard.t.shape
            stage = fw.dram(shard.name + "_st", [rows, cols], F32)
            step = max(1, min(rows, (4 << 20) // (cols * 4)))
            keys = []
            for r0 in range(0, rows, step):
                r1 = min(rows, r0 + step)
                fw.dma("sp", stage[r0:r1, :], shard[r0:r1, :])
                q_i = (fw.dnext["sp"] - 1) % fw.nds["sp"]
                keys.append((("sp", q_i), fw.dcnt["sp"][q_i]))
            for k_, c_ in keys:
                fw._wait("pool", k_, c_)
            i, key = fw._dma_slot("pool")
            ins = nc.gpsimd.collective_compute("AllGather", ALU.bypass, ins=[stage[:, :]], outs=[full[:, :]],
                                               replica_groups=[list(range(8))])
            fw.dcnt["pool"][i] += 16
            ins.then_inc(fw.dsem["pool"][i], 16)
            self.gwait[full.name] = (key, fw.dcnt["pool"][i])
        self.banks = [fw.ps([128, 512], F32, f"bank{i}") for i in range(8)]
        self.ident_f = fw.sb([128, 128], F32, "ident_f")
        self.ident_b = fw.sb([128, 128], BF16, "ident_b")
        self.perm = fw.sb([128, 2, 128], BF16, "perm")
        self.small = fw.sb([128, 512], F32, "small")
        self.bgc = fw.sb([128, 64], F32, "bgc")
        self.subwb = fw.sb([128, 128], F32, "subwb")
        fw.dma("sp", self.ident_f[:], self.c_ident[:], reads=[self.c_ident], writes=[self.ident_f])
        fw.op("dve", lambda e: e.tensor_copy(out=self.ident_b[:], in_=self.ident_f[:]), reads=[self.ident_f], writes=[self.ident_b])
        fw.dma("sp", self.perm[:], self.c_perm[:].rearrange("a p n -> p a n"), reads=[self.c_perm], writes=[self.perm])

    def need_w(self, *ws):
        for w in ws:
            nm = getattr(getattr(w, "_b", w), "name", None)
            if nm in self.gwait:
                key, cnt = self.gwait.pop(nm)
                for e in self.fw.eng:
                    self.fw._wait(e, key, cnt)

    def emit_T(self, src, dst_dram, n, st_pool, eng_rr):
        fw = self.fw
        xts = st_pool[n % len(st_pool)]
        for half in range(2):
            bank = self.banks[4 + (eng_rr[0] % 4)]
            eng_rr[0] += 1
            bb = bank[:].bitcast(BF16)
            for k in range(8):
                kc = half * 8 + k
                fw.op("pe", lambda e, kc=kc, k=k: e.transpose(out=bb[:, k * 128:(k + 1) * 128], in_=src[:, kc * 128:(kc + 1) * 128],
                                                              identity=self.ident_b[:]), reads=[src, self.ident_b], writes=[bank])
            eng = "act" if half == 0 else "dve"
            if eng == "act":
                fw.op("act", lambda e: e.activation(out=xts[:, half * 8:(half + 1) * 8, :], in_=bb[:, :].rearrange("p (a b) -> p a b", b=128), func=AF.Copy),
                      reads=[bank], writes=[xts])
            else:
                fw.op("dve", lambda e: e.tensor_copy(out=xts[:, half * 8:(half + 1) * 8, :], in_=bb[:, :].rearrange("p (a b) -> p a b", b=128)),
                      reads=[bank], writes=[xts])
        fw.dma("sp", dst_dram[:, :, n * 128:(n + 1) * 128], xts[:], reads=[xts], writes=[dst_dram])

    def layer_norm(self, y, gb, bb_, out, tmp, st6, mv, sm):
        fw = self.fw
        for c in range(4):
            fw.op("dve", lambda e, c=c: e.bn_stats(out=st6[:, c * 6:(c + 1) * 6], in_=y[:, c * 512:(c + 1) * 512]), reads=[y], writes=[st6])
        fw.op("dve", lambda e: e.bn_aggr(out=mv[:], in_=st6[:]), reads=[st6], writes=[mv])
        fw.op("act", lambda e: e.activation(out=sm[:, 0:1], in_=mv[:, 1:2], func=AF.Ln, bias=self.small[:, 9:10]), reads=[mv, self.small], writes=[sm])
        fw.op("act", lambda e: e.activation(out=sm[:, 1:2], in_=sm[:, 0:1], func=AF.Exp, scale=-0.5), reads=[sm], writes=[sm])
        fw.op("dve", lambda e: e.tensor_scalar(out=sm[:, 2:3], in0=mv[:, 0:1], scalar1=sm[:, 1:2], scalar2=-1.0, op0=ALU.mult, op1=ALU.mult),
              reads=[mv, sm], writes=[sm])
        fw.op("act", lambda e: e.activation(out=tmp[:], in_=y[:], func=AF.Identity, scale=sm[:, 1:2], bias=sm[:, 2:3]), reads=[y, sm], writes=[tmp])
        fw.op("dve", lambda e: e.tensor_tensor(out=tmp[:], in0=tmp[:], in1=gb[:], op=ALU.mult), reads=[tmp, gb], writes=[tmp])
        fw.op("pool", lambda e: e.tensor_tensor(out=out[:], in0=tmp[:], in1=bb_[:], op=ALU.add), reads=[tmp, bb_], writes=[out])

    def layer(self, li):
        fw = self.fw
        gl = self.first + li
        ph = getattr(self, "phases", "0spamf")
        if li == 0 and "0" in ph:
            self.phase0()
        if "s" in ph:
            self.phase_small(li)
        if "p" in ph:
            self.phase_proj(li)
        if "a" in ph:
            self.phase_att(li)
        if "m" in ph:
            self.phase_merge(li)
        if "f" in ph:
            self.phase_ffn(li, last=(li == self.nl - 1))

    def phase0(self):
        fw = self.fw
        with ExitStack() as ps:
            xt = [fw.sb([128, D], F32, "p0x", ps) for _ in range(2)]
            xb = [fw.sb([128, D], BF16, "p0b", ps) for _ in range(2)]
            xts = [fw.sb([128, 16, 128], BF16, "p0t", ps) for _ in range(2)]
            rr = [0]
            for n in range(NT):
                a, b = xt[n % 2], xb[n % 2]
                fw.dma("act", a[:], self.x[n * 128:(n + 1) * 128, :], reads=[self.x], writes=[a])
                fw.op("pool", lambda e: e.tensor_copy(out=b[:], in_=a[:]), reads=[a], writes=[b])
                self.emit_T(b, self.xT, n, xts, rr)
            fw.barrier()

    def phase_small(self, li):
        fw = self.fw
        sm = self.small
        with ExitStack() as ps:
            lam = fw.sb([128, 256], F32, "lam", ps)
            tmp = fw.sb([128, 128], F32, "lamt", ps)
            bg64 = fw.sb([64, 128], F32, "bg64", ps)
            fw.dma("sp", bg64[:], self.b_gate[li, :].rearrange("(c p) -> c p", p=128), reads=[self.b_gate], writes=[bg64])
            bank = self.banks[0]
            fw.op("pe", lambda e: e.transpose(out=bank[:, 0:64], in_=bg64[:], identity=self.ident_f[0:64, 0:64]), reads=[bg64, self.ident_f], writes=[bank])
            fw.op("dve", lambda e: e.tensor_copy(out=self.bgc[:], in_=bank[:, 0:64]), reads=[bank], writes=[self.bgc])
            fw.dma("sp", sm[:, 0:8], self.sink_logit[li:li + 1, :].partition_broadcast(128), reads=[self.sink_logit], writes=[sm])
            fw.dma("sp", sm[:, 16:20], self.lconst[li:li + 1, :].partition_broadcast(128), reads=[self.lconst], writes=[sm])
            fw.dma("sp", lam[:], self.diff_lambda[li:li + 1, :].partition_broadcast(128), reads=[self.diff_lambda], writes=[lam])
            fw.dma("sp", self.subwb[:], self.diff_subln[li:li + 1, :].partition_broadcast(128), reads=[self.diff_subln], writes=[self.subwb])
            fw.op("act", lambda e: e.activation(out=sm[:, 0:8], in_=sm[:, 0:8], func=AF.Exp), reads=[sm], writes=[sm])
            fw.op("dve", lambda e: e.memset(sm[:, 9:10], LN_EPS), writes=[sm])
            fw.op("dve", lambda e: e.tensor_tensor(out=tmp[:, 0:64], in0=lam[:, 0:64], in1=lam[:, 64:128], op=ALU.mult), reads=[lam], writes=[tmp])
            fw.op("dve", lambda e: e.tensor_tensor(out=tmp[:, 64:128], in0=lam[:, 128:192], in1=lam[:, 192:256], op=ALU.mult), reads=[lam], writes=[tmp])
            fw.op("dve", lambda e: e.tensor_reduce(out=sm[:, 20:22], in_=tmp[:, :].rearrange("p (a b) -> p a b", b=64), axis=mybir.AxisListType.X, op=ALU.add),
                  reads=[tmp], writes=[sm])
            fw.op("act", lambda e: e.activation(out=sm[:, 20:22], in_=sm[:, 20:22], func=AF.Exp), reads=[sm], writes=[sm])
            fw.op("dve", lambda e: e.tensor_tensor(out=sm[:, 22:23], in0=sm[:, 21:22], in1=sm[:, 20:21], op=ALU.subtract), reads=[sm], writes=[sm])
            fw.op("dve", lambda e: e.tensor_tensor(out=sm[:, 8:9], in0=sm[:, 22:23], in1=sm[:, 16:17], op=ALU.subtract), reads=[sm], writes=[sm])
            fw.op("dve", lambda e: e.tensor_scalar(out=self.subwb[:], in0=self.subwb[:], scalar1=sm[:, 17:18], scalar2=None, op0=ALU.mult),
                  reads=[self.subwb, sm], writes=[self.subwb])
            fw.barrier()

    def phase_proj(self, li):
        fw = self.fw
        TB = 2048
        NCH = TB // 512
        w_in = self.w_in
        self.need_w(w_in)
        fslabs = []
        groups = [("qa", 18), ("ka", 18), ("qb", 8), ("kb", 8), ("qc", 8), ("kc", 2), ("qd", 8), ("kd", 8)]
        for name, n in groups:
            b0 = QK_BASE[name]
            t = 0
            while t < n:
                nt = min(4, n - t)
                c0, rk, sc = HEAD_TILES[b0 + t]
                fslabs.append((c0, nt, rk, sc, "qk", b0 + t))
                t += nt
        for g in range(0, 64, 4):
            fslabs.append((IN_OFF[12] + 128 * g, 4, "G", 1.0, "g", g))
        with ExitStack() as ps:
            xTb = fw.sb([128, 16, TB], BF16, "xTb", ps)
            tabs = fw.sb([128, 4, TB], F32, "tabs", ps)
            sst = [fw.sb([128, 4, 512], F32, "sst", ps) for _ in range(3)]
            sbf = [fw.sb([128, 16, 512], BF16, "sbf", ps) for _ in range(2)]
            qb = [fw.sb([128, 512], BF16, "qb", ps) for _ in range(2)]
            t1 = [fw.sb([128, 512], F32, "t1", ps) for _ in range(2)]
            t2 = [fw.sb([128, 512], F32, "t2", ps) for _ in range(2)]
            ob = [fw.sb([128, 512], BF16, "ob", ps) for _ in range(4)]
            wv = w_in[li].rearrange("(kc p) n -> p kc n", p=128)
            cnt = 0
            pcs = 0
            dbg = getattr(self, 'pdbg', {})

            pend = []

            def flush_rope():
                while pend:
                    q_, Bk, a1, a2, o, pi, ks_, csl, di, tsl = pend.pop(0)
                    fw.op("pe", lambda e: e.matmul(Bk[:], lhsT=self.perm[:, pi, :], rhs=q_[:], start=True, stop=True),
                          reads=[self.perm, q_], writes=[Bk])
                    fw.op("dve", lambda e: e.tensor_tensor(out=a2[:], in0=Bk[:], in1=tabs[:, ks_, csl], op=ALU.mult),
                          reads=[Bk, tabs], writes=[a2])
                    fw.op("dve", lambda e: e.tensor_tensor(out=o[:], in0=a1[:], in1=a2[:], op=ALU.add), reads=[a1, a2], writes=[o])
                    fw.dma("sp", self.qkT[di, :, tsl], o[:], reads=[o], writes=[self.qkT])

            def load_slab(si, c0, w):
                nonlocal pcs
                sb_ = sbf[si % 2]
                for pz in range(4):
                    s_ = sst[pcs % 3]
                    fw.dma("sp", s_[:, :, 0:w], wv[:, pz * 4:(pz + 1) * 4, c0:c0 + w], reads=[w_in], writes=[s_])
                    dst = sb_[:, pz * 4:(pz + 1) * 4, 0:w]
                    if pcs % 2 == 0:
                        fw.op("pool", lambda e: e.tensor_copy(out=dst, in_=s_[:, :, 0:w]), reads=[s_], writes=[sb_])
                    else:
                        fw.op("act", lambda e: e.activation(out=dst, in_=s_[:, :, 0:w], func=AF.Copy), reads=[s_], writes=[sb_])
                    pcs += 1
                return sb_

            for tb in range(dbg.get('ntb', T // TB)):
                t0 = tb * TB
                for h in range(2):
                    fw.dma("sp", xTb[:, h * 8:(h + 1) * 8, :], self.xT[:, h * 8:(h + 1) * 8, t0:t0 + TB], reads=[self.xT], writes=[xTb])
                for k in range(4):
                    fw.dma("act", tabs[:, k, :], self.c_rope[k, :, t0:t0 + TB], reads=[self.c_rope], writes=[tabs])
                si = 0
                fs = [fslabs[i] for i in dbg['slabs']] if 'slabs' in dbg else fslabs
                for (c0, nt, rk, sc, dk_, d0) in fs:
                    sb_ = load_slab(si, c0, nt * 128)
                    si += 1
                    for t in range(nt):
                        for c in range(NCH):
                            A = self.banks[cnt % 3]
                            for kc in range(16):
                                fw.op("pe", lambda e, kc=kc: e.matmul(A[:], lhsT=sb_[:, kc, t * 128:(t + 1) * 128], rhs=xTb[:, kc, c * 512:(c + 1) * 512],
                                                                      start=(kc == 0), stop=(kc == 15)),
                                      reads=[sb_, xTb], writes=[A], pe_accum=(kc > 0))
                            o = ob[cnt % 4]
                            csl = slice(c * 512, (c + 1) * 512)
                            tsl = slice(t0 + c * 512, t0 + (c + 1) * 512)
                            flush_rope()
                            if rk in ("A", "B"):
                                q_ = qb[cnt % 2]
                                Bk = self.banks[3 + cnt % 2]
                                kc_, ks_, pi = (0, 1, 0) if rk == "A" else (2, 3, 1)
                                fw.op("act", lambda e: e.activation(out=q_[:], in_=A[:], func=AF.Copy, scale=float(sc)), reads=[A], writes=[q_])
                                a1, a2 = t1[cnt % 2], t2[cnt % 2]
                                fw.op("dve", lambda e: e.scalar_tensor_tensor(out=a1[:], in0=A[:], scalar=float(sc), in1=tabs[:, kc_, csl],
                                                                              op0=ALU.mult, op1=ALU.mult), reads=[A, tabs], writes=[a1])
                                pend.append((q_, Bk, a1, a2, o, pi, ks_, csl, d0 + t, tsl))
                                cnt += 1
                                continue
                            elif rk == "G":
                                g = d0 + t
                                fw.op("act", lambda e: e.activation(out=o[:], in_=A[:], func=AF.Sigmoid, bias=self.bgc[:, g:g + 1]),
                                      reads=[A, self.bgc], writes=[o])
                            else:
                                fw.op("act", lambda e: e.activation(out=o[:], in_=A[:], func=AF.Copy, scale=float(sc)), reads=[A], writes=[o])
                            if dk_ == "qk":
                                fw.dma("sp", self.qkT[d0 + t, :, tsl], o[:], reads=[o], writes=[self.qkT])
                            else:
                                fw.dma("sp", self.gT[d0 + t, :, tsl], o[:], reads=[o], writes=[self.gT])
                            cnt += 1
                flush_rope()
                for (c0, w, dc) in V_SLABS[:dbg.get('nv', 99)]:
                    sb_ = load_slab(si, c0, w)
                    si += 1
                    for tt in range(TB // 128):
                        A = self.banks[cnt % 3]
                        for kc in range(16):
                            fw.op("pe", lambda e, kc=kc: e.matmul(A[:, 0:w], lhsT=xTb[:, kc, tt * 128:(tt + 1) * 128], rhs=sb_[:, kc, 0:w],
                                                                  start=(kc == 0), stop=(kc == 15)),
                                  reads=[sb_, xTb], writes=[A], pe_accum=(kc > 0))
                        o = ob[cnt % 4]
                        if cnt % 2 == 0:
                            fw.op("act", lambda e: e.activation(out=o[:, 0:w], in_=A[:, 0:w], func=AF.Copy), reads=[A], writes=[o])
                        else:
                            fw.op("dve", lambda e: e.tensor_copy(out=o[:, 0:w], in_=A[:, 0:w]), reads=[A], writes=[o])
                        r0 = t0 + tt * 128
                        fw.dma("sp", self.v[r0:r0 + 128, dc:dc + w], o[:, 0:w], reads=[o], writes=[self.v])
                        cnt += 1
            fw.barrier()

    def att_jobs(self):
        jobs = []
        for h in range(A_HPG):
            srcs = []
            for g in range(3):
                hd = g * A_HPG + h
                srcs.append((QK_BASE["qa"] + hd, QK_BASE["ka"] + hd, V_BASE["a"] + 128 * hd, (0, 128)))

            def blocks(i):
                bl = []
                for dj, m in ((-1, 4), (0, 2), (1, 3)):
                    bl.append((0, i + dj, m, 0))
                for dj in range(-2, 3):
                    bl.append((1, i + dj, 7 if dj == -2 else (6 if dj == 2 else 5), 0))
                for dj in range(-8, 9):
                    bl.append((2, i + dj, 10 if dj == -8 else (9 if dj == 8 else 8), 0))
                return [b for b in bl if 0 <= b[1] < NT]
            jobs.append(dict(kind="A", slot=h, srcs=srcs, blocks=blocks, h=h))
        for h in range(8):
            srcs = [(QK_BASE["qb"] + h, QK_BASE["kb"] + h, V_BASE["b"] + 128 * h, (0, 64)),
                    (QK_BASE["qb"] + h, QK_BASE["kb"] + h, None, (64, 128))]

            def blocks(i):
                return [(m, j, None, m) for m in range(2) for j in range(NT)]
            jobs.append(dict(kind="B", slot=6 + h, srcs=srcs, blocks=blocks, h=h))
        for h in range(8):
            srcs = [(QK_BASE["qc"] + h, QK_BASE["kc"] + h // 4, V_BASE["c"] + 128 * (h // 4), (0, 128))]

            def blocks(i):
                bl = [(0, i - 1, 1, 0), (0, i, None, 0), (0, i + 1, 0, 0)]
                return [b for b in bl if 0 <= b[1] < NT]
            jobs.append(dict(kind="C", slot=14 + h, srcs=srcs, blocks=blocks, h=h))
        for h in range(8):
            srcs = [(QK_BASE["qd"] + h, QK_BASE["kd"] + h, V_BASE["d"] + 128 * h, (0, 128))]

            def blocks(i):
                return [(0, j, ("D", p), 0) for (j, p) in D_BLOCKS[i]]
            jobs.append(dict(kind="D", slot=22 + h, srcs=srcs, blocks=blocks, h=h))
        return jobs

    def phase_att(self, li):
        fw = self.fw
        with ExitStack() as ps:
            NSET = 4
            qs = [fw.sb([128, T], BF16, "qs", ps) for _ in range(NSET)]
            ks = [fw.sb([128, T], BF16, "ks", ps) for _ in range(NSET)]
            vs = [fw.sb([128, NT, 129], BF16, "vs", ps) for _ in range(NSET)]
            masks = fw.sb([128, 11, 128], BF16, "masks", ps)
            ebst = fw.sb([128, NPAT, 128], F32, "ebst", ps)
            eb = [fw.sb([128, NPAT, 128], BF16, "eb", ps) for _ in range(2)]
            Pb = [fw.sb([128, 512], BF16, "Pb", ps) for _ in range(6)]
            rec = [fw.sb([128, 8], F32, "rec", ps) for _ in range(4)]
            u1 = [fw.sb([128, 128], F32, "u1", ps) for _ in range(2)]
            of = [fw.sb([128, 128], F32, "of", ps) for _ in range(2)]
            junk = fw.sb([128, 128], F32, "junk", ps)
            ot = [fw.sb([128, 128], BF16, "ot", ps) for _ in range(3)]
            otT = [fw.sb([128, 128], BF16, "otT", ps) for _ in range(3)]
            fw.dma("sp", masks[:], self.c_mask[:], reads=[self.c_mask], writes=[masks])
            for s in range(NSET):
                fw.op("pool", lambda e, s=s: e.memset(vs[s][:, :, 128:129], 1.0), writes=[vs[s]])
            setrr = 0
            sbank = 0
            prr = 0
            cnt = 0
            sm = self.small
            for job in self.att_jobs():
                kind = job["kind"]
                S = []
                for (qi, ki, vc, pr) in job["srcs"]:
                    if vc is None:
                        S.append((S[-1][0], pr))
                        continue
                    s = setrr % NSET
                    setrr += 1
                    fw.dma("sp", qs[s][:], self.qkT[qi], reads=[self.qkT], writes=[qs[s]])
                    fw.dma("act", ks[s][:], self.qkT[ki], reads=[self.qkT], writes=[ks[s]])
                    fw.dma("sp", vs[s][:, :, 0:128], self.v[:, vc:vc + 128].rearrange("(n p) c -> p n c", p=128), reads=[self.v], writes=[vs[s]])
                    S.append((s, pr))
                ebj = None
                if kind == "D":
                    ebj = eb[job["h"] % 2]
                    fw.dma("act", ebst[:], self.dbias[li, job["h"]], reads=[self.dbias], writes=[ebst])
                    fw.op("act", lambda e: e.activation(out=ebj[:], in_=ebst[:], func=AF.Exp), reads=[ebst], writes=[ebj])
                nacc = 2 if kind == "B" else 1
                tasks = []
                for i in range(NT):
                    blocks = job["blocks"](i)
                    seen = [0] * nacc
                    tot = [sum(1 for b in blocks if b[3] == a) for a in range(nacc)]
                    ngr = (len(blocks) + 3) // 4
                    for gi in range(ngr):
                        grp = []
                        for (si, j, m, a) in blocks[gi * 4:(gi + 1) * 4]:
                            first = seen[a] == 0
                            seen[a] += 1
                            grp.append((si, j, m, a, first, seen[a] == tot[a]))
                        tasks.append(dict(i=i, grp=grp, last=(gi == ngr - 1)))

                def emit_qk(tk):
                    nonlocal sbank, prr
                    tk["Sb"] = self.banks[sbank % 3]
                    sbank += 1
                    tk["P"] = Pb[prr % 6]
                    prr += 1
                    i = tk["i"]
                    for bi, (si, j, m, a, first, last) in enumerate(tk["grp"]):
                        s, (p0, p1) = S[si]
                        fw.op("pe", lambda e: e.matmul(tk["Sb"][:, bi * 128:(bi + 1) * 128], lhsT=ks[s][p0:p1, j * 128:(j + 1) * 128],
                                                       rhs=qs[s][p0:p1, i * 128:(i + 1) * 128], start=True, stop=True),
                              reads=[ks[s], qs[s]], writes=[tk["Sb"]])

                pend = []

                def flush_T():
                    nonlocal sbank
                    o_, i_ = pend.pop(0)
                    tb_ = self.banks[3]
                    tbb = tb_[:].bitcast(BF16)
                    oT_ = otT[i_ % 3]
                    fw.op("pe", lambda e: e.transpose(out=tbb[:, 0:128], in_=o_[:], identity=self.ident_b[:]), reads=[o_, self.ident_b], writes=[tb_])
                    fw.op("dve", lambda e: e.tensor_copy(out=oT_[:], in_=tbb[:, 0:128]), reads=[tb_], writes=[oT_])
                    fw.dma("sp", self.oT[job["slot"], :, i_ * 128:(i_ + 1) * 128], oT_[:], reads=[oT_], writes=[self.oT])

                for tk0 in tasks[:2]:
                    emit_qk(tk0)
                for ti, tk in enumerate(tasks):
                    if ti + 2 < len(tasks):
                        emit_qk(tasks[ti + 2])
                    i = tk["i"]
                    grp = tk["grp"]
                    Sb, P = tk["Sb"], tk["P"]
                    accb = [self.banks[4 + (i % 4)]] if nacc == 1 else [self.banks[4 + 2 * (i % 2) + a] for a in range(nacc)]
                    n = len(grp) * 128
                    fw.op("act", lambda e: e.activation(out=P[:, 0:n], in_=Sb[:, 0:n], func=AF.Exp), reads=[Sb], writes=[P])
                    for bi, (si, j, m, a, first, last) in enumerate(grp):
                        if m is None:
                            continue
                        mk = ebj[:, m[1], :] if isinstance(m, tuple) else masks[:, m, :]
                        mb = ebj if isinstance(m, tuple) else masks
                        eng = "pool" if (cnt % 2 == 0) else "dve"
                        cnt += 1
                        fw.op(eng, lambda e: e.tensor_tensor(out=P[:, bi * 128:(bi + 1) * 128], in0=P[:, bi * 128:(bi + 1) * 128], in1=mk, op=ALU.mult),
                              reads=[P, mb], writes=[P])
                    for bi, (si, j, m, a, first, last) in enumerate(grp):
                        s, _ = S[si]
                        fw.op("pe", lambda e: e.matmul(accb[a][:, 0:129], lhsT=P[:, bi * 128:(bi + 1) * 128], rhs=vs[s][:, j, :], start=first, stop=last),
                              reads=[P, vs[s]], writes=[accb[a]], pe_accum=(not first))
                    if pend:
                        flush_T()
                    if not tk["last"]:
                        continue
                    r = rec[i % 4]
                    o = ot[i % 3]
                    if kind in ("A", "D"):
                        fw.op("dve", lambda e: e.reciprocal(out=r[:, 0:1], in_=accb[0][:, 128:129]), reads=[accb[0]], writes=[r])
                        fw.op("act", lambda e: e.activation(out=o[:], in_=accb[0][:, 0:128], func=AF.Copy, scale=r[:, 0:1]), reads=[accb[0], r], writes=[o])
                    elif kind == "C":
                        h = job["h"]
                        fw.op("dve", lambda e: e.tensor_tensor(out=r[:, 1:2], in0=accb[0][:, 128:129], in1=sm[:, h:h + 1], op=ALU.add), reads=[accb[0], sm], writes=[r])
                        fw.op("dve", lambda e: e.reciprocal(out=r[:, 0:1], in_=r[:, 1:2]), reads=[r], writes=[r])
                        fw.op("act", lambda e: e.activation(out=o[:], in_=accb[0][:, 0:128], func=AF.Copy, scale=r[:, 0:1]), reads=[accb[0], r], writes=[o])
                    else:
                        ua, ofa = u1[i % 2], of[i % 2]
                        fw.op("dve", lambda e: e.reciprocal(out=r[:, 0:1], in_=accb[0][:, 128:129]), reads=[accb[0]], writes=[r])
                        fw.op("dve", lambda e: e.reciprocal(out=r[:, 1:2], in_=accb[1][:, 128:129]), reads=[accb[1]], writes=[r])
                        fw.op("dve", lambda e: e.tensor_tensor(out=r[:, 2:3], in0=r[:, 1:2], in1=sm[:, 8:9], op=ALU.mult), reads=[r, sm], writes=[r])
                        fw.op("act", lambda e: e.activation(out=ua[:], in_=accb[0][:, 0:128], func=AF.Copy, scale=r[:, 0:1]), reads=[accb[0], r], writes=[ua])
                        fw.op("dve", lambda e: e.scalar_tensor_tensor(out=ofa[:], in0=accb[1][:, 0:128], scalar=r[:, 2:3], in1=ua[:], op0=ALU.mult, op1=ALU.add),
                              reads=[accb[1], r, ua], writes=[ofa])
                        fw.op("act", lambda e: e.activation(out=junk[:], in_=ofa[:], func=AF.Square, accum_out=r[:, 3:4]), reads=[ofa], writes=[junk, r])
                        fw.op("act", lambda e: e.activation(out=r[:, 4:5], in_=r[:, 3:4], func=AF.Ln, scale=1.0 / 128.0, bias=sm[:, 9:10]), reads=[r, sm], writes=[r])
                        fw.op("act", lambda e: e.activation(out=r[:, 5:6], in_=r[:, 4:5], func=AF.Exp, scale=-0.5), reads=[r], writes=[r])
                        fw.op("dve", lambda e: e.scalar_tensor_tensor(out=o[:], in0=ofa[:], scalar=r[:, 5:6], in1=self.subwb[:], op0=ALU.mult, op1=ALU.mult),
                              reads=[ofa, r, self.subwb], writes=[o])
                    pend.append((o, i))
                while pend:
                    flush_T()
            fw.barrier()

    def phase_merge(self, li):
        fw = self.fw
        TBM = 1024
        NCH = TBM // 512
        bslots = [(0, 6), (6, 8), (14, 8), (22, 8)]
        self.need_w(self.w_branch, self.w_out)
        wbv = self.w_branch[li].rearrange("(s p) n -> p s n", p=128)
        wov = self.w_out[li].rearrange("(kc p) n -> p kc n", p=128)
        with ExitStack() as ps:
            oTb = fw.sb([128, NSLOT, TBM], BF16, "oTb", ps)
            wbs = [fw.sb([128, 8, 128], F32, "wbs", ps) for _ in range(3)]
            wbb = [fw.sb([128, 8, 128], BF16, "wbb", ps) for _ in range(3)]
            gt = [fw.sb([128, TBM], BF16, "gt", ps) for _ in range(4)]
            mix = [fw.sb([128, TBM], F32, "mix", ps) for _ in range(2)]
            tmp = [fw.sb([128, TBM], F32, "tmp", ps) for _ in range(3)]
            mo = [fw.sb([128, TBM], BF16, "mo", ps) for _ in range(3)]
            cnt = 0
            for tb in range(T // TBM):
                t0 = tb * TBM
                fw.dma("sp", oTb[:], self.oT[:, :, t0:t0 + TBM].rearrange("s p t -> p s t"), reads=[self.oT], writes=[oTb])
                for dc in range(16):
                    mx = mix[dc % 2]
                    mo_ = mo[dc % 3]
                    for b, (s0, ns) in enumerate(bslots):
                        ws, wb_ = wbs[cnt % 3], wbb[cnt % 3]
                        g_ = gt[cnt % 4]
                        tp = tmp[cnt % 3]
                        cnt += 1
                        fw.dma("act", ws[:, 0:ns, :], wbv[:, s0:s0 + ns, dc * 128:(dc + 1) * 128], reads=[self.w_branch], writes=[ws])
                        fw.op("pool", lambda e: e.tensor_copy(out=wb_[:, 0:ns, :], in_=ws[:, 0:ns, :]), reads=[ws], writes=[wb_])
                        fw.dma("sp", g_[:], self.gT[b * 16 + dc, :, t0:t0 + TBM], reads=[self.gT], writes=[g_])
                        for c in range(NCH):
                            A = self.banks[(cnt * NCH + c) % 8]
                            for k in range(ns):
                                fw.op("pe", lambda e, k=k: e.matmul(A[:], lhsT=wb_[:, k, :], rhs=oTb[:, s0 + k, c * 512:(c + 1) * 512], start=(k == 0), stop=(k == ns - 1)),
                                      reads=[wb_, oTb], writes=[A], pe_accum=(k > 0))
                            dst = mx if b == 0 else tp
                            fw.op("dve", lambda e: e.tensor_tensor(out=dst[:, c * 512:(c + 1) * 512], in0=A[:], in1=g_[:, c * 512:(c + 1) * 512], op=ALU.mult),
                                  reads=[A, g_], writes=[dst])
                        if 0 < b < 3:
                            fw.op("pool", lambda e: e.tensor_tensor(out=mx[:], in0=mx[:], in1=tp[:], op=ALU.add), reads=[mx, tp], writes=[mx])
                        elif b == 3:
                            fw.op("pool", lambda e: e.tensor_tensor(out=mo_[:], in0=mx[:], in1=tp[:], op=ALU.add), reads=[mx, tp], writes=[mo_])
                    fw.dma("sp", self.mixT[:, dc, t0:t0 + TBM], mo_[:], reads=[mo_], writes=[self.mixT])
            fw.barrier()
        with ExitStack() as ps:
            wos = [fw.sb([128, 4, 512], F32, "wos", ps) for _ in range(2)]
            wob = fw.sb([128, 16, D], BF16, "wob", ps)
            mt = [fw.sb([128, 16, 128], BF16, "mt", ps) for _ in range(2)]
            xr = [fw.sb([128, D], F32, "xr", ps) for _ in range(2)]
            y = [fw.sb([128, D], F32, "y", ps) for _ in range(2)]
            lt = fw.sb([128, D], F32, "lt", ps)
            xb = [fw.sb([128, D], BF16, "xb", ps) for _ in range(2)]
            xts = [fw.sb([128, 16, 128], BF16, "xts", ps) for _ in range(2)]
            gb = fw.sb([128, D], F32, "gb", ps)
            bb_ = fw.sb([128, D], F32, "bb", ps)
            st6 = [fw.sb([128, 24], F32, "st6", ps) for _ in range(2)]
            mv = [fw.sb([128, 2], F32, "mv", ps) for _ in range(2)]
            lsm = [fw.sb([128, 4], F32, "lsm", ps) for _ in range(2)]
            fw.dma("sp", gb[:], self.ln_gain[li, 0:1, :].partition_broadcast(128), reads=[self.ln_gain], writes=[gb])
            fw.dma("sp", bb_[:], self.ln_bias[li, 0:1, :].partition_broadcast(128), reads=[self.ln_bias], writes=[bb_])
            k = 0
            for c in range(4):
                for pz in range(4):
                    s_ = wos[k % 2]
                    k += 1
                    fw.dma("act", s_[:], wov[:, pz * 4:(pz + 1) * 4, c * 512:(c + 1) * 512], reads=[self.w_out], writes=[s_])
                    fw.op("pool", lambda e: e.tensor_copy(out=wob[:, pz * 4:(pz + 1) * 4, c * 512:(c + 1) * 512], in_=s_[:]), reads=[s_], writes=[wob])
            xres_src = self.x if li == 0 else self.xres
            rr = [0]
            cnt = 0
            for n in range(NT):
                m_, x_, y_ = mt[n % 2], xr[n % 2], y[n % 2]
                fw.dma("sp", m_[:], self.mixT[:, :, n * 128:(n + 1) * 128], reads=[self.mixT], writes=[m_])
                fw.dma("act", x_[:], xres_src[n * 128:(n + 1) * 128, :], reads=[xres_src], writes=[x_])
                for c in range(4):
                    A = self.banks[cnt % 4]
                    cnt += 1
                    for kc in range(16):
                        fw.op("pe", lambda e, kc=kc: e.matmul(A[:], lhsT=m_[:, kc, :], rhs=wob[:, kc, c * 512:(c + 1) * 512], start=(kc == 0), stop=(kc == 15)),
                              reads=[m_, wob], writes=[A], pe_accum=(kc > 0))
                    fw.op("dve", lambda e: e.scalar_tensor_tensor(out=y_[:, c * 512:(c + 1) * 512], in0=x_[:, c * 512:(c + 1) * 512], scalar=float(ALPHA),
                                                                  in1=A[:], op0=ALU.mult, op1=ALU.add), reads=[x_, A], writes=[y_])
                self.layer_norm(y_, gb, bb_, y_, lt, st6[n % 2], mv[n % 2], lsm[n % 2])
                fw.dma("sp", self.x1res[n * 128:(n + 1) * 128, :], y_[:], reads=[y_], writes=[self.x1res])
                b_ = xb[n % 2]
                fw.op("act", lambda e: e.activation(out=b_[:], in_=y_[:], func=AF.Copy), reads=[y_], writes=[b_])
                fw.dma("sp", self.x1b[n * 128:(n + 1) * 128, :], b_[:], reads=[b_], writes=[self.x1b])
                self.emit_T(b_, self.x1T, n, xts, rr)
            fw.barrier()

    def phase_ffn(self, li, last):
        fw = self.fw
        IOA = bass.IndirectOffsetOnAxis
        self.need_w(self.w_eg, self.w_eu, self.w_ed)
        with ExitStack() as outer:
            idx_all = fw.sb([128, NT * NEXP], I32, "idx_all", outer)
            gsel_all = fw.sb([128, NT * NEXP], F32, "gsel_all", outer)
            with ExitStack() as ps:
                wrs = fw.sb([128, 16, NEXP], F32, "wrs", ps)
                wrb = fw.sb([128, 16, NEXP], BF16, "wrb", ps)
                xt = [fw.sb([128, 16, 128], BF16, "fxt", ps) for _ in range(2)]
                aff_all = fw.sb([128, NT * NEXP], F32, "aff_all", ps)
                ex = [fw.sb([128, NEXP], F32, "ex", ps) for _ in range(2)]
                s4 = [fw.sb([128, 4], F32, "s4", ps) for _ in range(2)]
                affT = fw.sb([16, T], F32, "affT", ps)
                work = fw.sb([16, T], F32, "work", ps)
                maskT = fw.sb([16, T], F32, "maskT", ps)
                onesT = fw.sb([16, T], F32, "onesT", ps)
                idxT = fw.sb([16, T], F32, "idxT", ps)
                gselT = fw.sb([16, T], F32, "gselT", ps)
                m8 = fw.sb([16, 8], F32, "m8", ps)
                fw.dma("sp", wrs[:], self.w_router[li].rearrange("(kc p) e -> p kc e", p=128), reads=[self.w_router], writes=[wrs])
                fw.op("dve", lambda e: e.tensor_copy(out=wrb[:], in_=wrs[:]), reads=[wrs], writes=[wrb])
                fw.op("pool", lambda e: e.memset(onesT[:], 1.0), writes=[onesT])
                for n in range(NT):
                    x_ = xt[n % 2]
                    lg = self.banks[n % 2]
                    tb_ = self.banks[2 + (n // 4) % 2]
                    fw.dma("sp", x_[:], self.x1T[:, :, n * 128:(n + 1) * 128], reads=[self.x1T], writes=[x_])
                    for kc in range(16):
                        fw.op("pe", lambda e, kc=kc: e.matmul(lg[:, 0:NEXP], lhsT=x_[:, kc, :], rhs=wrb[:, kc, :], start=(kc == 0), stop=(kc == 15)),
                              reads=[x_, wrb], writes=[lg], pe_accum=(kc > 0))
                    s_ = s4[n % 2]
                    e_ = ex[n % 2]
                    fw.op("dve", lambda e: e.tensor_reduce(out=s_[:, 0:1], in_=lg[:, 0:NEXP], axis=mybir.AxisListType.X, op=ALU.max), reads=[lg], writes=[s_])
                    fw.op("dve", lambda e: e.tensor_scalar(out=s_[:, 1:2], in0=s_[:, 0:1], scalar1=-1.0, scalar2=None, op0=ALU.mult), reads=[s_], writes=[s_])
                    fw.op("act", lambda e: e.activation(out=e_[:], in_=lg[:, 0:NEXP], func=AF.Exp, bias=s_[:, 1:2], accum_out=s_[:, 2:3]), reads=[lg, s_], writes=[e_, s_])
                    fw.op("dve", lambda e: e.reciprocal(out=s_[:, 3:4], in_=s_[:, 2:3]), reads=[s_], writes=[s_])
                    fw.op("dve", lambda e: e.tensor_scalar(out=aff_all[:, n * NEXP:(n + 1) * NEXP], in0=e_[:], scalar1=s_[:, 3:4], scalar2=None, op0=ALU.mult),
                          reads=[e_, s_], writes=[aff_all])
                    fw.op("pe", lambda e: e.transpose(out=tb_[0:NEXP, (n % 4) * 128:(n % 4 + 1) * 128], in_=aff_all[:, n * NEXP:(n + 1) * NEXP], identity=self.ident_f[:]),
                          reads=[aff_all, self.ident_f], writes=[tb_])
                    if n % 4 == 3:
                        fw.op("act", lambda e: e.activation(out=affT[:, (n - 3) * 128:(n + 1) * 128], in_=tb_[0:NEXP, :], func=AF.Copy), reads=[tb_], writes=[affT])
                cur = affT
                for it in range(CAP // 8):
                    fw.op("dve", lambda e: e.max(out=m8[:], in_=cur[:]), reads=[cur], writes=[m8])
                    if it < CAP // 8 - 1:
                        fw.op("dve", lambda e: e.match_replace(out=work[:], in_to_replace=m8[:], in_values=cur[:], imm_value=-1.0), reads=[m8, cur], writes=[work])
                        cur = work
                fw.op("dve", lambda e: e.tensor_scalar(out=maskT[:], in0=affT[:], scalar1=m8[:, 7:8], scalar2=None, op0=ALU.is_ge), reads=[affT, m8], writes=[maskT])
                fw.op("dve", lambda e: e.tensor_tensor_scan(out=work[:], data0=onesT[:], data1=maskT[:], initial=0.0, op0=ALU.mult, op1=ALU.add),
                      reads=[onesT, maskT], writes=[work])
                fw.op("dve", lambda e: e.scalar_tensor_tensor(out=idxT[:], in0=maskT[:], scalar=-(1.0 + BIGIDX), in1=work[:], op0=ALU.mult, op1=ALU.add),
                      reads=[maskT, work], writes=[idxT])
                fw.op("dve", lambda e: e.tensor_scalar(out=idxT[:], in0=idxT[:], scalar1=BIGIDX, scalar2=None, op0=ALU.add), reads=[idxT], writes=[idxT])
                fw.op("pool", lambda e: e.tensor_tensor(out=gselT[:], in0=maskT[:], in1=affT[:], op=ALU.mult), reads=[maskT, affT], writes=[gselT])
                bi_, bg_ = self.banks[4], self.banks[5]
                for n in range(NT):
                    fw.op("pe", lambda e: e.transpose(out=bi_[:, n * NEXP:(n + 1) * NEXP], in_=idxT[:, n * 128:(n + 1) * 128], identity=self.ident_f[0:NEXP, 0:NEXP]),
                          reads=[idxT, self.ident_f], writes=[bi_])
                    fw.op("pe", lambda e: e.transpose(out=bg_[:, n * NEXP:(n + 1) * NEXP], in_=gselT[:, n * 128:(n + 1) * 128], identity=self.ident_f[0:NEXP, 0:NEXP]),
                          reads=[gselT, self.ident_f], writes=[bg_])
                fw.op("dve", lambda e: e.tensor_copy(out=idx_all[:], in_=bi_[:]), reads=[bi_], writes=[idx_all])
                fw.op("act", lambda e: e.activation(out=gsel_all[:], in_=bg_[:], func=AF.Copy), reads=[bg_], writes=[gsel_all])
                if self.debug:
                    fw.dma("sp", self.dbg_aff[:], gsel_all[:], reads=[gsel_all], writes=[self.dbg_aff])
                    fw.dma("sp", self.dbg_idx[:], idx_all[:], reads=[idx_all], writes=[self.dbg_idx])
                fw.barrier()
            with ExitStack() as ps:
                xb = [fw.sb([128, D], BF16, "sxb", ps) for _ in range(3)]
                for n in range(NT):
                    b_ = xb[n % 3]
                    fw.dma("sp", b_[:], self.x1b[n * 128:(n + 1) * 128, :], reads=[self.x1b], writes=[b_])
                    for e_i in range(NEXP):
                        c = n * NEXP + e_i
                        fw.idma(self.xin[e_i][:, :], IOA(ap=idx_all[:, c:c + 1], axis=0), b_[:, :], None, CAP - 1, reads=[b_, idx_all], writes=[self.xin[e_i]])
                fw.barrier()
            with ExitStack() as ps:
                xi = [fw.sb([128, D], BF16, "xi", ps) for _ in range(4)]
                xinT = fw.sb([128, 16, CAP], BF16, "xinT", ps)
                wgs = [fw.sb([128, 16, 128], F32, "wgs", ps) for _ in range(4)]
                wgb = [fw.sb([128, 16, 128], BF16, "wgb", ps) for _ in range(4)]
                sg = [fw.sb([128, CAP], F32, "sg", ps) for _ in range(2)]
                hT = fw.sb([128, 8, CAP], BF16, "hT", ps)
                wds = [fw.sb([128, D], F32, "wds", ps) for _ in range(2)]
                wdb = fw.sb([128, 8, D], BF16, "wdb", ps)
                yo = [fw.sb([128, D], F32, "yo", ps) for _ in range(2)]
                cnt = 0
                for e_i in range(NEXP):
                    for jt in range(4):
                        fw.dma("sp", xi[jt][:], self.xin[e_i][jt * 128:(jt + 1) * 128, :], reads=[self.xin[e_i]], writes=[xi[jt]])
                        for half in range(2):
                            bank = self.banks[cnt % 2]
                            cnt += 1
                            bb = bank[:].bitcast(BF16)
                            for k in range(8):
                                kc = half * 8 + k
                                fw.op("pe", lambda e, kc=kc, k=k: e.transpose(out=bb[:, k * 128:(k + 1) * 128], in_=xi[jt][:, kc * 128:(kc + 1) * 128], identity=self.ident_b[:]),
                                      reads=[xi[jt], self.ident_b], writes=[bank])
                            src = bb[:, :].rearrange("p (a b) -> p a b", b=128)
                            dst = xinT[:, half * 8:(half + 1) * 8, jt * 128:(jt + 1) * 128]
                            if half == 0:
                                fw.op("act", lambda e: e.activation(out=dst, in_=src, func=AF.Copy), reads=[bank], writes=[xinT])
                            else:
                                fw.op("dve", lambda e: e.tensor_copy(out=dst, in_=src), reads=[bank], writes=[xinT])
                    wgv = self.w_eg[li, e_i].rearrange("(kc p) f -> p kc f", p=128)
                    wuv = self.w_eu[li, e_i].rearrange("(kc p) f -> p kc f", p=128)
                    for ft in range(8):
                        gs_, gb_ = wgs[(2 * ft) % 4], wgb[(2 * ft) % 4]
                        us_, ub_ = wgs[(2 * ft + 1) % 4], wgb[(2 * ft + 1) % 4]
                        fw.dma("sp", gs_[:], wgv[:, :, ft * 128:(ft + 1) * 128], reads=[self.w_eg], writes=[gs_])
                        fw.dma("act", us_[:], wuv[:, :, ft * 128:(ft + 1) * 128], reads=[self.w_eu], writes=[us_])
                        fw.op("pool", lambda e: e.tensor_copy(out=gb_[:], in_=gs_[:]), reads=[gs_], writes=[gb_])
                        fw.op("act", lambda e: e.activation(out=ub_[:], in_=us_[:], func=AF.Copy), reads=[us_], writes=[ub_])
                        G = self.banks[2 + (ft % 2) * 2]
                        U = self.banks[3 + (ft % 2) * 2]
                        for kc in range(16):
                            fw.op("pe", lambda e, kc=kc: e.matmul(G[:], lhsT=gb_[:, kc, :], rhs=xinT[:, kc, :], start=(kc == 0), stop=(kc == 15)),
                                  reads=[gb_, xinT], writes=[G], pe_accum=(kc > 0))
                        for kc in range(16):
                            fw.op("pe", lambda e, kc=kc: e.matmul(U[:], lhsT=ub_[:, kc, :], rhs=xinT[:, kc, :], start=(kc == 0), stop=(kc == 15)),
                                  reads=[ub_, xinT], writes=[U], pe_accum=(kc > 0))
                        s_ = sg[ft % 2]
                        fw.op("act", lambda e: e.activation(out=s_[:], in_=G[:], func=AF.Silu), reads=[G], writes=[s_])
                        fw.op("dve", lambda e: e.tensor_tensor(out=hT[:, ft, :], in0=s_[:], in1=U[:], op=ALU.mult), reads=[s_, U], writes=[hT])
                    for fc in range(8):
                        d_ = wds[fc % 2]
                        fw.dma("act", d_[:], self.w_ed[li, e_i, fc * 128:(fc + 1) * 128, :], reads=[self.w_ed], writes=[d_])
                        if fc % 2 == 0:
                            fw.op("pool", lambda e: e.tensor_copy(out=wdb[:, fc, :], in_=d_[:]), reads=[d_], writes=[wdb])
                        else:
                            fw.op("dve", lambda e: e.tensor_copy(out=wdb[:, fc, :], in_=d_[:]), reads=[d_], writes=[wdb])
                    for jt in range(4):
                        y_ = yo[jt % 2]
                        for dc in range(4):
                            Y = self.banks[6 + dc % 2]
                            for fc in range(8):
                                fw.op("pe", lambda e, fc=fc: e.matmul(Y[:], lhsT=hT[:, fc, jt * 128:(jt + 1) * 128], rhs=wdb[:, fc, dc * 512:(dc + 1) * 512],
                                                                      start=(fc == 0), stop=(fc == 7)), reads=[hT, wdb], writes=[Y], pe_accum=(fc > 0))
                            if dc % 2 == 0:
                                fw.op("act", lambda e: e.activation(out=y_[:, dc * 512:(dc + 1) * 512], in_=Y[:], func=AF.Copy), reads=[Y], writes=[y_])
                            else:
                                fw.op("dve", lambda e: e.tensor_copy(out=y_[:, dc * 512:(dc + 1) * 512], in_=Y[:]), reads=[Y], writes=[y_])
                        fw.dma("sp", self.yexp[e_i][jt * 128:(jt + 1) * 128, :], y_[:], reads=[y_], writes=[self.yexp[e_i]])
                fw.barrier()
            with ExitStack() as ps:
                acc = [fw.sb([128, D], F32, "acc", ps) for _ in range(2)]
                gbuf = [fw.sb([128, D], F32, "gbuf", ps) for _ in range(8)]
                lt = fw.sb([128, D], F32, "lt2", ps)
                gb = fw.sb([128, D], F32, "gb2", ps)
                bb_ = fw.sb([128, D], F32, "bb2", ps)
                xb = [fw.sb([128, D], BF16, "xb2", ps) for _ in range(2)]
                xts = [fw.sb([128, 16, 128], BF16, "xts2", ps) for _ in range(2)]
                st6 = [fw.sb([128, 24], F32, "st62", ps) for _ in range(2)]
                mv = [fw.sb([128, 2], F32, "mv2", ps) for _ in range(2)]
                lsm = [fw.sb([128, 4], F32, "lsm2", ps) for _ in range(2)]
                fw.dma("sp", gb[:], self.ln_gain[li, 1:2, :].partition_broadcast(128), reads=[self.ln_gain], writes=[gb])
                fw.dma("sp", bb_[:], self.ln_bias[li, 1:2, :].partition_broadcast(128), reads=[self.ln_bias], writes=[bb_])
                for g_ in gbuf:
                    fw.op("pool", lambda e: e.memset(g_[:], 0.0), writes=[g_])
                rr = [0]
                k = 0
                for n in range(NT):
                    a_ = acc[n % 2]
                    fw.dma("sp", a_[:], self.x1res[n * 128:(n + 1) * 128, :], reads=[self.x1res], writes=[a_])
                    fw.op("act", lambda e: e.activation(out=a_[:], in_=a_[:], func=AF.Copy, scale=float(ALPHA)), reads=[a_], writes=[a_])
                    for e_i in range(NEXP):
                        c = n * NEXP + e_i
                        g_ = gbuf[k % 8]
                        k += 1
                        fw.idma(g_[:, :], None, self.yexp[e_i][:, :], IOA(ap=idx_all[:, c:c + 1], axis=0), CAP - 1, reads=[self.yexp[e_i], idx_all], writes=[g_])
                        fw.op("dve", lambda e: e.scalar_tensor_tensor(out=a_[:], in0=g_[:], scalar=gsel_all[:, c:c + 1], in1=a_[:], op0=ALU.mult, op1=ALU.add),
                              reads=[g_, gsel_all, a_], writes=[a_])
                    self.layer_norm(a_, gb, bb_, a_, lt, st6[n % 2], mv[n % 2], lsm[n % 2])
                    if last:
                        fw.dma("sp", self.out[n * 128:(n + 1) * 128, :], a_[:], reads=[a_], writes=[self.out])
                    else:
                        fw.dma("sp", self.xres[n * 128:(n + 1) * 128, :], a_[:], reads=[a_], writes=[self.xres])
                        b_ = xb[n % 2]
                        fw.op("act", lambda e: e.activation(out=b_[:], in_=a_[:], func=AF.Copy), reads=[a_], writes=[b_])
                        self.emit_T(b_, self.xT, n, xts, rr)
                fw.barrier()


_CONSTS = None


def _consts():
    global _CONSTS
    if _CONSTS is None:
        _CONSTS = dict(c_ident=np.eye(128, dtype=np.float32), c_perm=_perms(), c_mask=_masks(), c_rope=_rope_tables())
    return _CONSTS


def _dbias(na_rpb):
    L = na_rpb.shape[0]
    out = np.empty((L, 8, 128, NPAT, 128), np.float32)
    for p, (roff, coff, valid) in enumerate(D_PATS):
        g = na_rpb[:, :, roff, coff]
        out[:, :, :, p, :] = np.where(valid[None, None], g, np.float32(NEGM))
    return out


def _lconst(l0, n):
    out = np.zeros((n, 4), np.float32)
    for i in range(n):
        li = 0.8 - 0.6 * math.exp(-0.3 * (l0 + i))
        out[i, 0] = li
        out[i, 1] = 1.0 - li
    return out


def _layer_inputs(inp, l0, n):
    sl = slice(l0, l0 + n)
    f = lambda a: np.ascontiguousarray(np.asarray(a, dtype=np.float32))
    d = dict(w_in=f(inp["w_in"][sl]), b_gate=f(inp["b_gate"][sl]), w_branch=f(inp["w_branch"][sl]), w_out=f(inp["w_out"][sl]),
             diff_lambda=f(inp["diff_lambda"][sl]).reshape(n, 256), diff_subln=f(inp["diff_subln"][sl]), sink_logit=f(inp["sink_logit"][sl]),
             dbias=_dbias(np.asarray(inp["na_rpb"][sl], dtype=np.float32)), w_router=f(inp["w_router"][sl]),
             w_exp_gate=f(inp["w_exp_gate"][sl]), w_exp_up=f(inp["w_exp_up"][sl]), w_exp_down=f(inp["w_exp_down"][sl]),
             ln_gain=f(inp["ln_gain"][sl]), ln_bias=f(inp["ln_bias"][sl]), lconst=_lconst(l0, n))
    d.update(_consts())
    return d


NLAYERS_PER_LAUNCH = 4
NCORES = 8
GATHERED = ("w_in", "w_branch", "w_out", "w_exp_gate", "w_exp_up", "w_exp_down")
_PROGS = {}


def _get_prog(nl, gather=True):
    key = (nl, gather)
    if key not in _PROGS:
        _PROGS[key] = Prog(nl, 0, False, gather).build()
    return _PROGS[key]


def _shard_rows(a, r):
    a2 = a.reshape(-1, a.shape[-1])
    n = a2.shape[0] // 8
    return np.ascontiguousarray(a2[r * n:(r + 1) * n])


def kernel(**inputs):
    x = np.ascontiguousarray(np.asarray(inputs["x"], dtype=np.float32))
    B = x.shape[0]
    cur = [x[b] for b in range(B)]
    nl = NLAYERS_PER_LAUNCH
    nc = _get_prog(nl, False)
    for l0 in range(0, DEPTH, nl):
        shared = _layer_inputs(inputs, l0, nl)
        in_maps = []
        for b in range(B):
            m = dict(shared)
            m["x"] = cur[b]
            in_maps.append(m)
        res = run_bass_kernel_spmd(nc, in_maps, core_ids=list(range(B)))
        cur = [np.asarray(res.results[b]["out"], dtype=np.float32) for b in range(B)]
        del res, in_maps, shared
    return np.stack(cur, 0)
```
